# Optimizing a Trainium2 kernel written in Bass

```python
import math
import jax, jax.numpy as jnp
from jax import lax
import numpy as np

D_MODEL = 2048
BATCH = 4
SEQ = 8192
DEPTH = 2
DEC_BATCH = 8
DEC_SEQ = 2048
PAST_LEN = 128

W_BRANCH = D_MODEL // 2
W_A = W_BRANCH
W_B = W_BRANCH
W_C = W_BRANCH
W_D = W_BRANCH
N_BRANCH = 4
CONV_WIDTH = 31
CONV_PAD = CONV_WIDTH // 2
POOL_WINDOWS = (2, 4, 8, 16)
POOL_GROUP = W_B // len(POOL_WINDOWS)
HG_HEAD_DIM = 128
HG_HEADS = W_C // HG_HEAD_DIM
HG_CHUNK = 64
S5_GROUP_CH = 16
S5_GROUPS = W_D // S5_GROUP_CH
S5_STATE = 64
DT_MIN = 0.001
DT_MAX = 0.1
EPS = 1e-6
IN_SIZES = (W_A, W_A, W_A, W_B, W_B, W_C, W_C, W_C, W_C, W_C, W_D, W_D, N_BRANCH * D_MODEL)
N_IN = 3 * W_A + 2 * W_B + 5 * W_C + 2 * W_D + N_BRANCH * D_MODEL

kernel_name = 'hybrid_bidir_gated_encoder'


def _rms_norm(x, g):
    xf = x.astype(jnp.float32)
    y = xf * lax.rsqrt(jnp.mean(xf * xf, axis=-1, keepdims=True) + EPS) * g.astype(jnp.float32)
    return y.astype(x.dtype)


def _conformer_conv(val, glu, conv_w, conv_b, ln_g, ln_b):
    u = val * jax.nn.sigmoid(glu)
    y = lax.conv_general_dilated(u, conv_w[:, None, :].astype(u.dtype), window_strides=(1,),
                                 padding=((CONV_PAD, CONV_PAD),),
                                 dimension_numbers=('NWC', 'WIO', 'NWC'),
                                 feature_group_count=W_A)
    yf = y.astype(jnp.float32) + conv_b.astype(jnp.float32)
    mu = jnp.mean(yf, axis=-1, keepdims=True)
    var = jnp.mean(jnp.square(yf - mu), axis=-1, keepdims=True)
    yn = (yf - mu) * lax.rsqrt(var + EPS) * ln_g.astype(jnp.float32) + ln_b.astype(jnp.float32)
    return jax.nn.silu(yn).astype(val.dtype)


def _pool_mixer(u, pool_w, pool_scale):
    bsz, L, _ = u.shape
    uf = u.astype(jnp.float32)
    cs = jnp.concatenate([jnp.zeros((bsz, 1, W_B), jnp.float32), jnp.cumsum(uf, axis=1)], axis=1)
    t = jnp.arange(L)
    means = []
    for g, win in enumerate(POOL_WINDOWS):
        left = win // 2
        right = win - 1 - left
        lo = jnp.maximum(t - left, 0)
        hi = jnp.minimum(t + right, L - 1) + 1
        csg = cs[:, :, g * POOL_GROUP:(g + 1) * POOL_GROUP]
        cnt = (hi - lo).astype(jnp.float32)[None, :, None]
        means.append((jnp.take(csg, hi, axis=1) - jnp.take(csg, lo, axis=1)) / cnt)
    d = (jnp.concatenate(means, axis=-1) - uf).reshape(bsz, L, len(POOL_WINDOWS), POOL_GROUP)
    y = jnp.einsum('blgc,gce->blge', d, pool_w.astype(jnp.float32)).reshape(bsz, L, W_B)
    return (y * pool_scale.astype(jnp.float32)).astype(u.dtype)


def _hgrn2_chunk_scan(q, log_f, k, v):
    bsz, L, H, K = q.shape
    V = v.shape[-1]
    n = L // HG_CHUNK

    def to_chunks(t):
        return t.reshape(bsz, n, HG_CHUNK, H, t.shape[-1]).transpose(1, 0, 3, 2, 4)

    qc, fc, kc, vc = to_chunks(q), to_chunks(log_f), to_chunks(k), to_chunks(v)
    lower = jnp.tril(jnp.ones((HG_CHUNK, HG_CHUNK), dtype=bool))[:, :, None]

    def step(S, inp):
        q_, lf_, k_, v_ = inp
        b = jnp.cumsum(lf_, axis=2)
        o_inter = jnp.einsum('bhck,bhkv->bhcv', q_ * jnp.exp(b), S)
        diff = b[:, :, :, None, :] - b[:, :, None, :, :]
        decay = jnp.exp(jnp.where(lower, diff, -jnp.inf))
        scores = jnp.einsum('bhtk,bhtsk,bhsk->bhts', q_, decay, k_)
        o_intra = jnp.einsum('bhts,bhsv->bhtv', scores, v_)
        b_end = b[:, :, -1:, :]
        S_new = jnp.exp(b_end[:, :, 0, :])[..., None] * S + jnp.einsum('bhsk,bhsv->bhkv', k_ * jnp.exp(b_end - b), v_)
        return S_new, o_inter + o_intra

    S0 = jnp.zeros((bsz, H, K, V), jnp.float32)
    _, o = lax.scan(step, S0, (qc, fc, kc, vc))
    return o.transpose(1, 0, 3, 2, 4).reshape(bsz, L, H, V)


def _hgrn2_mixer(q, zf_fwd, zf_bwd, v, lb, norm_g):
    bsz, L, _ = q.shape

    def heads(t):
        return t.astype(jnp.float32).reshape(bsz, L, HG_HEADS, HG_HEAD_DIM)

    qh, vh = heads(q), heads(v)
    lbh = lb.astype(jnp.float32).reshape(2, HG_HEADS, HG_HEAD_DIM)

    def gates(z, lbd):
        zh = heads(z)
        log_f = jnp.log(lbd + (1.0 - lbd) * jax.nn.sigmoid(zh))
        k = (1.0 - lbd) * jax.nn.sigmoid(-zh)
        return log_f, k

    lf_f, k_f = gates(zf_fwd, lbh[0])
    lf_b, k_b = gates(zf_bwd, lbh[1])
    o_fwd = _hgrn2_chunk_scan(qh, lf_f, k_f, vh)
    o_bwd = jnp.flip(_hgrn2_chunk_scan(jnp.flip(qh, 1), jnp.flip(lf_b, 1), jnp.flip(k_b, 1), jnp.flip(vh, 1)), 1)
    o = o_fwd + o_bwd
    o = o * lax.rsqrt(jnp.mean(o * o, axis=-1, keepdims=True) + EPS)
    o = o.reshape(bsz, L, W_C) * norm_g.astype(jnp.float32)
    return o.astype(q.dtype)


def _complex_affine_combine(e1, e2):
    ar1, ai1, br1, bi1 = e1
    ar2, ai2, br2, bi2 = e2
    ar = ar1 * ar2 - ai1 * ai2
    ai = ar1 * ai2 + ai1 * ar2
    br = ar2 * br1 - ai2 * bi1 + br2
    bi = ar2 * bi1 + ai2 * br1 + bi2
    return (ar, ai, br, bi)


def _s5_direction(ug, a_re, a_im, log_dt, b_re, b_im, c_re, c_im, reverse):
    f32 = jnp.float32
    a_re, a_im = a_re.astype(f32), a_im.astype(f32)
    b_re, b_im = b_re.astype(f32), b_im.astype(f32)
    dt = jnp.exp(log_dt.astype(f32))[:, None]
    mag = jnp.exp(dt * a_re)
    ang = dt * a_im
    ab_re, ab_im = mag * jnp.cos(ang), mag * jnp.sin(ang)
    den = a_re * a_re + a_im * a_im
    x_, y_ = ab_re - 1.0, ab_im
    g_re = (x_ * a_re + y_ * a_im) / den
    g_im = (y_ * a_re - x_ * a_im) / den
    bb_re = g_re[..., None] * b_re - g_im[..., None] * b_im
    bb_im = g_re[..., None] * b_im + g_im[..., None] * b_re
    bu_re = jnp.einsum('gpc,blgc->blgp', bb_re, ug)
    bu_im = jnp.einsum('gpc,blgc->blgp', bb_im, ug)
    L = ug.shape[1]
    ar = jnp.broadcast_to(ab_re[None, None], (1, L) + ab_re.shape)
    ai = jnp.broadcast_to(ab_im[None, None], (1, L) + ab_im.shape)
    _, _, h_re, h_im = lax.associative_scan(_complex_affine_combine, (ar, ai, bu_re, bu_im), axis=1, reverse=reverse)
    return (jnp.einsum('gcp,blgp->blgc', c_re.astype(f32), h_re)
            - jnp.einsum('gcp,blgp->blgc', c_im.astype(f32), h_im))


def _s5_mixer(u, a_re, a_im, log_dt, b_re, b_im, c_re, c_im, d_skip, glu_w, glu_b):
    bsz, L, _ = u.shape
    uf = u.astype(jnp.float32)
    ug = uf.reshape(bsz, L, S5_GROUPS, S5_GROUP_CH)
    y = uf * d_skip.astype(jnp.float32)
    for d in range(2):
        y = y + _s5_direction(ug, a_re[d], a_im[d], log_dt[d], b_re[d], b_im[d], c_re[d], c_im[d],
                              reverse=(d == 1)).reshape(bsz, L, W_D)
    z = jax.nn.gelu(y)
    z = z * jax.nn.sigmoid(z @ glu_w.astype(jnp.float32) + glu_b.astype(jnp.float32))
    return z.astype(u.dtype)


def _layer(x, lb, norm_g, w_in, conv_w, conv_b, conv_ln_g, conv_ln_b, w_a_out, pool_w, pool_scale, w_b_out,
           hg_norm_g, w_c_out, s5_a_re, s5_a_im, s5_log_dt, s5_b_re, s5_b_im, s5_c_re, s5_c_im, s5_d,
           s5_glu_w, s5_glu_b, w_d_out, w_out):
    bsz, L, _ = x.shape
    h = _rms_norm(x, norm_g)
    p = jnp.einsum('bld,dn->bln', h, w_in.astype(h.dtype))
    split_points = np.cumsum(np.array(IN_SIZES))[:-1].tolist()
    (a_val, a_glu, a_gate, b_in, b_gate, c_q, c_ff, c_fb, c_i, c_gate,
     d_in, d_gate, r) = jnp.split(p, split_points, axis=-1)
    ya = _conformer_conv(a_val, a_glu, conv_w, conv_b, conv_ln_g, conv_ln_b) * jax.nn.silu(a_gate)
    yb = _pool_mixer(b_in, pool_w, pool_scale) * jax.nn.silu(b_gate)
    yc = _hgrn2_mixer(c_q, c_ff, c_fb, c_i, lb, hg_norm_g) * jax.nn.silu(c_gate)
    yd = _s5_mixer(d_in, s5_a_re, s5_a_im, s5_log_dt, s5_b_re, s5_b_im, s5_c_re, s5_c_im, s5_d,
                   s5_glu_w, s5_glu_b) * jax.nn.silu(d_gate)
    gates = jax.nn.sigmoid(r.reshape(bsz, L, N_BRANCH, D_MODEL))
    m = (gates[:, :, 0] * (ya @ w_a_out.astype(ya.dtype))
         + gates[:, :, 1] * (yb @ w_b_out.astype(yb.dtype))
         + gates[:, :, 2] * (yc @ w_c_out.astype(yc.dtype))
         + gates[:, :, 3] * (yd @ w_d_out.astype(yd.dtype)))
    return x + m @ w_out.astype(m.dtype)


def setup_inputs(seed: int = 0) -> dict:
    key = jax.random.key(seed)
    ks = jax.random.split(key, 40)
    counter = [0]

    def nxt():
        k = ks[counter[0]]
        counter[0] += 1
        return k

    def nrm(shape, scale):
        return jax.random.normal(nxt(), shape, jnp.float32) * scale

    n_idx = jnp.arange(S5_STATE, dtype=jnp.float32)
    return {
        'x_prompt': nrm((BATCH, SEQ, D_MODEL), 1.0),
        'x_sample': nrm((DEC_BATCH, DEC_SEQ, D_MODEL), 1.0),
        'norm_g': 1.0 + nrm((DEPTH, D_MODEL), 0.02),
        'w_in': nrm((DEPTH, D_MODEL, N_IN), D_MODEL ** -0.5),
        'conv_w': nrm((DEPTH, CONV_WIDTH, W_A), CONV_WIDTH ** -0.5),
        'conv_b': nrm((DEPTH, W_A), 0.02),
        'conv_ln_g': 1.0 + nrm((DEPTH, W_A), 0.02),
        'conv_ln_b': nrm((DEPTH, W_A), 0.02),
        'w_a_out': nrm((DEPTH, W_A, D_MODEL), W_A ** -0.5),
        'pool_w': nrm((DEPTH, len(POOL_WINDOWS), POOL_GROUP, POOL_GROUP), POOL_GROUP ** -0.5),
        'pool_scale': 1.0 + nrm((DEPTH, W_B), 0.02),
        'w_b_out': nrm((DEPTH, W_B, D_MODEL), W_B ** -0.5),
        'hg_lb': nrm((DEPTH, 2, W_C), 1.0),
        'hg_norm_g': 1.0 + nrm((DEPTH, W_C), 0.02),
        'w_c_out': nrm((DEPTH, W_C, D_MODEL), W_C ** -0.5),
        's5_a_re': -0.5 + nrm((DEPTH, 2, S5_GROUPS, S5_STATE), 0.01),
        's5_a_im': math.pi * n_idx + nrm((DEPTH, 2, S5_GROUPS, S5_STATE), 0.01),
        's5_log_dt': math.log(DT_MIN) + (math.log(DT_MAX) - math.log(DT_MIN))
                     * jax.random.uniform(nxt(), (DEPTH, 2, S5_GROUPS), jnp.float32),
        's5_b_re': nrm((DEPTH, 2, S5_GROUPS, S5_STATE, S5_GROUP_CH), (2 * S5_GROUP_CH) ** -0.5),
        's5_b_im': nrm((DEPTH, 2, S5_GROUPS, S5_STATE, S5_GROUP_CH), (2 * S5_GROUP_CH) ** -0.5),
        's5_c_re': nrm((DEPTH, 2, S5_GROUPS, S5_GROUP_CH, S5_STATE), S5_STATE ** -0.5),
        's5_c_im': nrm((DEPTH, 2, S5_GROUPS, S5_GROUP_CH, S5_STATE), S5_STATE ** -0.5),
        's5_d': nrm((DEPTH, W_D), 1.0),
        's5_glu_w': nrm((DEPTH, W_D, W_D), W_D ** -0.5),
        's5_glu_b': nrm((DEPTH, W_D), 0.02),
        'w_d_out': nrm((DEPTH, W_D, D_MODEL), W_D ** -0.5),
        'w_out': nrm((DEPTH, D_MODEL, D_MODEL), D_MODEL ** -0.5),
        'final_g': 1.0 + nrm((D_MODEL,), 0.02),
    }


def reference(x_prompt, x_sample, norm_g, w_in, conv_w, conv_b, conv_ln_g, conv_ln_b, w_a_out, pool_w,
              pool_scale, w_b_out, hg_lb, hg_norm_g, w_c_out, s5_a_re, s5_a_im, s5_log_dt, s5_b_re, s5_b_im,
              s5_c_re, s5_c_im, s5_d, s5_glu_w, s5_glu_b, w_d_out, w_out, final_g):
    lb_all = jnp.cumsum(jax.nn.softmax(hg_lb.astype(jnp.float32), axis=0), axis=0)
    lb_all = lb_all - lb_all[:1]
    layered = (norm_g, w_in, conv_w, conv_b, conv_ln_g, conv_ln_b, w_a_out, pool_w, pool_scale, w_b_out,
               hg_norm_g, w_c_out, s5_a_re, s5_a_im, s5_log_dt, s5_b_re, s5_b_im, s5_c_re, s5_c_im, s5_d,
               s5_glu_w, s5_glu_b, w_d_out, w_out)

    def run(x):
        for l in range(DEPTH):
            x = _layer(x, lb_all[l], *[w[l] for w in layered])
        return _rms_norm(x, final_g)

    y_prompt = run(x_prompt)
    y_sample = run(x_sample)
    return (y_prompt, y_sample)
```

```python
import numpy as np
import concourse.bass as bass
import concourse.mybir as mybir
from concourse.bass_utils import run_bass_kernel_spmd

F32 = mybir.dt.float32
BF16 = mybir.dt.bfloat16
AF = mybir.ActivationFunctionType
ALU = mybir.AluOpType

D = 2048
WB = 1024
NIN = 20480
T = 512
HALO = 16
TE = T + 2 * HALO
EPS = 1e-6
CH = 64
PAD = 256
POOL_MOD = 1000
SW_S = 2
SW_H = 1
C_AVAL, C_AGLU, C_AGATE, C_BIN, C_BGATE, C_Q, C_FF, C_FB, C_I, C_CGATE, C_DIN, C_DGATE, C_R = (
    0, 8, 16, 24, 32, 40, 48, 56, 64, 72, 80, 88, 96)
V_NG, V_CB, V_LNG, V_LNB, V_PS, V_HG, V_SD, V_GB, V_LB0, V_LB1 = 0, 16, 24, 32, 40, 48, 56, 64, 72, 88
NVEC = 104


class Slot:
    __slots__ = ("w", "r")

    def __init__(self):
        self.w = {}
        self.r = {}


class EngW:
    def __init__(self, nc, eng, name, selfwait):
        self.eng = eng
        self.sem = nc.alloc_semaphore(name)
        self.n = 0
        self.seen = {}
        self.selfwait = selfwait


class Ker:
    def __init__(self, nc):
        self.nc = nc
        self.pe = EngW(nc, nc.tensor, "s_pe", False)
        self.act = EngW(nc, nc.scalar, "s_act", True)
        self.dve = EngW(nc, nc.vector, "s_dve", True)
        self.pool = EngW(nc, nc.gpsimd, "s_pool", True)
        self.sp = EngW(nc, nc.sync, "s_sp", False)
        self.ndsem = 24
        self.dsem = [nc.alloc_semaphore(f"s_dma{i}") for i in range(self.ndsem)]
        self.dval = [0] * self.ndsem
        self.dnext = 0
        self.slots = {}
        self.canon = {"xt": "arA", "ysf": "arA", "ubf": "arA", "zbf": "arA", ("cvt", 0): "arA", ("cvt", 1): "arA",
                      ("cvb", 0): "arA", ("cvb", 1): "arA",
                      "arB": "arB", ("ks", 1): "arB", "ue": "arB", ("cbuf", 0): "arB", ("cbuf", 1): "arB",
                      "hb": "arC", "mT": "arC", "hT": "arD", "fgb": "arD", "junk": "arD", "xh": "arB", ("psb", 0): "psb", ("psb", 1): "psb", "ya": "arY", "yb": "arY", ("kp", 0): "arY", ("kp", 1): "arY"}

    def slot(self, key):
        key = self.canon.get(key, key)
        s = self.slots.get(key)
        if s is None:
            s = Slot()
            self.slots[key] = s
        return s

    def _wait(self, E, R, W):
        deps = {}
        for k in R:
            for i, ev in self.slot(k).w.items():
                if deps.get(i, (None, 0))[1] < ev[1]:
                    deps[i] = ev
        for k in W:
            s = self.slot(k)
            for dd in (s.w, s.r):
                for i, ev in dd.items():
                    if deps.get(i, (None, 0))[1] < ev[1]:
                        deps[i] = ev
        for i, (sem, val) in deps.items():
            if sem is E.sem and not E.selfwait:
                continue
            if E.seen.get(i, 0) >= val:
                continue
            E.eng.wait_ge(sem, val)
            E.seen[i] = val

    def _update(self, ev, R, W):
        i = id(ev[0])
        for k in R:
            self.slot(k).r[i] = ev
        for k in W:
            s = self.slot(k)
            if s.r:
                s.r = {}
                s.w = {}
            s.w[i] = ev

    def op(self, E, fn, R=(), W=()):
        self._wait(E, R, W)
        ins = fn(E.eng)
        E.n += 1
        ins.then_inc(E.sem, 1)
        self._update((E.sem, E.n), R, W)

    def dma(self, out, in_, R=(), W=()):
        E = self.sp
        self._wait(E, R, W)
        k = self.dnext
        self.dnext = (k + 1) % self.ndsem
        sem = self.dsem[k]
        if self.dval[k] > 0 and E.seen.get(id(sem), 0) < self.dval[k]:
            E.eng.wait_ge(sem, self.dval[k])
            E.seen[id(sem)] = self.dval[k]
        E.eng.dma_start(out=out, in_=in_).then_inc(sem, 16)
        self.dval[k] += 16
        self._update((sem, self.dval[k]), R, W)

    def finish(self):
        E = self.sp
        for k in range(self.ndsem):
            if self.dval[k] > 0:
                E.eng.wait_ge(self.dsem[k], self.dval[k])
        for X in (self.pe, self.act, self.dve, self.pool):
            if X.n > 0:
                E.eng.wait_ge(X.sem, X.n)


def build(segs, depth=2):
    nc = bass.Bass("TRN2", target_bir_lowering=False)
    K = Ker(nc)
    pe, act, dve, pool = K.pe, K.act, K.dve, K.pool
    nseg = len(segs)

    def din(name, shape, dt=F32):
        return nc.dram_tensor(name, list(shape), dt, kind="ExternalInput").ap()

    def dscr(name, shape, dt=F32):
        return nc.dram_tensor(name, list(shape), dt).ap()

    xin = [din(f"x{s}", [segs[s] + 2 * HALO, D]) for s in range(nseg)]
    yout = [nc.dram_tensor(f"y{s}", [segs[s], D], F32, kind="ExternalOutput").ap() for s in range(nseg)]
    x1 = [dscr(f"x1_{s}", [segs[s] + 2 * HALO, D]) for s in range(nseg)]
    rcin = [din(f"rc{s}", [4, segs[s]]) for s in range(nseg)]
    cst = din("cst", [128, 128 + 256 + 256 + 512])
    norm_g = din("norm_g", [depth, D]); w_in = din("w_in", [depth, D, NIN])
    conv_w = din("conv_w", [depth, 31, WB]); conv_b = din("conv_b", [depth, WB])
    conv_ln_g = din("conv_ln_g", [depth, WB]); conv_ln_b = din("conv_ln_b", [depth, WB])
    w_a_out = din("w_a_out", [depth, WB, D]); pool_w = din("pool_w", [depth, 4, 256, 256])
    pool_scale = din("pool_scale", [depth, WB]); w_b_out = din("w_b_out", [depth, WB, D])
    hg_lb = din("hg_lb", [depth, 2, WB]); hg_norm_g = din("hg_norm_g", [depth, WB])
    w_c_out = din("w_c_out", [depth, WB, D])
    s5_a_re = din("s5_a_re", [depth, 2, 64, 64]); s5_a_im = din("s5_a_im", [depth, 2, 64, 64])
    s5_log_dt = din("s5_log_dt", [depth, 2, 64])
    s5_b_re = din("s5_b_re", [depth, 2, 64, 64, 16]); s5_b_im = din("s5_b_im", [depth, 2, 64, 64, 16])
    s5_c_re = din("s5_c_re", [depth, 2, 64, 16, 64]); s5_c_im = din("s5_c_im", [depth, 2, 64, 16, 64])
    s5_d = din("s5_d", [depth, WB]); s5_glu_w = din("s5_glu_w", [depth, WB, WB])
    s5_glu_b = din("s5_glu_b", [depth, WB]); w_d_out = din("w_d_out", [depth, WB, D])
    w_out = din("w_out", [depth, D, D]); final_g = din("final_g", [D])

    def wscr(name, Kdim, N):
        return dscr(name, [N // 128, 128, Kdim // 128, 128], BF16)
    wb_in = [wscr(f"wb_in{l}", D, NIN) for l in range(depth)]
    wb_ao = [[wscr(f"wb_o{i}_{l}", WB, D) for i in range(4)] for l in range(depth)]
    wb_out = [wscr(f"wb_out{l}", D, D) for l in range(depth)]
    wb_glu = [wscr(f"wb_glu{l}", WB, WB) for l in range(depth)]
    wb_pool = [[wscr(f"wb_pool{l}_{g}", 256, 256) for g in range(4)] for l in range(depth)]
    cdiag = [dscr(f"cdiag{l}", [8, 128, 31, 128], BF16) for l in range(depth)]
    tabd = [[dscr(f"tabd{l}_{d}", [32, 128, 2, T]) for d in range(2)] for l in range(depth)]
    sp_q = [dscr(f"sp_q{s}", [8, 128, segs[s]]) for s in range(nseg)]
    sp_o = [dscr(f"sp_o{s}", [8, 128, segs[s]]) for s in range(nseg)]
    sp_y = [dscr(f"sp_y{s}", [8, 128, segs[s]]) for s in range(nseg)]
    sp_u = [dscr(f"sp_u{s}", [8, 128, segs[s]], BF16) for s in range(nseg)]
    sp_v = [dscr(f"sp_v{s}", [segs[s], WB], BF16) for s in range(nseg)]

    def sb(name, shape, dt=F32):
        return nc.alloc_sbuf_tensor(name, list(shape), dt)
    cs = sb("cs", [128, 128 + 256 + 256 + 512])
    ident = cs[:, 0:128]
    maskF = cs[:, 128:384]
    maskB = cs[:, 384:640]
    rmask = cs[:, 640:1152]
    identb = sb("identb", [128, 128], BF16)
    onesW = sb("onesW", [128, 128], BF16)
    onesH = sb("onesH", [128, 128], BF16)
    vecT = sb("vecT", [128, NVEC])
    lbv = sb("lbv", [128, 16]); omlb = sb("omlb", [128, 16])
    arA = sb("arA", [128, 8192])
    arB = sb("arB", [128, 4352])
    arC = sb("arC", [128, 5120])
    arD = sb("arD", [128, 4352])
    xt = arA[:, :].rearrange("p (s d) -> p s d", d=D)
    ysf = arA[:, 0:4096].rearrange("p (j t) -> p j t", t=T)
    ubf = arA[:, 4096:6144].bitcast(BF16).rearrange("p (j t) -> p j t", t=T)
    zbf = arA[:, 6144:8192].bitcast(BF16).rearrange("p (j t) -> p j t", t=T)
    cvt = [arA[:, i * 2048:(i + 1) * 2048].rearrange("p (k n) -> p k n", n=128) for i in range(2)]
    cvb = [arA[:, 4096 + i * 1024:4096 + (i + 1) * 1024].bitcast(BF16).rearrange("p (k n) -> p k n", n=128) for i in range(2)]
    s5buf = [arB[:, i * 512:(i + 1) * 512] for i in range(4)]
    tbuf = [arB[:, 2048 + i * 1024:2048 + (i + 1) * 1024].rearrange("p (a t) -> p a t", t=T) for i in range(2)]
    ks = [[s5buf[0]]]
    ue = arB[:, 0:2176].bitcast(BF16).rearrange("p (c t) -> p c t", t=TE)
    cb0 = arB[:, 2176:2176 + 1984].bitcast(BF16).rearrange("p (j n) -> p j n", n=128)
    cbuf = [cb0, cb0]
    hb = arC[:, :].bitcast(BF16).rearrange("p (s d) -> p s d", d=D)
    mT = arC[:, 0:4096].bitcast(BF16).rearrange("p (c t) -> p c t", t=T)
    hT = arD[:, :].bitcast(BF16).rearrange("p (c t) -> p c t", t=TE)
    fgb = arD[:, 0:D]
    xh = arB[0:32, 0:D]
    NWB = 3
    wbuf = [sb(f"wbuf{i}", [128, 16, 128], BF16) for i in range(NWB)]
    ss = sb("ss", [128, 8]); rstd = sb("rstd", [128, 8])
    junk = arD[:, 2048:3072].bitcast(BF16)
    NSC = 10
    sc = [sb(f"sc{i}", [128, TE]) for i in range(NSC)]
    bq = sb("bq", [128, T], BF16); bk = sb("bk", [128, T], BF16); bvf = sb("bvf", [128, T], BF16)
    vT = sb("vT", [128, 8, 4, 128], BF16)
    kT = sb("kT", [128, 4, 128], BF16)
    At = sb("At", [128, 4, CH], BF16)
    S32 = sb("S32", [128, 8, 128]); Sbf = sb("Sbf", [128, 8, 128], BF16); Tm = sb("Tm", [128, 128])
    Hb = [[sb(f"Hb{a}{b}", [128, T], BF16) for b in range(2)] for a in range(2)]
    s5p = sb("s5p", [128, 32, 32])
    hc = sb("hc", [128, 2, 32]); cin = sb("cin", [128, 2, 32]); tmp32 = sb("tmp32", [128, 4, 32])
    Bm = sb("Bm", [128, 2, 16, 128], BF16)
    Cm = sb("Cm", [128, 2, 32, 64], BF16)
    stg = sb("stg", [128, 128])
    stgA = sb("stgA", [128, 128]); stgB = sb("stgB", [128, 128])
    stg2 = sb("stg2", [128, 256])
    tmpi = sb("tmpi", [128, 32], mybir.dt.int32)
    arY = sb("arY", [128, 4096])
    ya = arY[:, 0:2048].bitcast(BF16).rearrange("p (c t) -> p c t", t=T)
    yb = arY[:, 2048:4096].bitcast(BF16).rearrange("p (c t) -> p c t", t=T)
    ks2 = [[arY[:, (2 * a + b) * 1024:(2 * a + b + 1) * 1024] for b in range(2)] for a in range(2)]
    yc = sb("yc", [128, 8, T], BF16); yd = sb("yd", [128, 8, T], BF16)
    rcb = sc[8][:, 0:T]
    print("sbuf remaining", nc.sbuf_bytes_remaining)
    ps = [nc.alloc_psum_tensor(f"ps{i}", [128, T], F32) for i in range(6)]
    psb = nc.alloc_psum_tensor("psb", [128, 2 * T], BF16)
    psx = nc.alloc_psum_tensor("psx", [128, T], F32)
    ps5b = ps[5][:, 256:512].bitcast(BF16)
    pcnt = [0]

    def nps():
        i = pcnt[0] % 2
        pcnt[0] += 1
        return ps[i], ("ps", i)

    def A(E, out, in_, func, R, W, bias=None, scale=None, accum=None):
        kw = {}
        if bias is not None:
            kw["bias"] = bias
        if scale is not None:
            kw["scale"] = scale
        if accum is not None:
            kw["accum_out"] = accum
        K.op(act, lambda e: e.activation(out=out, in_=in_, func=func, **kw), R, W)

    def TS(E, out, in0, s1, s2, op0, op1, R, W):
        if s2 is None:
            K.op(E, lambda e: e.tensor_scalar(out, in0, s1, None, op0), R, W)
        else:
            K.op(E, lambda e: e.tensor_scalar(out, in0, s1, s2, op0, op1), R, W)

    def TT(E, out, in0, in1, op, R, W):
        K.op(E, lambda e: e.tensor_tensor(out, in0, in1, op), R, W)

    def STT(E, out, in0, scalar, in1, op0, op1, R, W):
        K.op(E, lambda e: e.scalar_tensor_tensor(out, in0, scalar, in1, op0, op1), R, W)

    def MM(out, lhsT, rhs, start, stop, R, W):
        K.op(pe, lambda e: e.matmul(out, lhsT, rhs, start=start, stop=stop), R, W)

    def TR(out, in_, idn, R, W):
        K.op(pe, lambda e: e.transpose(out, in_, idn), R, W)

    wcnt = [0]

    def linear(wb, nk, mlist, rhs_fn, ncols_list, evac, Rin):
        n = len(mlist)
        loaded = {}

        def load(i):
            b = wcnt[0] % NWB
            wcnt[0] += 1
            K.dma(wbuf[b][:, 0:nk, :], wb[mlist[i]], R=[], W=[("wbuf", b)])
            loaded[i] = b
        for i in range(min(NWB - 1, n)):
            load(i)
        for i in range(n):
            if i + NWB - 1 < n:
                load(i + NWB - 1)
            b = loaded.pop(i)
            for gi, (c0, ncol) in enumerate(ncols_list):
                p, pslot = nps()
                for kc in range(nk):
                    MM(p[:, 0:ncol], wbuf[b][:, kc, :], rhs_fn(kc)[:, c0:c0 + ncol], kc == 0, kc == nk - 1,
                       R=[("wbuf", b)] + Rin, W=[pslot])
                evac(i, mlist[i], gi, p[:, 0:ncol], pslot)

    ccnt = [0]

    def convert(src, Kdim, N, dst):
        nk = Kdim // 128
        srcv = src.rearrange("(kc kp) n -> kp kc n", kp=128)
        dstv = dst.rearrange("m kp kc j -> kp m kc j")
        for n0 in range(0, N, 128):
            i = ccnt[0] % 2
            ccnt[0] += 1
            K.dma(cvt[i][:, 0:nk, :], srcv[:, :, n0:n0 + 128], R=[], W=[("cvt", i)])
            eng = act if (ccnt[0] % 2 == 0) else dve
            if eng is act:
                A(act, cvb[i][:, 0:nk, :], cvt[i][:, 0:nk, :], AF.Copy, R=[("cvt", i)], W=[("cvb", i)])
            else:
                K.op(dve, lambda e: e.tensor_copy(cvb[i][:, 0:nk, :], cvt[i][:, 0:nk, :]), R=[("cvt", i)], W=[("cvb", i)])
            K.dma(dstv[:, n0 // 128, :, :], cvb[i][:, 0:nk, :], R=[("cvb", i)], W=[("wdram",)])

    K.dma(cs[:], cst[:, :], W=["cs"])
    K.op(dve, lambda e: e.tensor_copy(identb[:], ident), R=["cs"], W=["identb"])
    K.op(dve, lambda e: e.memset(ss[:], 1.0), W=["ss"])
    K.op(dve, lambda e: e.memset(onesW[:], 1.0 / 1024.0), W=["onesW"])
    K.op(dve, lambda e: e.memset(onesH[:], 1.0 / 128.0), W=["onesH"])
    for s in range(nseg):
        for off in (0, HALO + segs[s]):
            K.op(dve, lambda e: e.memset(xh[0:HALO, :], 0.0), W=["xh"])
            K.dma(x1[s][off:off + HALO, :], xh[0:HALO, :], R=["xh"], W=[("x1", s)])

    for l in range(depth):
        convert(w_in[l], D, NIN, wb_in[l])
        for i, wsrc in enumerate((w_a_out, w_b_out, w_c_out, w_d_out)):
            convert(wsrc[l], WB, D, wb_ao[l][i])
        convert(w_out[l], D, D, wb_out[l])
        convert(s5_glu_w[l], WB, WB, wb_glu[l])
        for g in range(4):
            convert(pool_w[l, g], 256, 256, wb_pool[l][g])

    def prep_layer(l):
        rows = [(norm_g[l], 16), (conv_b[l], 8), (conv_ln_g[l], 8), (conv_ln_b[l], 8), (pool_scale[l], 8),
                (hg_norm_g[l], 8), (s5_d[l], 8), (s5_glu_b[l], 8), (hg_lb[0, 0], 8), (hg_lb[0, 1], 8),
                (hg_lb[1 if depth > 1 else 0, 0], 8), (hg_lb[1 if depth > 1 else 0, 1], 8)]
        r = 0
        K.op(dve, lambda e: e.memset(stg[:], 0.0), W=["stg"])
        for v, n in rows:
            K.dma(stg[r:r + n, :], v.rearrange("(c p) -> c p", p=128), W=["stg"])
            r += n
        TR(psx[:, 0:128], stg[:, :], ident, R=["stg", "cs"], W=["psx"])
        K.op(dve, lambda e: e.tensor_copy(vecT[:], psx[:, 0:NVEC]), R=["psx"], W=["vecT"])
        if l == 0:
            K.op(dve, lambda e: e.memset(lbv[:], 0.0), W=["lbv"])
        else:
            TT(dve, lbv[:], vecT[:, V_LB1:V_LB1 + 16], vecT[:, V_LB0:V_LB0 + 16], ALU.subtract, R=["vecT"], W=["lbv"])
            A(act, lbv[:], lbv[:], AF.Sigmoid, R=["lbv"], W=["lbv"])
        TS(dve, omlb[:], lbv[:], -1.0, 1.0, ALU.mult, ALU.add, R=["lbv"], W=["omlb"])
        for half in range(2):
            nr = 128 if half == 0 else 31 * 8 - 128
            K.op(dve, lambda e: e.memset(stg[:], 0.0), W=["stg"])
            src = conv_w[l].rearrange("j (c p) -> (j c) p", p=128)
            K.dma(stg[0:nr, :], src[half * 128:half * 128 + nr, :], W=["stg"])
            TR(psx[:, 0:128], stg[:, :], ident, R=["stg", "cs"], W=["psx"])
            K.op(dve, lambda e: e.tensor_copy(stg2[:, half * 128:half * 128 + 128], psx[:, 0:128]), R=["psx"], W=["stg2"])
        for c in range(8):
            b = c % 2
            for j in range(31):
                col = j * 8 + c
                TS(dve, cbuf[b][:, j, :], ident, stg2[:, col:col + 1], None, ALU.mult, None, R=["stg2", "cs"], W=[("cbuf", b)])
            K.dma(cdiag[l][c], cbuf[b][:], R=[("cbuf", b)], W=[("cdiag",)])

    def prep_s5(l, d):
        def ldT(src, dstcol):
            K.op(dve, lambda e: e.memset(stg[:], 0.0), W=["stg"])
            K.dma(stg[0:32, :], src.rearrange("(q two) p -> q (two p)", two=2), W=["stg"])
            TR(psx[:, 0:128], stg[:, :], ident, R=["stg", "cs"], W=["psx"])
            K.op(dve, lambda e: e.tensor_copy(tmp32[:, dstcol, :], psx[:, 0:32]), R=["psx"], W=["tmp32"])
        ldT(s5_a_re[l, d], 0)
        ldT(s5_a_im[l, d], 1)
        K.op(dve, lambda e: e.memset(stg[:], 0.0), W=["stg"])
        K.dma(stg2[0:32, 0:2], s5_log_dt[l, d].rearrange("(q two) -> q two", two=2), W=["stg2"])
        for two in range(2):
            TS(dve, stg[0:32, two * 64:(two + 1) * 64], stg[0:32, two * 64:(two + 1) * 64], stg2[0:32, two:two + 1], None,
               ALU.add, None, R=["stg", "stg2"], W=["stg"])
        TR(psx[:, 0:128], stg[:, :], ident, R=["stg", "cs"], W=["psx"])
        A(act, tmp32[:, 2, :], psx[:, 0:32], AF.Exp, R=["psx"], W=["tmp32"])
        are, aim, dt = tmp32[:, 0, :], tmp32[:, 1, :], tmp32[:, 2, :]
        t3 = tmp32[:, 3, :]
        R_, W_ = ["tmp32", "s5p"], ["tmp32", "s5p"]
        mag, ang, sn, cn = s5p[:, :, 27], s5p[:, :, 28], s5p[:, :, 29], s5p[:, :, 30]
        TT(dve, mag, dt, are, ALU.mult, R_, W_)
        A(act, mag, mag, AF.Exp, R_, W_)
        TT(dve, ang, dt, aim, ALU.mult, R_, W_)
        TWO_PI = 2.0 * np.pi

        def sin_of(dst, phase):
            TS(dve, dst, ang, 1.0 / TWO_PI, phase / TWO_PI, ALU.mult, ALU.add, R_, W_)
            K.op(dve, lambda e: e.tensor_copy(tmpi[:], dst), R_ + ["tmpi"], W_ + ["tmpi"])
            K.op(dve, lambda e: e.tensor_copy(t3, tmpi[:]), R_ + ["tmpi"], W_)
            TT(dve, dst, dst, t3, ALU.subtract, R_, W_)
            TS(dve, t3, dst, 0.5, None, ALU.is_gt, None, R_, W_)
            TT(dve, dst, dst, t3, ALU.subtract, R_, W_)
            TS(dve, t3, dst, -0.5, None, ALU.is_lt, None, R_, W_)
            TT(dve, dst, dst, t3, ALU.add, R_, W_)
            A(act, dst, dst, AF.Sin, R_, W_, scale=TWO_PI)
        sin_of(sn, 0.0)
        sin_of(cn, np.pi / 2)
        lre, lim = s5p[:, :, 1], s5p[:, :, 2]
        TT(dve, lre, mag, cn, ALU.mult, R_, W_)
        TT(dve, lim, mag, sn, ALU.mult, R_, W_)
        K.op(dve, lambda e: e.tensor_copy(s5p[:, :, 0], mag), R_, W_)
        K.op(dve, lambda e: e.tensor_copy(s5p[:, :, 8], cn), R_, W_)
        K.op(dve, lambda e: e.tensor_copy(s5p[:, :, 9], sn), R_, W_)
        den, xm, gre, gim = s5p[:, :, 27], s5p[:, :, 28], s5p[:, :, 29], s5p[:, :, 30]
        TT(dve, den, are, are, ALU.mult, R_, W_)
        TT(dve, t3, aim, aim, ALU.mult, R_, W_)
        TT(dve, den, den, t3, ALU.add, R_, W_)
        K.op(dve, lambda e: e.reciprocal(den, den), R_, W_)
        TS(dve, xm, lre, -1.0, None, ALU.add, None, R_, W_)
        TT(dve, gre, xm, are, ALU.mult, R_, W_)
        TT(dve, t3, lim, aim, ALU.mult, R_, W_)
        TT(dve, gre, gre, t3, ALU.add, R_, W_)
        TT(dve, gre, gre, den, ALU.mult, R_, W_)
        TT(dve, gim, lim, are, ALU.mult, R_, W_)
        TT(dve, t3, xm, aim, ALU.mult, R_, W_)
        TT(dve, gim, gim, t3, ALU.subtract, R_, W_)
        TT(dve, gim, gim, den, ALU.mult, R_, W_)
        for k in range(1, 10):
            pr, pi_, qr, qi = s5p[:, :, 6 + 2 * k], s5p[:, :, 7 + 2 * k], s5p[:, :, 8 + 2 * k], s5p[:, :, 9 + 2 * k]
            TT(dve, qr, pr, pr, ALU.mult, R_, W_)
            TT(dve, t3, pi_, pi_, ALU.mult, R_, W_)
            TT(dve, qr, qr, t3, ALU.subtract, R_, W_)
            TT(dve, qi, pr, pi_, ALU.mult, R_, W_)
            TS(dve, qi, qi, 2.0, None, ALU.mult, None, R_, W_)
        TS(dve, s5p[:, :, 3], s5p[:, :, 9], -1.0, None, ALU.mult, None, R_, W_)
        for q in range(32):
            tb = tbuf[q % 2]
            tn = ("tb", q % 2)
            cc, ssn = tb[:, 0, :], tb[:, 1, :]
            K.op(dve, lambda e: e.memset(cc[:, 0:1], 1.0), W=[tn, "arB"])
            K.op(dve, lambda e: e.memset(ssn[:, 0:1], 0.0), W=[tn])
            for k in range(9):
                n = 1 << k
                ur, ui = s5p[:, q, 8 + 2 * k:9 + 2 * k], s5p[:, q, 9 + 2 * k:10 + 2 * k]
                TS(dve, cc[:, n:2 * n], cc[:, 0:n], ur, None, ALU.mult, None, R=[tn, "s5p"], W=[tn])
                TS(dve, ssn[:, n:2 * n], cc[:, 0:n], ui, None, ALU.mult, None, R=[tn, "s5p"], W=[tn])
                TS(dve, sc[0][:, 0:n], ssn[:, 0:n], ui, None, ALU.mult, None, R=[tn, "s5p"], W=["sc0"])
                TT(dve, cc[:, n:2 * n], cc[:, n:2 * n], sc[0][:, 0:n], ALU.subtract, R=[tn, "sc0"], W=[tn])
                STT(dve, ssn[:, n:2 * n], ssn[:, 0:n], ur, ssn[:, n:2 * n], ALU.mult, ALU.add, R=[tn, "s5p"], W=[tn])
            K.dma(tabd[l][d][q], tb, R=[tn, "arB"], W=[("tabd",)])
        K.op(dve, lambda e: e.memset(Bm[:], 0.0), W=["Bm"])
        K.op(dve, lambda e: e.memset(Cm[:], 0.0), W=["Cm"])
        for j in range(8):
            K.op(dve, lambda e: e.memset(stgA[:], 0.0), W=["stgA"])
            K.op(dve, lambda e: e.memset(stgB[:], 0.0), W=["stgB"])
            for qq in range(4):
                q = 4 * j + qq
                base = 32 * qq
                K.dma(stg2[:, 0:16], s5_b_re[l, d, 2 * q:2 * q + 2].rearrange("g p c -> (g p) c"), W=["stg2"])
                K.dma(stg2[:, 16:32], s5_b_im[l, d, 2 * q:2 * q + 2].rearrange("g p c -> (g p) c"), W=["stg2"])
                br, bi = stg2[:, 0:16], stg2[:, 16:32]
                o1, o2 = stg2[:, 32:48], stg2[:, 48:64]
                grq, giq = s5p[:, q, 29:30], s5p[:, q, 30:31]
                Rq, Wq = ["stg2", "s5p"], ["stg2"]
                TS(dve, o1, bi, giq, -1.0, ALU.mult, ALU.mult, Rq, Wq)
                STT(dve, o1, br, grq, o1, ALU.mult, ALU.add, Rq, Wq)
                TS(dve, o2, br, giq, None, ALU.mult, None, Rq, Wq)
                STT(dve, o2, bi, grq, o2, ALU.mult, ALU.add, Rq, Wq)
                for o, st, sn_ in ((o1, stgA, "stgA"), (o2, stgB, "stgB")):
                    K.op(dve, lambda e: e.tensor_copy(st[0:64, base:base + 16], o[0:64, :]), R=["stg2"], W=[sn_])
                    K.op(dve, lambda e: e.tensor_copy(st[64:128, base + 16:base + 32], o[64:128, :]), R=["stg2"], W=[sn_])
            for ri, (st, sn_) in enumerate(((stgA, "stgA"), (stgB, "stgB"))):
                TR(psx[:, 0:128], st[:, :], ident, R=[sn_, "cs"], W=["psx"])
                K.op(dve, lambda e: e.tensor_copy(Bm[:, ri, j, :], psx[:, 0:128]), R=["psx"], W=["Bm"])
                K.op(dve, lambda e: e.tensor_copy(Bm[:, ri, 8 + j, :], psx[:, 0:128]), R=["psx"], W=["Bm"])
                K.op(dve, lambda e: e.memset(Bm[64:96, ri, 8 + j, :], 0.0), W=["Bm"])
            for ri, csrc in enumerate((s5_c_re, s5_c_im)):
                K.op(dve, lambda e: e.memset(stgA[:], 0.0), W=["stgA"])
                for qq in range(4):
                    q = 4 * j + qq
                    K.dma(stgA[32 * qq:32 * qq + 16, 0:64], csrc[l, d, 2 * q], W=["stgA"])
                    K.dma(stgA[32 * qq + 16:32 * qq + 32, 64:128], csrc[l, d, 2 * q + 1], W=["stgA"])
                TR(psx[:, 0:128], stgA[:, :], ident, R=["stgA", "cs"], W=["psx"])
                for qq in range(4):
                    co = 32 if qq == 3 else 0
                    cdst = Cm[:, ri, 4 * j + qq, co:co + 32]
                    TS(dve, cdst, psx[:, 32 * qq:32 * qq + 32], 1.0 if ri == 0 else -1.0, None, ALU.mult, None, R=["psx"], W=["Cm"])

    def load_norm(src, seg, r0, l, ext):
        K.dma(xt, src[HALO + r0:HALO + r0 + T, :].rearrange("(s p) d -> p s d", p=128), W=["xt"])
        nsub = 4
        if ext:
            K.dma(xh[0:HALO, :], src[r0:r0 + HALO, :], W=["xh"])
            K.dma(xh[HALO:2 * HALO, :], src[HALO + r0 + T:HALO + r0 + T + HALO, :], W=["xh"])
            nsub = 5
        for s in range(nsub):
            np_ = 128 if s < 4 else 32
            xin_ = xt[:, s, :] if s < 4 else xh[:, :]
            rs = ["xt"] if s < 4 else ["xh"]
            A(act, junk[0:np_, :], xin_, AF.Square, R=rs, W=["junk", "ss"], accum=ss[0:np_, s:s + 1])
        TS(dve, rstd[:, 0:nsub], ss[:, 0:nsub], 1.0 / D, EPS, ALU.mult, ALU.add, R=["ss"], W=["rstd"])
        A(act, rstd[:, 0:nsub], rstd[:, 0:nsub], AF.Ln, R=["rstd"], W=["rstd"])
        A(act, rstd[:, 0:nsub], rstd[:, 0:nsub], AF.Exp, R=["rstd"], W=["rstd"], scale=-0.5)
        for s in range(nsub):
            np_ = 128 if s < 4 else 32
            xin_ = xt[:, s, :] if s < 4 else xh[:, :]
            rs = ["xt"] if s < 4 else ["xh"]
            A(act, hb[0:np_, s, :], xin_, AF.Copy, R=rs + ["rstd"], W=["hb"], scale=rstd[0:np_, s:s + 1])
        for dc in range(16):
            half = dc % 2
            pb = psb[:, 0:T] if half == 0 else ps[5][:, 0:256].bitcast(BF16)
            pbn = "psb" if half == 0 else ("ps", 5)
            for s in range(4):
                TR(pb[:, s * 128:(s + 1) * 128], hb[:, s, dc * 128:(dc + 1) * 128], identb[:], R=["hb", "identb"], W=[pbn])
            A(act, hT[:, dc, HALO:HALO + T], pb, AF.Copy, R=[pbn, "vecT"], W=["hT"], scale=vecT[:, V_NG + dc:V_NG + dc + 1])
            if ext:
                TR(psx[:, 0:32].bitcast(BF16)[:, 0:32], hb[0:32, 4, dc * 128:(dc + 1) * 128], identb[0:32, 0:32], R=["hb", "identb"], W=["psx"])
                pxb = psx[:, 0:32].bitcast(BF16)
                A(act, hT[:, dc, 0:HALO], pxb[:, 0:HALO], AF.Copy, R=["psx", "vecT"], W=["hT"], scale=vecT[:, V_NG + dc:V_NG + dc + 1])
                A(act, hT[:, dc, HALO + T:TE], pxb[:, HALO:2 * HALO], AF.Copy, R=["psx", "vecT"], W=["hT"], scale=vecT[:, V_NG + dc:V_NG + dc + 1])

    def hT_k(kc):
        return hT[:, kc, :]

    def reset_states():
        K.op(dve, lambda e: e.memset(S32[:], 0.0), W=["S32"])
        K.op(dve, lambda e: e.memset(Sbf[:], 0.0), W=["Sbf"])
        K.op(dve, lambda e: e.memset(hc[:], 0.0), W=["hc"])

    def hg_head(hd, d):
        q_, sg, f_, lf, b_, e1, e2, k_ = (sc[i][:, 0:T] for i in range(8))
        lcol = hd if d == 0 else 8 + hd
        TS(dve, f_, sg, omlb[:, lcol:lcol + 1], lbv[:, lcol:lcol + 1], ALU.mult, ALU.add, R=["sc1", "omlb", "lbv"], W=["sc2"])
        A(act, lf, f_, AF.Ln, R=["sc2"], W=["sc3"])
        TS(dve, k_, f_, -1.0, 1.0, ALU.mult, ALU.add, R=["sc2"], W=["sc7"])
        yield
        K.op(dve, lambda e: e.tensor_tensor_scan(b_, rmask, lf, 0.0, ALU.mult, ALU.add), R=["cs", "sc3"], W=["sc4"])
        yield
        bend = sc[4][:, 0:T].rearrange("p (c t) -> p c t", t=CH)[:, :, CH - 1]
        A(act, sc[8][:, 0:8], bend, AF.Exp, R=["sc4"], W=["sc8"])
        if d == 0:
            A(act, e1, b_, AF.Exp, R=["sc4"], W=["sc5"])
            A(act, e2, b_, AF.Exp, R=["sc4"], W=["sc6"], scale=-1.0)
        else:
            TT(dve, b_, b_, lf, ALU.subtract, R=["sc4", "sc3"], W=["sc4"])
            A(act, e1, b_, AF.Exp, R=["sc4"], W=["sc5"], scale=-1.0)
            A(act, e2, b_, AF.Exp, R=["sc4"], W=["sc6"])
        yield
        TT(dve, bq[:], q_, e1, ALU.mult, R=["sc0", "sc5"], W=["bq"])
        TT(dve, bk[:], k_, e2, ALU.mult, R=["sc7", "sc6"], W=["bk"])
        for s in range(4):
            TR(ps5b[:, s * 128:(s + 1) * 128], bk[:, s * 128:(s + 1) * 128], identb[:], R=["bk", "identb"], W=[("ps", 5)])
        K.op(dve, lambda e: e.tensor_copy(kT[:], ps5b.rearrange("p (s k) -> p s k", k=128)), R=[("ps", 5)], W=["kT"])
        yield
        for c in range(8):
            h64 = (c % 2) * 64
            MM(ps[5][h64:h64 + 64, (c // 2) * CH:(c // 2 + 1) * CH], bk[:, c * CH:(c + 1) * CH], bq[:, c * CH:(c + 1) * CH],
               True, True, R=["bk", "bq"], W=[("ps", 5)])
        yield
        msk = maskF if d == 0 else maskB
        TT(dve, At[:], ps[5][:, 0:256].rearrange("p (a t) -> p a t", t=CH), msk.rearrange("p (a t) -> p a t", t=CH),
           ALU.mult, R=[("ps", 5), "cs"], W=["At"])
        order = range(8) if d == 0 else range(7, -1, -1)
        for c in order:
            h64 = (c % 2) * 64
            et = sc[8][:, c:c + 1]
            if d == 1:
                TS(dve, S32[:, hd, :], S32[:, hd, :], et, None, ALU.mult, None, R=["S32", "sc8"], W=["S32"])
                A(act, Sbf[:, hd, :], S32[:, hd, :], AF.Copy, R=["S32"], W=["Sbf"])
            oc = ps[4][:, c * CH:(c + 1) * CH]
            MM(oc, vT[h64:h64 + 64, hd, c // 2, :], At[h64:h64 + 64, c // 2, :], True, False, R=["vT", "At"], W=[("ps", 4)])
            MM(oc, Sbf[:, hd, :], bq[:, c * CH:(c + 1) * CH], False, True, R=["Sbf", "bq"], W=[("ps", 4)])
            MM(psx[:, 0:128], kT[h64:h64 + 64, c // 2, :], vT[h64:h64 + 64, hd, c // 2, :], True, True, R=["kT", "vT"], W=["psx"])
            yield
            if d == 0:
                TT(dve, Tm[:], psx[:, 0:128], S32[:, hd, :], ALU.add, R=["psx", "S32"], W=["Tm"])
                TS(dve, S32[:, hd, :], Tm[:], et, None, ALU.mult, None, R=["Tm", "sc8"], W=["S32"])
                A(act, Sbf[:, hd, :], S32[:, hd, :], AF.Copy, R=["S32"], W=["Sbf"])
            else:
                TT(dve, S32[:, hd, :], psx[:, 0:128], S32[:, hd, :], ALU.add, R=["psx", "S32"], W=["S32"])
            yield

    def run_streams(gens, weights):
        alive = [True] * len(gens)
        while any(alive):
            for gi, g in enumerate(gens):
                if not alive[gi]:
                    continue
                for _ in range(weights[gi]):
                    try:
                        next(g)
                    except StopIteration:
                        alive[gi] = False
                        break

    def s5_dir(l, d):
        ETc, ETs = s5p[:, :, 26], s5p[:, :, 27]
        R_, W_ = ["s5p", "hc", "cin", "tmp32"], ["cin", "tmp32"]
        TT(dve, cin[:, 0, :], ETc, hc[:, 0, :], ALU.mult, R_, W_)
        TT(dve, tmp32[:, 0, :], ETs, hc[:, 1, :], ALU.mult, R_, W_)
        TT(dve, cin[:, 0, :], cin[:, 0, :], tmp32[:, 0, :], ALU.subtract, R_, W_)
        TT(dve, cin[:, 1, :], ETc, hc[:, 1, :], ALU.mult, R_, W_)
        TT(dve, tmp32[:, 0, :], ETs, hc[:, 0, :], ALU.mult, R_, W_)
        TT(dve, cin[:, 1, :], cin[:, 1, :], tmp32[:, 0, :], ALU.add, R_, W_)
        bA, bB, bC, bD = s5buf
        psI = psb[:, :].bitcast(F32)
        K.dma(tbuf[0], tabd[l][d][0], R=[("tabd",)], W=["arB", ("tb", 0)])
        pending = []
        for q in range(32):
            j, base = q // 4, 32 * (q % 4)
            tb = tbuf[q % 2]
            tn = ("tb", q % 2)
            if q + 1 < 32:
                K.dma(tbuf[(q + 1) % 2], tabd[l][d][q + 1], R=[("tabd",)], W=[("tb", (q + 1) % 2)])
            hsel = q % 2
            hbq = Hb[hsel]
            hbn = ("Hb", hsel)
            for ri, (pt, pn) in enumerate(((ps[2][:, :], ("ps", 2)), (psI, "psb"))):
                if base == 96:
                    MM(pt, Bm[64:128, ri, 8 + j, :], ubf[64:128, j, :], True, True, R=["Bm", "ubf"], W=[pn])
                else:
                    MM(pt, Bm[base:base + 32, ri, j, :], ubf[base:base + 32, j, :], True, True, R=["Bm", "ubf"], W=[pn])
            if d == 0:
                vr, vi = ps[2][:, :], psI
                hro, hio = hbq[0][:], hbq[1][:]
            else:
                vr, vi = ps[2][:, ::-1], psI[:, ::-1]
                hro, hio = hbq[0][:, ::-1], hbq[1][:, ::-1]
            cc, ssn = tb[:, 0, :], tb[:, 1, :]
            TT(dve, bA, cc, vr, ALU.mult, R=[tn, ("ps", 2)], W=["s5A"])
            TT(dve, bB, ssn, vi, ALU.mult, R=[tn, "psb"], W=["s5B"])
            TT(dve, bC, cc, vi, ALU.mult, R=[tn, "psb"], W=["s5C"])
            TT(dve, bD, ssn, vr, ALU.mult, R=[tn, ("ps", 2)], W=["s5D"])
            TT(dve, bA, bA, bB, ALU.add, R=["s5A", "s5B"], W=["s5A"])
            TT(dve, bC, bC, bD, ALU.subtract, R=["s5C", "s5D"], W=["s5C"])
            yield
            rho_bc = s5p[:, q, 0:1].to_broadcast([128, T])
            K.op(dve, lambda e: e.tensor_tensor_scan(bB, rho_bc, bA, cin[:, 0, q:q + 1], ALU.mult, ALU.add), R=["s5A", "s5p", "cin"], W=["s5B"])
            K.op(dve, lambda e: e.tensor_tensor_scan(bD, rho_bc, bC, cin[:, 1, q:q + 1], ALU.mult, ALU.add), R=["s5C", "s5p", "cin"], W=["s5D"])
            yield
            TT(dve, bA, cc, bB, ALU.mult, R=[tn, "s5B"], W=["s5A"])
            TT(dve, bC, ssn, bD, ALU.mult, R=[tn, "s5D"], W=["s5C"])
            TT(dve, hro, bA, bC, ALU.subtract, R=["s5A", "s5C", "arB"], W=[hbn])
            TT(dve, bA, cc, bD, ALU.mult, R=[tn, "s5D"], W=["s5A"])
            TT(dve, bC, ssn, bB, ALU.mult, R=[tn, "s5B"], W=["s5C"])
            TT(dve, hio, bA, bC, ALU.add, R=["s5A", "s5C", "arB"], W=[hbn])
            A(act, hc[:, 0, q:q + 1], bB[:, T - 1:T], AF.Copy, R=["s5B", "cin"], W=["hc"])
            A(act, hc[:, 1, q:q + 1], bD[:, T - 1:T], AF.Copy, R=["s5D", "cin"], W=["hc"])

            def cmm(q=q, j=j, base=base, hbq=hbq, hbn=hbn):
                if base < 64:
                    MM(ps[3][base:base + 32, :], Cm[:, 0, q, 0:32], hbq[0][:], True, False, R=["Cm", hbn], W=[("psY",)])
                    MM(ps[3][base:base + 32, :], Cm[:, 1, q, 0:32], hbq[1][:], False, True, R=["Cm", hbn], W=[("psY",)])
                else:
                    MM(ps[3][64:128, :], Cm[:, 0, q, :], hbq[0][:], base == 64, False, R=["Cm", hbn], W=[("psY",)])
                    MM(ps[3][64:128, :], Cm[:, 1, q, :], hbq[1][:], False, base == 96, R=["Cm", hbn], W=[("psY",)])
                if q % 4 == 3:
                    TT(dve, ysf[:, j, :], ysf[:, j, :], ps[3][:, :], ALU.add, R=[("psY",), "ysf"], W=["ysf"])
            pending.append(cmm)
            if len(pending) > 1:
                pending.pop(0)()
            yield
        while pending:
            pending.pop(0)()
        yield

    def spill(dst, src_ap, R):
        K.dma(dst, src_ap, R=R, W=[("spill",)])

    def passA(l, seg, ti):
        r0 = ti * T
        src = xin[seg] if l == 0 else x1[seg]
        load_norm(src, seg, r0, l, False)
        if ti == 0:
            reset_states()
        def streamH():
            for hd in range(8):
                def ev(i, m, gi, p, pslot, hd=hd):
                    if i == 0:
                        A(act, sc[0][:, 0:T], p, AF.Copy, R=[pslot], W=["sc0"])
                        spill(sp_q[seg][hd, :, r0:r0 + T], sc[0][:, 0:T], R=["sc0"])
                    elif i == 1:
                        A(act, sc[1][:, 0:T], p, AF.Sigmoid, R=[pslot], W=["sc1"])
                    else:
                        A(act, bvf[:], p, AF.Copy, R=[pslot], W=["bvf"])
                        for s in range(4):
                            TR(ps5b[:, s * 128:(s + 1) * 128], bvf[:, s * 128:(s + 1) * 128], identb[:], R=["bvf", "identb"], W=[("ps", 5)])
                        K.op(dve, lambda e: e.tensor_copy(vT[:, hd, :, :], ps5b.rearrange("p (s k) -> p s k", k=128)), R=[("ps", 5)], W=["vT"])
                        spill(sp_v[seg][r0:r0 + T, hd * 128:(hd + 1) * 128].rearrange("(s p) v -> p s v", p=128), vT[:, hd, :, :], R=["vT"])
                linear(wb_in[l], 16, [C_Q + hd, C_FF + hd, C_I + hd], hT_k, [(HALO, T)], ev, ["hT"])
                yield
                yield from hg_head(hd, 0)
                A(act, sc[9][:, 0:T], ps[4][:, :], AF.Copy, R=[("ps", 4)], W=["sc9"])
                spill(sp_o[seg][hd, :, r0:r0 + T], sc[9][:, 0:T], R=["sc9"])
                yield

        def streamS():
            def evu(i, m, gi, p, pslot):
                A(act, ubf[:, i, :], p, AF.Copy, R=[pslot], W=["ubf"])
                TS(dve, ysf[:, i, :], p, vecT[:, V_SD + i:V_SD + i + 1], None, ALU.mult, None, R=[pslot, "vecT"], W=["ysf"])
            linear(wb_in[l], 16, [C_DIN + j for j in range(8)], hT_k, [(HALO, T)], evu, ["hT"])
            spill(sp_u[seg][:, :, r0:r0 + T].rearrange("j p t -> p j t"), ubf, R=["ubf"])
            yield
            yield from s5_dir(l, 0)
            spill(sp_y[seg][:, :, r0:r0 + T].rearrange("j p t -> p j t"), ysf, R=["ysf"])
        run_streams([streamS(), streamH()], [SW_S, SW_H])

    def rstd_from(ps_ms, out_sc, R, W):
        TS(dve, out_sc, ps_ms, EPS, None, ALU.add, None, R=R, W=W)
        A(act, out_sc, out_sc, AF.Ln, R=W, W=W)
        A(act, out_sc, out_sc, AF.Exp, R=W, W=W, scale=-0.5)

    def passB(l, seg, ti, ntiles, last_layer):
        r0 = ti * T
        src = xin[seg] if l == 0 else x1[seg]
        load_norm(src, seg, r0, l, True)
        if ti == ntiles - 1:
            reset_states()
        def streamD():
            K.dma(ubf, sp_u[seg][:, :, r0:r0 + T].rearrange("j p t -> p j t"), R=[("spill",)], W=["ubf"])
            K.dma(ysf, sp_y[seg][:, :, r0:r0 + T].rearrange("j p t -> p j t"), R=[("spill",)], W=["ysf"])
            yield
            yield from s5_dir(l, 1)
            for j in range(8):
                A(act, ysf[:, j, :], ysf[:, j, :], AF.Gelu, R=["ysf"], W=["ysf"])
                K.op(dve, lambda e: e.tensor_copy(zbf[:, j, :], ysf[:, j, :]), R=["ysf"], W=["zbf"])

            def evglu(i, m, gi, p, pslot):
                A(act, s5buf[0], p, AF.Sigmoid, R=[pslot, "vecT"], W=["arB"], bias=vecT[:, V_GB + i:V_GB + i + 1])
                TT(dve, ysf[:, i, :], ysf[:, i, :], s5buf[0], ALU.mult, R=["ysf", "arB"], W=["ysf"])
            linear(wb_glu[l], 8, list(range(8)), lambda kc: zbf[:, kc, :], [(0, T)], evglu, ["zbf"])

            def evdg(i, m, gi, p, pslot):
                A(act, s5buf[0], p, AF.Silu, R=[pslot], W=["arB"])
                TT(dve, yd[:, i, :], ysf[:, i, :], s5buf[0], ALU.mult, R=["ysf", "arB"], W=["yd"])
            linear(wb_in[l], 16, [C_DGATE + j for j in range(8)], hT_k, [(HALO, T)], evdg, ["hT"])
            yield

        def streamC():
            for hd in range(8):
                K.dma(sc[0][:, 0:T], sp_q[seg][hd, :, r0:r0 + T], R=[("spill",)], W=["sc0"])
                K.dma(vT[:, hd, :, :], sp_v[seg][r0:r0 + T, hd * 128:(hd + 1) * 128].rearrange("(s p) v -> p s v", p=128), R=[("spill",)], W=["vT"])
                K.dma(sc[9][:, 0:T], sp_o[seg][hd, :, r0:r0 + T], R=[("spill",)], W=["sc9"])

                def ev(i, m, gi, p, pslot):
                    A(act, sc[1][:, 0:T], p, AF.Sigmoid, R=[pslot], W=["sc1"])
                linear(wb_in[l], 16, [C_FB + hd], hT_k, [(HALO, T)], ev, ["hT"])
                yield
                yield from hg_head(hd, 1)
                TT(dve, sc[9][:, 0:T], ps[4][:, :], sc[9][:, 0:T], ALU.add, R=[("ps", 4), "sc9"], W=["sc9"])
                A(act, bq[:], sc[9][:, 0:T], AF.Square, R=["sc9"], W=["bq"])
                MM(ps[5][:, :], onesH[:], bq[:], True, True, R=["onesH", "bq"], W=[("ps", 5)])
                rstd_from(ps[5][:, :], sc[2][:, 0:T], R=[("ps", 5)], W=["sc2"])
                STT(dve, sc[3][:, 0:T], sc[9][:, 0:T], vecT[:, V_HG + hd:V_HG + hd + 1], sc[2][:, 0:T], ALU.mult, ALU.mult,
                    R=["sc9", "vecT", "sc2"], W=["sc3"])

                def evg(i, m, gi, p, pslot, hd=hd):
                    A(act, sc[5][:, 0:T], p, AF.Silu, R=[pslot], W=["sc5"])
                    TT(dve, yc[:, hd, :], sc[3][:, 0:T], sc[5][:, 0:T], ALU.mult, R=["sc3", "sc5"], W=["yc"])
                linear(wb_in[l], 16, [C_CGATE + hd], hT_k, [(HALO, T)], evg, ["hT"])
                yield
            yield

        run_streams([streamD(), streamC()], [SW_S, SW_H])
        CG = [(0, T), (T, 2 * HALO)]
        for c in range(8):
            def eva(i, m, gi, p, pslot, c=c):
                c0, ncol = CG[gi]
                if i == 0:
                    A(act, sc[0][:, c0:c0 + ncol], p, AF.Copy, R=[pslot], W=["sc0"])
                else:
                    A(act, sc[1][:, c0:c0 + ncol], p, AF.Sigmoid, R=[pslot], W=["sc1"])
                    TT(dve, ue[:, c, c0:c0 + ncol], sc[0][:, c0:c0 + ncol], sc[1][:, c0:c0 + ncol], ALU.mult, R=["sc0", "sc1"], W=["ue"])
            linear(wb_in[l], 16, [C_AVAL + c, C_AGLU + c], hT_k, CG, eva, ["hT"])
        for c in range(8):
            b = c % 2
            K.dma(cbuf[b][:], cdiag[l][c], R=[("cdiag",)], W=[("cbuf", b)])
            p, pslot = nps()
            for j in range(31):
                MM(p[:, :], cbuf[b][:, j, :], ue[:, c, j + 1:j + 1 + T], j == 0, j == 30, R=[("cbuf", b), "ue"], W=[pslot])
            A(act, ysf[:, c, :], p[:, :], AF.Identity, R=[pslot, "vecT"], W=["ysf"], bias=vecT[:, V_CB + c:V_CB + c + 1])
            K.op(dve, lambda e: e.tensor_copy(zbf[:, c, :], ysf[:, c, :]), R=["ysf"], W=["zbf"])
        for c in range(8):
            MM(ps[4][:, :], onesW[:], zbf[:, c, :], c == 0, c == 7, R=["onesW", "zbf"], W=[("ps", 4)])
        for c in range(8):
            A(act, bq[:], ysf[:, c, :], AF.Square, R=["ysf"], W=["bq"])
            MM(ps[5][:, :], onesW[:], bq[:], c == 0, c == 7, R=["onesW", "bq"], W=[("ps", 5)])
        A(act, sc[0][:, 0:T], ps[4][:, :], AF.Copy, R=[("ps", 4)], W=["sc0"])
        TT(dve, sc[1][:, 0:T], sc[0][:, 0:T], sc[0][:, 0:T], ALU.mult, R=["sc0"], W=["sc1"])
        TT(dve, sc[1][:, 0:T], ps[5][:, :], sc[1][:, 0:T], ALU.subtract, R=[("ps", 5), "sc1"], W=["sc1"])
        rstd_from(sc[1][:, 0:T], sc[2][:, 0:T], R=["sc1"], W=["sc2"])
        for c in range(8):
            TT(dve, sc[3][:, 0:T], ysf[:, c, :], sc[0][:, 0:T], ALU.subtract, R=["ysf", "sc0"], W=["sc3"])
            STT(dve, sc[3][:, 0:T], sc[3][:, 0:T], vecT[:, V_LNG + c:V_LNG + c + 1], sc[2][:, 0:T], ALU.mult, ALU.mult,
                R=["sc3", "vecT", "sc2"], W=["sc3"])
            A(act, sc[4][:, 0:T], sc[3][:, 0:T], AF.Silu, R=["sc3", "vecT"], W=["sc4"], bias=vecT[:, V_LNB + c:V_LNB + c + 1])

            def evag(i, m, gi, p, pslot, c=c):
                A(act, sc[5][:, 0:T], p, AF.Silu, R=[pslot], W=["sc5"])
                TT(dve, ya[:, c, :], sc[4][:, 0:T], sc[5][:, 0:T], ALU.mult, R=["sc4", "sc5"], W=["ya"])
            linear(wb_in[l], 16, [C_AGATE + c], hT_k, [(HALO, T)], evag, ["hT"])
        def evb(i, m, gi, p, pslot):
            c0, ncol = CG[gi]
            A(act, ue[:, i, c0:c0 + ncol], p, AF.Copy, R=[pslot], W=["ue"])
        linear(wb_in[l], 16, [C_BIN + c for c in range(8)], hT_k, CG, evb, ["hT"])
        for g in range(4):
            K.dma(rcb, rcin[seg][g, r0:r0 + T].partition_broadcast(128), W=["sc8"])
            for mm in range(2):
                c = 2 * g + mm

                def evz(i, m, gi, p, pslot):
                    c0, ncol = CG[gi]
                    A(act, sc[0][:, c0:c0 + ncol], p, AF.Copy, R=[pslot], W=["sc0"])
                linear(wb_pool[l][g], 2, [mm], lambda kc, g=g: ue[:, 2 * g + kc, :], CG, evz, ["ue"])
                z = sc[0]
                a, bb = sc[2], sc[3]
                TT(dve, a[:, 1:TE], z[:, 0:TE - 1], z[:, 1:TE], ALU.add, R=["sc0"], W=["sc2"])
                cur, lo, hi = a, 1, TE
                if g >= 1:
                    TT(dve, bb[:, lo + 1:hi - 1], cur[:, lo:hi - 2], cur[:, lo + 2:hi], ALU.add, R=["sc2"], W=["sc3"])
                    cur, lo, hi = bb, lo + 1, hi - 1
                if g >= 2:
                    TT(dve, a[:, lo + 2:hi - 2], cur[:, lo:hi - 4], cur[:, lo + 4:hi], ALU.add, R=["sc3"], W=["sc2"])
                    cur, lo, hi = a, lo + 2, hi - 2
                if g >= 3:
                    TT(dve, bb[:, lo + 4:hi - 4], cur[:, lo:hi - 8], cur[:, lo + 8:hi], ALU.add, R=["sc2"], W=["sc3"])
                    cur, lo, hi = bb, lo + 4, hi - 4
                cn_ = "sc2" if cur is a else "sc3"
                TT(dve, sc[4][:, 0:T], cur[:, HALO:HALO + T], rcb, ALU.mult, R=[cn_, "sc8"], W=["sc4"])
                TT(dve, sc[4][:, 0:T], sc[4][:, 0:T], z[:, HALO:HALO + T], ALU.subtract, R=["sc4", "sc0"], W=["sc4"])

                def evbg(i, m, gi, p, pslot, c=c):
                    A(act, sc[5][:, 0:T], p, AF.Silu, R=[pslot], W=["sc5"])
                    STT(dve, yb[:, c, :], sc[4][:, 0:T], vecT[:, V_PS + c:V_PS + c + 1], sc[5][:, 0:T], ALU.mult, ALU.mult,
                        R=["sc4", "vecT", "sc5"], W=["yb"])
                linear(wb_in[l], 16, [C_BGATE + c], hT_k, [(HALO, T)], evbg, ["hT"])
        ys = [ya, yb, yc, yd]
        yn = ["ya", "yb", "yc", "yd"]
        for dch in range(16):
            for i in range(4):
                def evr(ii, m, gi, p, pslot, i=i):
                    A(act, sc[5][:, 0:T], p, AF.Sigmoid, R=[pslot], W=["sc5"])
                linear(wb_in[l], 16, [C_R + i * 16 + dch], hT_k, [(HALO, T)], evr, ["hT"])

                def evo(ii, m, gi, p, pslot, i=i):
                    if i == 0:
                        TT(dve, sc[6][:, 0:T], p, sc[5][:, 0:T], ALU.mult, R=[pslot, "sc5"], W=["sc6"])
                    else:
                        TT(dve, sc[7][:, 0:T], p, sc[5][:, 0:T], ALU.mult, R=[pslot, "sc5"], W=["sc7"])
                        TT(dve, sc[6][:, 0:T], sc[6][:, 0:T], sc[7][:, 0:T], ALU.add, R=["sc6", "sc7"], W=["sc6"])
                linear(wb_ao[l][i], 8, [dch], lambda kc, i=i: ys[i][:, kc, :], [(0, T)], evo, [yn[i]])
            K.op(dve, lambda e: e.tensor_copy(mT[:, dch, :], sc[6][:, 0:T]), R=["sc6"], W=["mT"])
        def evout(i, m, gi, p, pslot):
            A(act, sc[0][:, 0:T], p, AF.Copy, R=[pslot], W=["sc0"])
            for s in range(4):
                TR(psx[:, 0:128], sc[0][:, s * 128:(s + 1) * 128], ident, R=["sc0", "cs"], W=["psx"])
                TT(dve, xt[:, s, i * 128:(i + 1) * 128], xt[:, s, i * 128:(i + 1) * 128], psx[:, 0:128], ALU.add, R=["psx", "xt"], W=["xt"])
        K.dma(xt, src[HALO + r0:HALO + r0 + T, :].rearrange("(s p) d -> p s d", p=128), W=["xt"])
        linear(wb_out[l], 16, list(range(16)), lambda kc: mT[:, kc, :], [(0, T)], evout, ["mT"])
        if not last_layer:
            K.dma(x1[seg][HALO + r0:HALO + r0 + T, :].rearrange("(s p) d -> p s d", p=128), xt, R=["xt"], W=[("x1", seg)])
        else:
            for s in range(4):
                A(act, junk[:, :], xt[:, s, :], AF.Square, R=["xt"], W=["junk", "ss"], accum=ss[:, s:s + 1])
            TS(dve, rstd[:, 0:4], ss[:, 0:4], 1.0 / D, EPS, ALU.mult, ALU.add, R=["ss"], W=["rstd"])
            A(act, rstd[:, 0:4], rstd[:, 0:4], AF.Ln, R=["rstd"], W=["rstd"])
            A(act, rstd[:, 0:4], rstd[:, 0:4], AF.Exp, R=["rstd"], W=["rstd"], scale=-0.5)
            K.dma(fgb, final_g.partition_broadcast(128), W=["fgb"])
            for s in range(4):
                STT(dve, xt[:, s, :], xt[:, s, :], rstd[:, s:s + 1], fgb, ALU.mult, ALU.mult, R=["xt", "rstd", "fgb"], W=["xt"])
            K.dma(yout[seg][r0:r0 + T, :].rearrange("(s p) d -> p s d", p=128), xt, R=["xt"], W=[("yout", seg)])

    for l in range(depth):
        prep_layer(l)
        prep_s5(l, 0)
        for seg in range(nseg):
            nt = segs[seg] // T
            for ti in range(nt):
                passA(l, seg, ti)
        prep_s5(l, 1)
        for seg in range(nseg):
            nt = segs[seg] // T
            for ti in range(nt - 1, -1, -1):
                passB(l, seg, ti, nt, l == depth - 1)
    K.finish()
    return nc


def _consts():
    c = np.zeros((128, 128 + 256 + 256 + 512), np.float32)
    c[:, 0:128] = np.eye(128, dtype=np.float32)
    s = np.arange(128) % 64
    t = np.arange(256) % 64
    c[:, 128:384] = (s[:, None] <= t[None, :]).astype(np.float32)
    c[:, 384:640] = (s[:, None] >= t[None, :]).astype(np.float32)
    rm = np.ones(512, np.float32)
    rm[::64] = 0.0
    c[:, 640:1152] = rm[None, :]
    return c


def _rc(L):
    t = np.arange(L)
    out = np.zeros((4, L), np.float32)
    for g, win in enumerate((2, 4, 8, 16)):
        left = win // 2
        right = win - 1 - left
        lo = np.maximum(t - left, 0)
        hi = np.minimum(t + right, L - 1) + 1
        out[g] = 1.0 / (hi - lo).astype(np.float32)
    return out


_WNAMES = ["norm_g", "w_in", "conv_w", "conv_b", "conv_ln_g", "conv_ln_b", "w_a_out", "pool_w", "pool_scale", "w_b_out",
           "hg_lb", "hg_norm_g", "w_c_out", "s5_a_re", "s5_a_im", "s5_log_dt", "s5_b_re", "s5_b_im", "s5_c_re", "s5_c_im",
           "s5_d", "s5_glu_w", "s5_glu_b", "w_d_out", "w_out", "final_g"]


def run(x_prompt, x_sample, weights, ncores=8):
    Lp, Ls = x_prompt.shape[1], x_sample.shape[1]
    depth = weights["w_in"].shape[0]
    nc = build([Lp, Ls], depth)
    cst = _consts()
    base = {k: np.ascontiguousarray(np.asarray(weights[k], dtype=np.float32)) for k in _WNAMES}
    base["cst"] = cst
    base["rc0"] = _rc(Lp)
    base["rc1"] = _rc(Ls)
    in_maps = []
    nb_p, nb_s = x_prompt.shape[0], x_sample.shape[0]
    assign = []
    for c in range(ncores):
        pi = c % nb_p
        si = c % nb_s
        assign.append((pi, si))
        m = dict(base)
        xp = np.zeros((Lp + 2 * HALO, D), np.float32)
        xp[HALO:HALO + Lp] = x_prompt[pi]
        xs = np.zeros((Ls + 2 * HALO, D), np.float32)
        xs[HALO:HALO + Ls] = x_sample[si]
        m["x0"] = xp
        m["x1"] = xs
        in_maps.append(m)
    res = run_bass_kernel_spmd(nc, in_maps, core_ids=list(range(ncores)))
    yp = np.zeros(x_prompt.shape, np.float32)
    ysm = np.zeros(x_sample.shape, np.float32)
    for c in range(ncores):
        pi, si = assign[c]
        if c < nb_p or nb_p >= ncores:
            yp[pi] = res.results[c]["y0"]
        ysm[si] = res.results[c]["y1"]
    return yp, ysm


def kernel(x_prompt, x_sample, **weights):
    x_prompt = np.asarray(x_prompt, dtype=np.float32)
    x_sample = np.asarray(x_sample, dtype=np.float32)
    return run(x_prompt, x_sample, weights)
```

```python
import numpy as np
import concourse.bass as bass
import concourse.mybir as mybir
from concourse.bass_utils import run_bass_kernel_spmd

F32 = mybir.dt.float32
BF16 = mybir.dt.bfloat16
AF = mybir.ActivationFunctionType
ALU = mybir.AluOpType

D = 2048
WB = 1024
NIN = 20480
T = 512
HALO = 16
TE = T + 2 * HALO
EPS = 1e-6
CH = 64
PAD = 256
POOL_MOD = 1000
SW_S = 2
SW_H = 1
C_AVAL, C_AGLU, C_AGATE, C_BIN, C_BGATE, C_Q, C_FF, C_FB, C_I, C_CGATE, C_DIN, C_DGATE, C_R = (
    0, 8, 16, 24, 32, 40, 48, 56, 64, 72, 80, 88, 96)
V_NG, V_CB, V_LNG, V_LNB, V_PS, V_HG, V_SD, V_GB, V_LB0, V_LB1 = 0, 16, 24, 32, 40, 48, 56, 64, 72, 88
NVEC = 104


class Slot:
    __slots__ = ("w", "r")

    def __init__(self):
        self.w = {}
        self.r = {}


class EngW:
    def __init__(self, nc, eng, name, selfwait):
        self.eng = eng
        self.sem = nc.alloc_semaphore(name)
        self.n = 0
        self.seen = {}
        self.selfwait = selfwait


class Ker:
    def __init__(self, nc):
        self.nc = nc
        self.pe = EngW(nc, nc.tensor, "s_pe", False)
        self.act = EngW(nc, nc.scalar, "s_act", True)
        self.dve = EngW(nc, nc.vector, "s_dve", True)
        self.pool = EngW(nc, nc.gpsimd, "s_pool", True)
        self.sp = EngW(nc, nc.sync, "s_sp", False)
        self.ndsem = 24
        self.dsem = [nc.alloc_semaphore(f"s_dma{i}") for i in range(self.ndsem)]
        self.dval = [0] * self.ndsem
        self.dnext = 0
        self.slots = {}
        self.canon = {"xt": "arA", "ysf": "arA", "ubf": "arA", "zbf": "arA", ("cvt", 0): "arA", ("cvt", 1): "arA",
                      ("cvb", 0): "arA", ("cvb", 1): "arA",
                      "arB": "arB", ("ks", 1): "arB", "ue": "arB", ("cbuf", 0): "arB", ("cbuf", 1): "arB",
                      "hb": "arC", "mT": "arC", "hT": "arD", "fgb": "arD", "junk": "arD", "xh": "arB", ("psb", 0): "psb", ("psb", 1): "psb", "ya": "arY", "yb": "arY", ("kp", 0): "arY", ("kp", 1): "arY"}

    def slot(self, key):
        key = self.canon.get(key, key)
        s = self.slots.get(key)
        if s is None:
            s = Slot()
            self.slots[key] = s
        return s

    def _wait(self, E, R, W):
        deps = {}
        for k in R:
            for i, ev in self.slot(k).w.items():
                if deps.get(i, (None, 0))[1] < ev[1]:
                    deps[i] = ev
        for k in W:
            s = self.slot(k)
            for dd in (s.w, s.r):
                for i, ev in dd.items():
                    if deps.get(i, (None, 0))[1] < ev[1]:
                        deps[i] = ev
        for i, (sem, val) in deps.items():
            if sem is E.sem and not E.selfwait:
                continue
            if E.seen.get(i, 0) >= val:
                continue
            E.eng.wait_ge(sem, val)
            E.seen[i] = val

    def _update(self, ev, R, W):
        i = id(ev[0])
        for k in R:
            self.slot(k).r[i] = ev
        for k in W:
            s = self.slot(k)
            if s.r:
                s.r = {}
                s.w = {}
            s.w[i] = ev

    def op(self, E, fn, R=(), W=()):
        self._wait(E, R, W)
        ins = fn(E.eng)
        E.n += 1
        ins.then_inc(E.sem, 1)
        self._update((E.sem, E.n), R, W)

    def dma(self, out, in_, R=(), W=()):
        E = self.sp
        self._wait(E, R, W)
        k = self.dnext
        self.dnext = (k + 1) % self.ndsem
        sem = self.dsem[k]
        if self.dval[k] > 0 and E.seen.get(id(sem), 0) < self.dval[k]:
            E.eng.wait_ge(sem, self.dval[k])
            E.seen[id(sem)] = self.dval[k]
        E.eng.dma_start(out=out, in_=in_).then_inc(sem, 16)
        self.dval[k] += 16
        self._update((sem, self.dval[k]), R, W)

    def finish(self):
        E = self.sp
        for k in range(self.ndsem):
            if self.dval[k] > 0:
                E.eng.wait_ge(self.dsem[k], self.dval[k])
        for X in (self.pe, self.act, self.dve, self.pool):
            if X.n > 0:
                E.eng.wait_ge(X.sem, X.n)


def build(segs, depth=2, segt=None):
    nc = bass.Bass("TRN2", target_bir_lowering=False)
    K = Ker(nc)
    pe, act, dve, pool = K.pe, K.act, K.dve, K.pool
    nseg = len(segs)

    def din(name, shape, dt=F32):
        return nc.dram_tensor(name, list(shape), dt, kind="ExternalInput").ap()

    def dscr(name, shape, dt=F32):
        return nc.dram_tensor(name, list(shape), dt).ap()

    xin = [din(f"x{s}", [segs[s] + 2 * HALO, D]) for s in range(nseg)]
    yout = [nc.dram_tensor(f"y{s}", [segs[s], D], F32, kind="ExternalOutput").ap() for s in range(nseg)]
    x1 = [dscr(f"x1_{s}", [segs[s] + 2 * HALO, D]) for s in range(nseg)]
    rcin = [din(f"rc{s}", [4, segs[s]]) for s in range(nseg)]
    cst = din("cst", [128, 128 + 256 + 256 + 512])
    cfl_d = din("cfl", [128, 1])
    hmk_d = din("hmk", [32, 3])
    norm_g = din("norm_g", [depth, D]); w_in = din("w_in", [depth, D, NIN])
    conv_w = din("conv_w", [depth, 31, WB]); conv_b = din("conv_b", [depth, WB])
    conv_ln_g = din("conv_ln_g", [depth, WB]); conv_ln_b = din("conv_ln_b", [depth, WB])
    w_a_out = din("w_a_out", [depth, WB, D]); pool_w = din("pool_w", [depth, 4, 256, 256])
    pool_scale = din("pool_scale", [depth, WB]); w_b_out = din("w_b_out", [depth, WB, D])
    hg_lb = din("hg_lb", [depth, 2, WB]); hg_norm_g = din("hg_norm_g", [depth, WB])
    w_c_out = din("w_c_out", [depth, WB, D])
    s5_a_re = din("s5_a_re", [depth, 2, 64, 64]); s5_a_im = din("s5_a_im", [depth, 2, 64, 64])
    s5_log_dt = din("s5_log_dt", [depth, 2, 64])
    s5_b_re = din("s5_b_re", [depth, 2, 64, 64, 16]); s5_b_im = din("s5_b_im", [depth, 2, 64, 64, 16])
    s5_c_re = din("s5_c_re", [depth, 2, 64, 16, 64]); s5_c_im = din("s5_c_im", [depth, 2, 64, 16, 64])
    s5_d = din("s5_d", [depth, WB]); s5_glu_w = din("s5_glu_w", [depth, WB, WB])
    s5_glu_b = din("s5_glu_b", [depth, WB]); w_d_out = din("w_d_out", [depth, WB, D])
    w_out = din("w_out", [depth, D, D]); final_g = din("final_g", [D])

    def wscr(name, Kdim, N):
        return dscr(name, [N // 128, 128, Kdim // 128, 128], BF16)
    wb_in = [wscr(f"wb_in{l}", D, NIN) for l in range(depth)]
    wb_ao = [[wscr(f"wb_o{i}_{l}", WB, D) for i in range(4)] for l in range(depth)]
    wb_out = [wscr(f"wb_out{l}", D, D) for l in range(depth)]
    wb_glu = [wscr(f"wb_glu{l}", WB, WB) for l in range(depth)]
    wb_pool = [[wscr(f"wb_pool{l}_{g}", 256, 256) for g in range(4)] for l in range(depth)]
    cdiag = [dscr(f"cdiag{l}", [8, 128, 31, 128], BF16) for l in range(depth)]
    tabd = [[dscr(f"tabd{l}_{d}", [32, 128, 2, T]) for d in range(2)] for l in range(depth)]
    sp_q = [dscr(f"sp_q{s}", [8, 128, segs[s]]) for s in range(nseg)]
    sp_o = [dscr(f"sp_o{s}", [8, 128, segs[s]]) for s in range(nseg)]
    sp_y = [dscr(f"sp_y{s}", [8, 128, segs[s]]) for s in range(nseg)]
    sp_u = [dscr(f"sp_u{s}", [8, 128, segs[s]], BF16) for s in range(nseg)]
    sp_v = [dscr(f"sp_v{s}", [segs[s], WB], BF16) for s in range(nseg)]

    def sb(name, shape, dt=F32):
        return nc.alloc_sbuf_tensor(name, list(shape), dt)
    cs = sb("cs", [128, 128 + 256 + 256 + 512])
    ident = cs[:, 0:128]
    maskF = cs[:, 128:384]
    maskB = cs[:, 384:640]
    rmask = cs[:, 640:1152]
    identb = sb("identb", [128, 128], BF16)
    onesW = sb("onesW", [128, 128], BF16)
    onesH = sb("onesH", [128, 128], BF16)
    vecT = sb("vecT", [128, NVEC])
    lbv = sb("lbv", [128, 16]); omlb = sb("omlb", [128, 16])
    arA = sb("arA", [128, 8192])
    arB = sb("arB", [128, 4352])
    arC = sb("arC", [128, 5120])
    arD = sb("arD", [128, 4352])
    xt = arA[:, :].rearrange("p (s d) -> p s d", d=D)
    ysf = arA[:, 0:4096].rearrange("p (j t) -> p j t", t=T)
    ubf = arA[:, 4096:6144].bitcast(BF16).rearrange("p (j t) -> p j t", t=T)
    zbf = arA[:, 6144:8192].bitcast(BF16).rearrange("p (j t) -> p j t", t=T)
    cvt = [arA[:, i * 2048:(i + 1) * 2048].rearrange("p (k n) -> p k n", n=128) for i in range(2)]
    cvb = [arA[:, 4096 + i * 1024:4096 + (i + 1) * 1024].bitcast(BF16).rearrange("p (k n) -> p k n", n=128) for i in range(2)]
    s5buf = [arB[:, i * 512:(i + 1) * 512] for i in range(4)]
    tbuf = [arB[:, 2048 + i * 1024:2048 + (i + 1) * 1024].rearrange("p (a t) -> p a t", t=T) for i in range(2)]
    ks = [[s5buf[0]]]
    ue = arB[:, 0:2176].bitcast(BF16).rearrange("p (c t) -> p c t", t=TE)
    cb0 = arB[:, 2176:2176 + 1984].bitcast(BF16).rearrange("p (j n) -> p j n", n=128)
    cbuf = [cb0, cb0]
    hb = arC[:, :].bitcast(BF16).rearrange("p (s d) -> p s d", d=D)
    mT = arC[:, 0:4096].bitcast(BF16).rearrange("p (c t) -> p c t", t=T)
    hT = arD[:, :].bitcast(BF16).rearrange("p (c t) -> p c t", t=TE)
    fgb = arD[:, 0:D]
    xh = arB[0:32, 0:D]
    NWB = 3
    wbuf = [sb(f"wbuf{i}", [128, 16, 128], BF16) for i in range(NWB)]
    ss = sb("ss", [128, 8]); rstd = sb("rstd", [128, 8])
    junk = arD[:, 2048:3072].bitcast(BF16)
    NSC = 10
    sc = [sb(f"sc{i}", [128, TE]) for i in range(NSC)]
    bq = sb("bq", [128, T], BF16); bk = sb("bk", [128, T], BF16); bvf = sb("bvf", [128, T], BF16)
    vT = sb("vT", [128, 8, 4, 128], BF16)
    kT = sb("kT", [128, 4, 128], BF16)
    At = sb("At", [128, 4, CH], BF16)
    S32 = sb("S32", [128, 8, 128]); Sbf = sb("Sbf", [128, 8, 128], BF16); Tm = sb("Tm", [128, 128])
    Hb = [[sb(f"Hb{a}{b}", [128, T], BF16) for b in range(2)] for a in range(2)]
    s5p = sb("s5p", [128, 32, 32])
    hc = sb("hc", [128, 2, 32]); cin = sb("cin", [128, 2, 32]); tmp32 = sb("tmp32", [128, 4, 32])
    Bm = sb("Bm", [128, 2, 16, 128], BF16)
    Cm = sb("Cm", [128, 2, 32, 64], BF16)
    stg = sb("stg", [128, 128])
    stgA = sb("stgA", [128, 128]); stgB = sb("stgB", [128, 128])
    stg2 = sb("stg2", [128, 256])
    cfl = sb("cflag", [128, 1]); hmk = sb("hmask", [32, 3])
    tmpi = sb("tmpi", [128, 32], mybir.dt.int32)
    arY = sb("arY", [128, 4096])
    ya = arY[:, 0:2048].bitcast(BF16).rearrange("p (c t) -> p c t", t=T)
    yb = arY[:, 2048:4096].bitcast(BF16).rearrange("p (c t) -> p c t", t=T)
    ks2 = [[arY[:, (2 * a + b) * 1024:(2 * a + b + 1) * 1024] for b in range(2)] for a in range(2)]
    yc = sb("yc", [128, 8, T], BF16); yd = sb("yd", [128, 8, T], BF16)
    rcb = sc[8][:, 0:T]
    print("sbuf remaining", nc.sbuf_bytes_remaining)
    ps = [nc.alloc_psum_tensor(f"ps{i}", [128, T], F32) for i in range(6)]
    psb = nc.alloc_psum_tensor("psb", [128, 2 * T], BF16)
    psx = nc.alloc_psum_tensor("psx", [128, T], F32)
    ps5b = ps[5][:, 256:512].bitcast(BF16)
    pcnt = [0]

    def nps():
        i = pcnt[0] % 2
        pcnt[0] += 1
        return ps[i], ("ps", i)

    def A(E, out, in_, func, R, W, bias=None, scale=None, accum=None):
        kw = {}
        if bias is not None:
            kw["bias"] = bias
        if scale is not None:
            kw["scale"] = scale
        if accum is not None:
            kw["accum_out"] = accum
        K.op(act, lambda e: e.activation(out=out, in_=in_, func=func, **kw), R, W)

    def TS(E, out, in0, s1, s2, op0, op1, R, W):
        if s2 is None:
            K.op(E, lambda e: e.tensor_scalar(out, in0, s1, None, op0), R, W)
        else:
            K.op(E, lambda e: e.tensor_scalar(out, in0, s1, s2, op0, op1), R, W)

    def TT(E, out, in0, in1, op, R, W):
        K.op(E, lambda e: e.tensor_tensor(out, in0, in1, op), R, W)

    def STT(E, out, in0, scalar, in1, op0, op1, R, W):
        K.op(E, lambda e: e.scalar_tensor_tensor(out, in0, scalar, in1, op0, op1), R, W)

    def MM(out, lhsT, rhs, start, stop, R, W):
        K.op(pe, lambda e: e.matmul(out, lhsT, rhs, start=start, stop=stop), R, W)

    def TR(out, in_, idn, R, W):
        K.op(pe, lambda e: e.transpose(out, in_, idn), R, W)

    wcnt = [0]

    def linear(wb, nk, mlist, rhs_fn, ncols_list, evac, Rin):
        n = len(mlist)
        loaded = {}

        def load(i):
            b = wcnt[0] % NWB
            wcnt[0] += 1
            K.dma(wbuf[b][:, 0:nk, :], wb[mlist[i]], R=[], W=[("wbuf", b)])
            loaded[i] = b
        for i in range(min(NWB - 1, n)):
            load(i)
        for i in range(n):
            if i + NWB - 1 < n:
                load(i + NWB - 1)
            b = loaded.pop(i)
            for gi, (c0, ncol) in enumerate(ncols_list):
                p, pslot = nps()
                for kc in range(nk):
                    MM(p[:, 0:ncol], wbuf[b][:, kc, :], rhs_fn(kc)[:, c0:c0 + ncol], kc == 0, kc == nk - 1,
                       R=[("wbuf", b)] + Rin, W=[pslot])
                evac(i, mlist[i], gi, p[:, 0:ncol], pslot)

    ccnt = [0]

    def convert(src, Kdim, N, dst):
        nk = Kdim // 128
        srcv = src.rearrange("(kc kp) n -> kp kc n", kp=128)
        dstv = dst.rearrange("m kp kc j -> kp m kc j")
        for n0 in range(0, N, 128):
            i = ccnt[0] % 2
            ccnt[0] += 1
            K.dma(cvt[i][:, 0:nk, :], srcv[:, :, n0:n0 + 128], R=[], W=[("cvt", i)])
            eng = act if (ccnt[0] % 2 == 0) else dve
            if eng is act:
                A(act, cvb[i][:, 0:nk, :], cvt[i][:, 0:nk, :], AF.Copy, R=[("cvt", i)], W=[("cvb", i)])
            else:
                K.op(dve, lambda e: e.tensor_copy(cvb[i][:, 0:nk, :], cvt[i][:, 0:nk, :]), R=[("cvt", i)], W=[("cvb", i)])
            K.dma(dstv[:, n0 // 128, :, :], cvb[i][:, 0:nk, :], R=[("cvb", i)], W=[("wdram",)])

    K.dma(cs[:], cst[:, :], W=["cs"])
    K.dma(cfl[:], cfl_d[:, :], W=["cfl"])
    K.dma(hmk[:], hmk_d[:, :], W=["hmk"])
    K.op(dve, lambda e: e.tensor_copy(identb[:], ident), R=["cs"], W=["identb"])
    K.op(dve, lambda e: e.memset(ss[:], 1.0), W=["ss"])
    K.op(dve, lambda e: e.memset(onesW[:], 1.0 / 1024.0), W=["onesW"])
    K.op(dve, lambda e: e.memset(onesH[:], 1.0 / 128.0), W=["onesH"])
    for s in range(nseg):
        for off in (0, HALO + segs[s]):
            K.op(dve, lambda e: e.memset(xh[0:HALO, :], 0.0), W=["xh"])
            K.dma(x1[s][off:off + HALO, :], xh[0:HALO, :], R=["xh"], W=[("x1", s)])

    for l in range(depth):
        convert(w_in[l], D, NIN, wb_in[l])
        for i, wsrc in enumerate((w_a_out, w_b_out, w_c_out, w_d_out)):
            convert(wsrc[l], WB, D, wb_ao[l][i])
        convert(w_out[l], D, D, wb_out[l])
        convert(s5_glu_w[l], WB, WB, wb_glu[l])
        for g in range(4):
            convert(pool_w[l, g], 256, 256, wb_pool[l][g])

    def prep_layer(l):
        rows = [(norm_g[l], 16), (conv_b[l], 8), (conv_ln_g[l], 8), (conv_ln_b[l], 8), (pool_scale[l], 8),
                (hg_norm_g[l], 8), (s5_d[l], 8), (s5_glu_b[l], 8), (hg_lb[0, 0], 8), (hg_lb[0, 1], 8),
                (hg_lb[1 if depth > 1 else 0, 0], 8), (hg_lb[1 if depth > 1 else 0, 1], 8)]
        r = 0
        K.op(dve, lambda e: e.memset(stg[:], 0.0), W=["stg"])
        for v, n in rows:
            K.dma(stg[r:r + n, :], v.rearrange("(c p) -> c p", p=128), W=["stg"])
            r += n
        TR(psx[:, 0:128], stg[:, :], ident, R=["stg", "cs"], W=["psx"])
        K.op(dve, lambda e: e.tensor_copy(vecT[:], psx[:, 0:NVEC]), R=["psx"], W=["vecT"])
        if l == 0:
            K.op(dve, lambda e: e.memset(lbv[:], 0.0), W=["lbv"])
        else:
            TT(dve, lbv[:], vecT[:, V_LB1:V_LB1 + 16], vecT[:, V_LB0:V_LB0 + 16], ALU.subtract, R=["vecT"], W=["lbv"])
            A(act, lbv[:], lbv[:], AF.Sigmoid, R=["lbv"], W=["lbv"])
        TS(dve, omlb[:], lbv[:], -1.0, 1.0, ALU.mult, ALU.add, R=["lbv"], W=["omlb"])
        for half in range(2):
            nr = 128 if half == 0 else 31 * 8 - 128
            K.op(dve, lambda e: e.memset(stg[:], 0.0), W=["stg"])
            src = conv_w[l].rearrange("j (c p) -> (j c) p", p=128)
            K.dma(stg[0:nr, :], src[half * 128:half * 128 + nr, :], W=["stg"])
            TR(psx[:, 0:128], stg[:, :], ident, R=["stg", "cs"], W=["psx"])
            K.op(dve, lambda e: e.tensor_copy(stg2[:, half * 128:half * 128 + 128], psx[:, 0:128]), R=["psx"], W=["stg2"])
        for c in range(8):
            b = c % 2
            for j in range(31):
                col = j * 8 + c
                TS(dve, cbuf[b][:, j, :], ident, stg2[:, col:col + 1], None, ALU.mult, None, R=["stg2", "cs"], W=[("cbuf", b)])
            K.dma(cdiag[l][c], cbuf[b][:], R=[("cbuf", b)], W=[("cdiag",)])

    def prep_s5(l, d):
        def ldT(src, dstcol):
            K.op(dve, lambda e: e.memset(stg[:], 0.0), W=["stg"])
            K.dma(stg[0:32, :], src.rearrange("(q two) p -> q (two p)", two=2), W=["stg"])
            TR(psx[:, 0:128], stg[:, :], ident, R=["stg", "cs"], W=["psx"])
            K.op(dve, lambda e: e.tensor_copy(tmp32[:, dstcol, :], psx[:, 0:32]), R=["psx"], W=["tmp32"])
        ldT(s5_a_re[l, d], 0)
        ldT(s5_a_im[l, d], 1)
        K.op(dve, lambda e: e.memset(stg[:], 0.0), W=["stg"])
        K.dma(stg2[0:32, 0:2], s5_log_dt[l, d].rearrange("(q two) -> q two", two=2), W=["stg2"])
        for two in range(2):
            TS(dve, stg[0:32, two * 64:(two + 1) * 64], stg[0:32, two * 64:(two + 1) * 64], stg2[0:32, two:two + 1], None,
               ALU.add, None, R=["stg", "stg2"], W=["stg"])
        TR(psx[:, 0:128], stg[:, :], ident, R=["stg", "cs"], W=["psx"])
        A(act, tmp32[:, 2, :], psx[:, 0:32], AF.Exp, R=["psx"], W=["tmp32"])
        are, aim, dt = tmp32[:, 0, :], tmp32[:, 1, :], tmp32[:, 2, :]
        t3 = tmp32[:, 3, :]
        R_, W_ = ["tmp32", "s5p"], ["tmp32", "s5p"]
        mag, ang, sn, cn = s5p[:, :, 27], s5p[:, :, 28], s5p[:, :, 29], s5p[:, :, 30]
        TT(dve, mag, dt, are, ALU.mult, R_, W_)
        A(act, mag, mag, AF.Exp, R_, W_)
        TT(dve, ang, dt, aim, ALU.mult, R_, W_)
        TWO_PI = 2.0 * np.pi

        def sin_of(dst, phase):
            TS(dve, dst, ang, 1.0 / TWO_PI, phase / TWO_PI, ALU.mult, ALU.add, R_, W_)
            K.op(dve, lambda e: e.tensor_copy(tmpi[:], dst), R_ + ["tmpi"], W_ + ["tmpi"])
            K.op(dve, lambda e: e.tensor_copy(t3, tmpi[:]), R_ + ["tmpi"], W_)
            TT(dve, dst, dst, t3, ALU.subtract, R_, W_)
            TS(dve, t3, dst, 0.5, None, ALU.is_gt, None, R_, W_)
            TT(dve, dst, dst, t3, ALU.subtract, R_, W_)
            TS(dve, t3, dst, -0.5, None, ALU.is_lt, None, R_, W_)
            TT(dve, dst, dst, t3, ALU.add, R_, W_)
            A(act, dst, dst, AF.Sin, R_, W_, scale=TWO_PI)
        sin_of(sn, 0.0)
        sin_of(cn, np.pi / 2)
        lre, lim = s5p[:, :, 1], s5p[:, :, 2]
        TT(dve, lre, mag, cn, ALU.mult, R_, W_)
        TT(dve, lim, mag, sn, ALU.mult, R_, W_)
        K.op(dve, lambda e: e.tensor_copy(s5p[:, :, 0], mag), R_, W_)
        K.op(dve, lambda e: e.tensor_copy(s5p[:, :, 8], cn), R_, W_)
        K.op(dve, lambda e: e.tensor_copy(s5p[:, :, 9], sn), R_, W_)
        den, xm, gre, gim = s5p[:, :, 27], s5p[:, :, 28], s5p[:, :, 29], s5p[:, :, 30]
        TT(dve, den, are, are, ALU.mult, R_, W_)
        TT(dve, t3, aim, aim, ALU.mult, R_, W_)
        TT(dve, den, den, t3, ALU.add, R_, W_)
        K.op(dve, lambda e: e.reciprocal(den, den), R_, W_)
        TS(dve, xm, lre, -1.0, None, ALU.add, None, R_, W_)
        TT(dve, gre, xm, are, ALU.mult, R_, W_)
        TT(dve, t3, lim, aim, ALU.mult, R_, W_)
        TT(dve, gre, gre, t3, ALU.add, R_, W_)
        TT(dve, gre, gre, den, ALU.mult, R_, W_)
        TT(dve, gim, lim, are, ALU.mult, R_, W_)
        TT(dve, t3, xm, aim, ALU.mult, R_, W_)
        TT(dve, gim, gim, t3, ALU.subtract, R_, W_)
        TT(dve, gim, gim, den, ALU.mult, R_, W_)
        for k in range(1, 10):
            pr, pi_, qr, qi = s5p[:, :, 6 + 2 * k], s5p[:, :, 7 + 2 * k], s5p[:, :, 8 + 2 * k], s5p[:, :, 9 + 2 * k]
            TT(dve, qr, pr, pr, ALU.mult, R_, W_)
            TT(dve, t3, pi_, pi_, ALU.mult, R_, W_)
            TT(dve, qr, qr, t3, ALU.subtract, R_, W_)
            TT(dve, qi, pr, pi_, ALU.mult, R_, W_)
            TS(dve, qi, qi, 2.0, None, ALU.mult, None, R_, W_)
        TS(dve, s5p[:, :, 3], s5p[:, :, 9], -1.0, None, ALU.mult, None, R_, W_)
        for q in range(32):
            tb = tbuf[q % 2]
            tn = ("tb", q % 2)
            cc, ssn = tb[:, 0, :], tb[:, 1, :]
            K.op(dve, lambda e: e.memset(cc[:, 0:1], 1.0), W=[tn, "arB"])
            K.op(dve, lambda e: e.memset(ssn[:, 0:1], 0.0), W=[tn])
            for k in range(9):
                n = 1 << k
                ur, ui = s5p[:, q, 8 + 2 * k:9 + 2 * k], s5p[:, q, 9 + 2 * k:10 + 2 * k]
                TS(dve, cc[:, n:2 * n], cc[:, 0:n], ur, None, ALU.mult, None, R=[tn, "s5p"], W=[tn])
                TS(dve, ssn[:, n:2 * n], cc[:, 0:n], ui, None, ALU.mult, None, R=[tn, "s5p"], W=[tn])
                TS(dve, sc[0][:, 0:n], ssn[:, 0:n], ui, None, ALU.mult, None, R=[tn, "s5p"], W=["sc0"])
                TT(dve, cc[:, n:2 * n], cc[:, n:2 * n], sc[0][:, 0:n], ALU.subtract, R=[tn, "sc0"], W=[tn])
                STT(dve, ssn[:, n:2 * n], ssn[:, 0:n], ur, ssn[:, n:2 * n], ALU.mult, ALU.add, R=[tn, "s5p"], W=[tn])
            K.dma(tabd[l][d][q], tb, R=[tn, "arB"], W=[("tabd",)])
        K.op(dve, lambda e: e.memset(Bm[:], 0.0), W=["Bm"])
        K.op(dve, lambda e: e.memset(Cm[:], 0.0), W=["Cm"])
        for j in range(8):
            K.op(dve, lambda e: e.memset(stgA[:], 0.0), W=["stgA"])
            K.op(dve, lambda e: e.memset(stgB[:], 0.0), W=["stgB"])
            for qq in range(4):
                q = 4 * j + qq
                base = 32 * qq
                K.dma(stg2[:, 0:16], s5_b_re[l, d, 2 * q:2 * q + 2].rearrange("g p c -> (g p) c"), W=["stg2"])
                K.dma(stg2[:, 16:32], s5_b_im[l, d, 2 * q:2 * q + 2].rearrange("g p c -> (g p) c"), W=["stg2"])
                br, bi = stg2[:, 0:16], stg2[:, 16:32]
                o1, o2 = stg2[:, 32:48], stg2[:, 48:64]
                grq, giq = s5p[:, q, 29:30], s5p[:, q, 30:31]
                Rq, Wq = ["stg2", "s5p"], ["stg2"]
                TS(dve, o1, bi, giq, -1.0, ALU.mult, ALU.mult, Rq, Wq)
                STT(dve, o1, br, grq, o1, ALU.mult, ALU.add, Rq, Wq)
                TS(dve, o2, br, giq, None, ALU.mult, None, Rq, Wq)
                STT(dve, o2, bi, grq, o2, ALU.mult, ALU.add, Rq, Wq)
                for o, st, sn_ in ((o1, stgA, "stgA"), (o2, stgB, "stgB")):
                    K.op(dve, lambda e: e.tensor_copy(st[0:64, base:base + 16], o[0:64, :]), R=["stg2"], W=[sn_])
                    K.op(dve, lambda e: e.tensor_copy(st[64:128, base + 16:base + 32], o[64:128, :]), R=["stg2"], W=[sn_])
            for ri, (st, sn_) in enumerate(((stgA, "stgA"), (stgB, "stgB"))):
                TR(psx[:, 0:128], st[:, :], ident, R=[sn_, "cs"], W=["psx"])
                K.op(dve, lambda e: e.tensor_copy(Bm[:, ri, j, :], psx[:, 0:128]), R=["psx"], W=["Bm"])
                K.op(dve, lambda e: e.tensor_copy(Bm[:, ri, 8 + j, :], psx[:, 0:128]), R=["psx"], W=["Bm"])
                K.op(dve, lambda e: e.memset(Bm[64:96, ri, 8 + j, :], 0.0), W=["Bm"])
            for ri, csrc in enumerate((s5_c_re, s5_c_im)):
                K.op(dve, lambda e: e.memset(stgA[:], 0.0), W=["stgA"])
                for qq in range(4):
                    q = 4 * j + qq
                    K.dma(stgA[32 * qq:32 * qq + 16, 0:64], csrc[l, d, 2 * q], W=["stgA"])
                    K.dma(stgA[32 * qq + 16:32 * qq + 32, 64:128], csrc[l, d, 2 * q + 1], W=["stgA"])
                TR(psx[:, 0:128], stgA[:, :], ident, R=["stgA", "cs"], W=["psx"])
                for qq in range(4):
                    co = 32 if qq == 3 else 0
                    cdst = Cm[:, ri, 4 * j + qq, co:co + 32]
                    TS(dve, cdst, psx[:, 32 * qq:32 * qq + 32], 1.0 if ri == 0 else -1.0, None, ALU.mult, None, R=["psx"], W=["Cm"])

    def load_norm(src, seg, r0, l, ext, hm=0):
        K.dma(xt, src[HALO + r0:HALO + r0 + T, :].rearrange("(s p) d -> p s d", p=128), W=["xt"])
        nsub = 4
        if ext:
            K.dma(xh[0:HALO, :], src[r0:r0 + HALO, :], W=["xh"])
            K.dma(xh[HALO:2 * HALO, :], src[HALO + r0 + T:HALO + r0 + T + HALO, :], W=["xh"])
            nsub = 5
        for s in range(nsub):
            np_ = 128 if s < 4 else 32
            xin_ = xt[:, s, :] if s < 4 else xh[:, :]
            rs = ["xt"] if s < 4 else ["xh"]
            A(act, junk[0:np_, :], xin_, AF.Square, R=rs, W=["junk", "ss"], accum=ss[0:np_, s:s + 1])
        TS(dve, rstd[:, 0:nsub], ss[:, 0:nsub], 1.0 / D, EPS, ALU.mult, ALU.add, R=["ss"], W=["rstd"])
        A(act, rstd[:, 0:nsub], rstd[:, 0:nsub], AF.Ln, R=["rstd"], W=["rstd"])
        A(act, rstd[:, 0:nsub], rstd[:, 0:nsub], AF.Exp, R=["rstd"], W=["rstd"], scale=-0.5)
        if ext and hm != 0:
            TT(dve, rstd[0:32, 4:5], rstd[0:32, 4:5], hmk[0:32, hm:hm + 1], ALU.mult, R=["rstd", "hmk"], W=["rstd"])
        for s in range(nsub):
            np_ = 128 if s < 4 else 32
            xin_ = xt[:, s, :] if s < 4 else xh[:, :]
            rs = ["xt"] if s < 4 else ["xh"]
            A(act, hb[0:np_, s, :], xin_, AF.Copy, R=rs + ["rstd"], W=["hb"], scale=rstd[0:np_, s:s + 1])
        for dc in range(16):
            half = dc % 2
            pb = psb[:, 0:T] if half == 0 else ps[5][:, 0:256].bitcast(BF16)
            pbn = "psb" if half == 0 else ("ps", 5)
            for s in range(4):
                TR(pb[:, s * 128:(s + 1) * 128], hb[:, s, dc * 128:(dc + 1) * 128], identb[:], R=["hb", "identb"], W=[pbn])
            A(act, hT[:, dc, HALO:HALO + T], pb, AF.Copy, R=[pbn, "vecT"], W=["hT"], scale=vecT[:, V_NG + dc:V_NG + dc + 1])
            if ext:
                TR(psx[:, 0:32].bitcast(BF16)[:, 0:32], hb[0:32, 4, dc * 128:(dc + 1) * 128], identb[0:32, 0:32], R=["hb", "identb"], W=["psx"])
                pxb = psx[:, 0:32].bitcast(BF16)
                A(act, hT[:, dc, 0:HALO], pxb[:, 0:HALO], AF.Copy, R=["psx", "vecT"], W=["hT"], scale=vecT[:, V_NG + dc:V_NG + dc + 1])
                A(act, hT[:, dc, HALO + T:TE], pxb[:, HALO:2 * HALO], AF.Copy, R=["psx", "vecT"], W=["hT"], scale=vecT[:, V_NG + dc:V_NG + dc + 1])

    def hT_k(kc):
        return hT[:, kc, :]

    def reset_states():
        K.op(dve, lambda e: e.memset(S32[:], 0.0), W=["S32"])
        K.op(dve, lambda e: e.memset(Sbf[:], 0.0), W=["Sbf"])
        K.op(dve, lambda e: e.memset(hc[:], 0.0), W=["hc"])

    def scale_states():
        TS(dve, S32[:].rearrange("p h k -> p (h k)"), S32[:].rearrange("p h k -> p (h k)"), cfl[:, 0:1], None, ALU.mult, None, R=["S32", "cfl"], W=["S32"])
        A(act, Sbf[:].rearrange("p h k -> p (h k)"), S32[:].rearrange("p h k -> p (h k)"), AF.Copy, R=["S32"], W=["Sbf"])
        TS(dve, hc[:].rearrange("p a q -> p (a q)"), hc[:].rearrange("p a q -> p (a q)"), cfl[:, 0:1], None, ALU.mult, None, R=["hc", "cfl"], W=["hc"])

    def soft_left(ti, nt):
        return segt is not None and ti > 0 and ti % segt == 0

    def soft_right(ti, nt):
        return segt is not None and ti < nt - 1 and ti % segt == segt - 1

    def hg_head(hd, d):
        q_, sg, f_, lf, b_, e1, e2, k_ = (sc[i][:, 0:T] for i in range(8))
        lcol = hd if d == 0 else 8 + hd
        TS(dve, f_, sg, omlb[:, lcol:lcol + 1], lbv[:, lcol:lcol + 1], ALU.mult, ALU.add, R=["sc1", "omlb", "lbv"], W=["sc2"])
        A(act, lf, f_, AF.Ln, R=["sc2"], W=["sc3"])
        TS(dve, k_, f_, -1.0, 1.0, ALU.mult, ALU.add, R=["sc2"], W=["sc7"])
        yield
        K.op(dve, lambda e: e.tensor_tensor_scan(b_, rmask, lf, 0.0, ALU.mult, ALU.add), R=["cs", "sc3"], W=["sc4"])
        yield
        bend = sc[4][:, 0:T].rearrange("p (c t) -> p c t", t=CH)[:, :, CH - 1]
        A(act, sc[8][:, 0:8], bend, AF.Exp, R=["sc4"], W=["sc8"])
        if d == 0:
            A(act, e1, b_, AF.Exp, R=["sc4"], W=["sc5"])
            A(act, e2, b_, AF.Exp, R=["sc4"], W=["sc6"], scale=-1.0)
        else:
            TT(dve, b_, b_, lf, ALU.subtract, R=["sc4", "sc3"], W=["sc4"])
            A(act, e1, b_, AF.Exp, R=["sc4"], W=["sc5"], scale=-1.0)
            A(act, e2, b_, AF.Exp, R=["sc4"], W=["sc6"])
        yield
        TT(dve, bq[:], q_, e1, ALU.mult, R=["sc0", "sc5"], W=["bq"])
        TT(dve, bk[:], k_, e2, ALU.mult, R=["sc7", "sc6"], W=["bk"])
        for s in range(4):
            TR(ps5b[:, s * 128:(s + 1) * 128], bk[:, s * 128:(s + 1) * 128], identb[:], R=["bk", "identb"], W=[("ps", 5)])
        K.op(dve, lambda e: e.tensor_copy(kT[:], ps5b.rearrange("p (s k) -> p s k", k=128)), R=[("ps", 5)], W=["kT"])
        yield
        for c in range(8):
            h64 = (c % 2) * 64
            MM(ps[5][h64:h64 + 64, (c // 2) * CH:(c // 2 + 1) * CH], bk[:, c * CH:(c + 1) * CH], bq[:, c * CH:(c + 1) * CH],
               True, True, R=["bk", "bq"], W=[("ps", 5)])
        yield
        msk = maskF if d == 0 else maskB
        TT(dve, At[:], ps[5][:, 0:256].rearrange("p (a t) -> p a t", t=CH), msk.rearrange("p (a t) -> p a t", t=CH),
           ALU.mult, R=[("ps", 5), "cs"], W=["At"])
        order = range(8) if d == 0 else range(7, -1, -1)
        for c in order:
            h64 = (c % 2) * 64
            et = sc[8][:, c:c + 1]
            if d == 1:
                TS(dve, S32[:, hd, :], S32[:, hd, :], et, None, ALU.mult, None, R=["S32", "sc8"], W=["S32"])
                A(act, Sbf[:, hd, :], S32[:, hd, :], AF.Copy, R=["S32"], W=["Sbf"])
            oc = ps[4][:, c * CH:(c + 1) * CH]
            MM(oc, vT[h64:h64 + 64, hd, c // 2, :], At[h64:h64 + 64, c // 2, :], True, False, R=["vT", "At"], W=[("ps", 4)])
            MM(oc, Sbf[:, hd, :], bq[:, c * CH:(c + 1) * CH], False, True, R=["Sbf", "bq"], W=[("ps", 4)])
            MM(psx[:, 0:128], kT[h64:h64 + 64, c // 2, :], vT[h64:h64 + 64, hd, c // 2, :], True, True, R=["kT", "vT"], W=["psx"])
            yield
            if d == 0:
                TT(dve, Tm[:], psx[:, 0:128], S32[:, hd, :], ALU.add, R=["psx", "S32"], W=["Tm"])
                TS(dve, S32[:, hd, :], Tm[:], et, None, ALU.mult, None, R=["Tm", "sc8"], W=["S32"])
                A(act, Sbf[:, hd, :], S32[:, hd, :], AF.Copy, R=["S32"], W=["Sbf"])
            else:
                TT(dve, S32[:, hd, :], psx[:, 0:128], S32[:, hd, :], ALU.add, R=["psx", "S32"], W=["S32"])
            yield

    def run_streams(gens, weights):
        alive = [True] * len(gens)
        while any(alive):
            for gi, g in enumerate(gens):
                if not alive[gi]:
                    continue
                for _ in range(weights[gi]):
                    try:
                        next(g)
                    except StopIteration:
                        alive[gi] = False
                        break

    def s5_dir(l, d):
        ETc, ETs = s5p[:, :, 26], s5p[:, :, 27]
        R_, W_ = ["s5p", "hc", "cin", "tmp32"], ["cin", "tmp32"]
        TT(dve, cin[:, 0, :], ETc, hc[:, 0, :], ALU.mult, R_, W_)
        TT(dve, tmp32[:, 0, :], ETs, hc[:, 1, :], ALU.mult, R_, W_)
        TT(dve, cin[:, 0, :], cin[:, 0, :], tmp32[:, 0, :], ALU.subtract, R_, W_)
        TT(dve, cin[:, 1, :], ETc, hc[:, 1, :], ALU.mult, R_, W_)
        TT(dve, tmp32[:, 0, :], ETs, hc[:, 0, :], ALU.mult, R_, W_)
        TT(dve, cin[:, 1, :], cin[:, 1, :], tmp32[:, 0, :], ALU.add, R_, W_)
        bA, bB, bC, bD = s5buf
        psI = psb[:, :].bitcast(F32)
        K.dma(tbuf[0], tabd[l][d][0], R=[("tabd",)], W=["arB", ("tb", 0)])
        pending = []
        for q in range(32):
            j, base = q // 4, 32 * (q % 4)
            tb = tbuf[q % 2]
            tn = ("tb", q % 2)
            if q + 1 < 32:
                K.dma(tbuf[(q + 1) % 2], tabd[l][d][q + 1], R=[("tabd",)], W=[("tb", (q + 1) % 2)])
            hsel = q % 2
            hbq = Hb[hsel]
            hbn = ("Hb", hsel)
            for ri, (pt, pn) in enumerate(((ps[2][:, :], ("ps", 2)), (psI, "psb"))):
                if base == 96:
                    MM(pt, Bm[64:128, ri, 8 + j, :], ubf[64:128, j, :], True, True, R=["Bm", "ubf"], W=[pn])
                else:
                    MM(pt, Bm[base:base + 32, ri, j, :], ubf[base:base + 32, j, :], True, True, R=["Bm", "ubf"], W=[pn])
            if d == 0:
                vr, vi = ps[2][:, :], psI
                hro, hio = hbq[0][:], hbq[1][:]
            else:
                vr, vi = ps[2][:, ::-1], psI[:, ::-1]
                hro, hio = hbq[0][:, ::-1], hbq[1][:, ::-1]
            cc, ssn = tb[:, 0, :], tb[:, 1, :]
            TT(dve, bA, cc, vr, ALU.mult, R=[tn, ("ps", 2)], W=["s5A"])
            TT(dve, bB, ssn, vi, ALU.mult, R=[tn, "psb"], W=["s5B"])
            TT(dve, bC, cc, vi, ALU.mult, R=[tn, "psb"], W=["s5C"])
            TT(dve, bD, ssn, vr, ALU.mult, R=[tn, ("ps", 2)], W=["s5D"])
            TT(dve, bA, bA, bB, ALU.add, R=["s5A", "s5B"], W=["s5A"])
            TT(dve, bC, bC, bD, ALU.subtract, R=["s5C", "s5D"], W=["s5C"])
            yield
            rho_bc = s5p[:, q, 0:1].to_broadcast([128, T])
            K.op(dve, lambda e: e.tensor_tensor_scan(bB, rho_bc, bA, cin[:, 0, q:q + 1], ALU.mult, ALU.add), R=["s5A", "s5p", "cin"], W=["s5B"])
            K.op(dve, lambda e: e.tensor_tensor_scan(bD, rho_bc, bC, cin[:, 1, q:q + 1], ALU.mult, ALU.add), R=["s5C", "s5p", "cin"], W=["s5D"])
            yield
            TT(dve, bA, cc, bB, ALU.mult, R=[tn, "s5B"], W=["s5A"])
            TT(dve, bC, ssn, bD, ALU.mult, R=[tn, "s5D"], W=["s5C"])
            TT(dve, hro, bA, bC, ALU.subtract, R=["s5A", "s5C", "arB"], W=[hbn])
            TT(dve, bA, cc, bD, ALU.mult, R=[tn, "s5D"], W=["s5A"])
            TT(dve, bC, ssn, bB, ALU.mult, R=[tn, "s5B"], W=["s5C"])
            TT(dve, hio, bA, bC, ALU.add, R=["s5A", "s5C", "arB"], W=[hbn])
            A(act, hc[:, 0, q:q + 1], bB[:, T - 1:T], AF.Copy, R=["s5B", "cin"], W=["hc"])
            A(act, hc[:, 1, q:q + 1], bD[:, T - 1:T], AF.Copy, R=["s5D", "cin"], W=["hc"])

            def cmm(q=q, j=j, base=base, hbq=hbq, hbn=hbn):
                if base < 64:
                    MM(ps[3][base:base + 32, :], Cm[:, 0, q, 0:32], hbq[0][:], True, False, R=["Cm", hbn], W=[("psY",)])
                    MM(ps[3][base:base + 32, :], Cm[:, 1, q, 0:32], hbq[1][:], False, True, R=["Cm", hbn], W=[("psY",)])
                else:
                    MM(ps[3][64:128, :], Cm[:, 0, q, :], hbq[0][:], base == 64, False, R=["Cm", hbn], W=[("psY",)])
                    MM(ps[3][64:128, :], Cm[:, 1, q, :], hbq[1][:], False, base == 96, R=["Cm", hbn], W=[("psY",)])
                if q % 4 == 3:
                    TT(dve, ysf[:, j, :], ysf[:, j, :], ps[3][:, :], ALU.add, R=[("psY",), "ysf"], W=["ysf"])
            pending.append(cmm)
            if len(pending) > 1:
                pending.pop(0)()
            yield
        while pending:
            pending.pop(0)()
        yield

    def spill(dst, src_ap, R):
        K.dma(dst, src_ap, R=R, W=[("spill",)])

    def passA(l, seg, ti):
        r0 = ti * T
        src = xin[seg] if l == 0 else x1[seg]
        load_norm(src, seg, r0, l, False)
        if ti == 0:
            reset_states()
        elif soft_left(ti, segs[seg] // T):
            scale_states()
        def streamH():
            for hd in range(8):
                def ev(i, m, gi, p, pslot, hd=hd):
                    if i == 0:
                        A(act, sc[0][:, 0:T], p, AF.Copy, R=[pslot], W=["sc0"])
                        spill(sp_q[seg][hd, :, r0:r0 + T], sc[0][:, 0:T], R=["sc0"])
                    elif i == 1:
                        A(act, sc[1][:, 0:T], p, AF.Sigmoid, R=[pslot], W=["sc1"])
                    else:
                        A(act, bvf[:], p, AF.Copy, R=[pslot], W=["bvf"])
                        for s in range(4):
                            TR(ps5b[:, s * 128:(s + 1) * 128], bvf[:, s * 128:(s + 1) * 128], identb[:], R=["bvf", "identb"], W=[("ps", 5)])
                        K.op(dve, lambda e: e.tensor_copy(vT[:, hd, :, :], ps5b.rearrange("p (s k) -> p s k", k=128)), R=[("ps", 5)], W=["vT"])
                        spill(sp_v[seg][r0:r0 + T, hd * 128:(hd + 1) * 128].rearrange("(s p) v -> p s v", p=128), vT[:, hd, :, :], R=["vT"])
                linear(wb_in[l], 16, [C_Q + hd, C_FF + hd, C_I + hd], hT_k, [(HALO, T)], ev, ["hT"])
                yield
                yield from hg_head(hd, 0)
                A(act, sc[9][:, 0:T], ps[4][:, :], AF.Copy, R=[("ps", 4)], W=["sc9"])
                spill(sp_o[seg][hd, :, r0:r0 + T], sc[9][:, 0:T], R=["sc9"])
                yield

        def streamS():
            def evu(i, m, gi, p, pslot):
                A(act, ubf[:, i, :], p, AF.Copy, R=[pslot], W=["ubf"])
                TS(dve, ysf[:, i, :], p, vecT[:, V_SD + i:V_SD + i + 1], None, ALU.mult, None, R=[pslot, "vecT"], W=["ysf"])
            linear(wb_in[l], 16, [C_DIN + j for j in range(8)], hT_k, [(HALO, T)], evu, ["hT"])
            spill(sp_u[seg][:, :, r0:r0 + T].rearrange("j p t -> p j t"), ubf, R=["ubf"])
            yield
            yield from s5_dir(l, 0)
            spill(sp_y[seg][:, :, r0:r0 + T].rearrange("j p t -> p j t"), ysf, R=["ysf"])
        run_streams([streamS(), streamH()], [SW_S, SW_H])

    def rstd_from(ps_ms, out_sc, R, W):
        TS(dve, out_sc, ps_ms, EPS, None, ALU.add, None, R=R, W=W)
        A(act, out_sc, out_sc, AF.Ln, R=W, W=W)
        A(act, out_sc, out_sc, AF.Exp, R=W, W=W, scale=-0.5)

    def passB(l, seg, ti, ntiles, last_layer):
        r0 = ti * T
        src = xin[seg] if l == 0 else x1[seg]
        load_norm(src, seg, r0, l, True, hm=(1 if soft_left(ti, ntiles) else (2 if soft_right(ti, ntiles) else 0)))
        if ti == ntiles - 1:
            reset_states()
        elif soft_right(ti, ntiles):
            scale_states()
        def streamD():
            K.dma(ubf, sp_u[seg][:, :, r0:r0 + T].rearrange("j p t -> p j t"), R=[("spill",)], W=["ubf"])
            K.dma(ysf, sp_y[seg][:, :, r0:r0 + T].rearrange("j p t -> p j t"), R=[("spill",)], W=["ysf"])
            yield
            yield from s5_dir(l, 1)
            for j in range(8):
                A(act, ysf[:, j, :], ysf[:, j, :], AF.Gelu, R=["ysf"], W=["ysf"])
                K.op(dve, lambda e: e.tensor_copy(zbf[:, j, :], ysf[:, j, :]), R=["ysf"], W=["zbf"])

            def evglu(i, m, gi, p, pslot):
                A(act, s5buf[0], p, AF.Sigmoid, R=[pslot, "vecT"], W=["arB"], bias=vecT[:, V_GB + i:V_GB + i + 1])
                TT(dve, ysf[:, i, :], ysf[:, i, :], s5buf[0], ALU.mult, R=["ysf", "arB"], W=["ysf"])
            linear(wb_glu[l], 8, list(range(8)), lambda kc: zbf[:, kc, :], [(0, T)], evglu, ["zbf"])

            def evdg(i, m, gi, p, pslot):
                A(act, s5buf[0], p, AF.Silu, R=[pslot], W=["arB"])
                TT(dve, yd[:, i, :], ysf[:, i, :], s5buf[0], ALU.mult, R=["ysf", "arB"], W=["yd"])
            linear(wb_in[l], 16, [C_DGATE + j for j in range(8)], hT_k, [(HALO, T)], evdg, ["hT"])
            yield

        def streamC():
            for hd in range(8):
                K.dma(sc[0][:, 0:T], sp_q[seg][hd, :, r0:r0 + T], R=[("spill",)], W=["sc0"])
                K.dma(vT[:, hd, :, :], sp_v[seg][r0:r0 + T, hd * 128:(hd + 1) * 128].rearrange("(s p) v -> p s v", p=128), R=[("spill",)], W=["vT"])
                K.dma(sc[9][:, 0:T], sp_o[seg][hd, :, r0:r0 + T], R=[("spill",)], W=["sc9"])

                def ev(i, m, gi, p, pslot):
                    A(act, sc[1][:, 0:T], p, AF.Sigmoid, R=[pslot], W=["sc1"])
                linear(wb_in[l], 16, [C_FB + hd], hT_k, [(HALO, T)], ev, ["hT"])
                yield
                yield from hg_head(hd, 1)
                TT(dve, sc[9][:, 0:T], ps[4][:, :], sc[9][:, 0:T], ALU.add, R=[("ps", 4), "sc9"], W=["sc9"])
                A(act, bq[:], sc[9][:, 0:T], AF.Square, R=["sc9"], W=["bq"])
                MM(ps[5][:, :], onesH[:], bq[:], True, True, R=["onesH", "bq"], W=[("ps", 5)])
                rstd_from(ps[5][:, :], sc[2][:, 0:T], R=[("ps", 5)], W=["sc2"])
                STT(dve, sc[3][:, 0:T], sc[9][:, 0:T], vecT[:, V_HG + hd:V_HG + hd + 1], sc[2][:, 0:T], ALU.mult, ALU.mult,
                    R=["sc9", "vecT", "sc2"], W=["sc3"])

                def evg(i, m, gi, p, pslot, hd=hd):
                    A(act, sc[5][:, 0:T], p, AF.Silu, R=[pslot], W=["sc5"])
                    TT(dve, yc[:, hd, :], sc[3][:, 0:T], sc[5][:, 0:T], ALU.mult, R=["sc3", "sc5"], W=["yc"])
                linear(wb_in[l], 16, [C_CGATE + hd], hT_k, [(HALO, T)], evg, ["hT"])
                yield
            yield

        run_streams([streamD(), streamC()], [SW_S, SW_H])
        CG = [(0, T), (T, 2 * HALO)]
        for c in range(8):
            def eva(i, m, gi, p, pslot, c=c):
                c0, ncol = CG[gi]
                if i == 0:
                    A(act, sc[0][:, c0:c0 + ncol], p, AF.Copy, R=[pslot], W=["sc0"])
                else:
                    A(act, sc[1][:, c0:c0 + ncol], p, AF.Sigmoid, R=[pslot], W=["sc1"])
                    TT(dve, ue[:, c, c0:c0 + ncol], sc[0][:, c0:c0 + ncol], sc[1][:, c0:c0 + ncol], ALU.mult, R=["sc0", "sc1"], W=["ue"])
            linear(wb_in[l], 16, [C_AVAL + c, C_AGLU + c], hT_k, CG, eva, ["hT"])
        for c in range(8):
            b = c % 2
            K.dma(cbuf[b][:], cdiag[l][c], R=[("cdiag",)], W=[("cbuf", b)])
            p, pslot = nps()
            for j in range(31):
                MM(p[:, :], cbuf[b][:, j, :], ue[:, c, j + 1:j + 1 + T], j == 0, j == 30, R=[("cbuf", b), "ue"], W=[pslot])
            A(act, ysf[:, c, :], p[:, :], AF.Identity, R=[pslot, "vecT"], W=["ysf"], bias=vecT[:, V_CB + c:V_CB + c + 1])
            K.op(dve, lambda e: e.tensor_copy(zbf[:, c, :], ysf[:, c, :]), R=["ysf"], W=["zbf"])
        for c in range(8):
            MM(ps[4][:, :], onesW[:], zbf[:, c, :], c == 0, c == 7, R=["onesW", "zbf"], W=[("ps", 4)])
        for c in range(8):
            A(act, bq[:], ysf[:, c, :], AF.Square, R=["ysf"], W=["bq"])
            MM(ps[5][:, :], onesW[:], bq[:], c == 0, c == 7, R=["onesW", "bq"], W=[("ps", 5)])
        A(act, sc[0][:, 0:T], ps[4][:, :], AF.Copy, R=[("ps", 4)], W=["sc0"])
        TT(dve, sc[1][:, 0:T], sc[0][:, 0:T], sc[0][:, 0:T], ALU.mult, R=["sc0"], W=["sc1"])
        TT(dve, sc[1][:, 0:T], ps[5][:, :], sc[1][:, 0:T], ALU.subtract, R=[("ps", 5), "sc1"], W=["sc1"])
        rstd_from(sc[1][:, 0:T], sc[2][:, 0:T], R=["sc1"], W=["sc2"])
        for c in range(8):
            TT(dve, sc[3][:, 0:T], ysf[:, c, :], sc[0][:, 0:T], ALU.subtract, R=["ysf", "sc0"], W=["sc3"])
            STT(dve, sc[3][:, 0:T], sc[3][:, 0:T], vecT[:, V_LNG + c:V_LNG + c + 1], sc[2][:, 0:T], ALU.mult, ALU.mult,
                R=["sc3", "vecT", "sc2"], W=["sc3"])
            A(act, sc[4][:, 0:T], sc[3][:, 0:T], AF.Silu, R=["sc3", "vecT"], W=["sc4"], bias=vecT[:, V_LNB + c:V_LNB + c + 1])

            def evag(i, m, gi, p, pslot, c=c):
                A(act, sc[5][:, 0:T], p, AF.Silu, R=[pslot], W=["sc5"])
                TT(dve, ya[:, c, :], sc[4][:, 0:T], sc[5][:, 0:T], ALU.mult, R=["sc4", "sc5"], W=["ya"])
            linear(wb_in[l], 16, [C_AGATE + c], hT_k, [(HALO, T)], evag, ["hT"])
        def evb(i, m, gi, p, pslot):
            c0, ncol = CG[gi]
            A(act, ue[:, i, c0:c0 + ncol], p, AF.Copy, R=[pslot], W=["ue"])
        linear(wb_in[l], 16, [C_BIN + c for c in range(8)], hT_k, CG, evb, ["hT"])
        for g in range(4):
            K.dma(rcb, rcin[seg][g, r0:r0 + T].partition_broadcast(128), W=["sc8"])
            for mm in range(2):
                c = 2 * g + mm

                def evz(i, m, gi, p, pslot):
                    c0, ncol = CG[gi]
                    A(act, sc[0][:, c0:c0 + ncol], p, AF.Copy, R=[pslot], W=["sc0"])
                linear(wb_pool[l][g], 2, [mm], lambda kc, g=g: ue[:, 2 * g + kc, :], CG, evz, ["ue"])
                z = sc[0]
                a, bb = sc[2], sc[3]
                TT(dve, a[:, 1:TE], z[:, 0:TE - 1], z[:, 1:TE], ALU.add, R=["sc0"], W=["sc2"])
                cur, lo, hi = a, 1, TE
                if g >= 1:
                    TT(dve, bb[:, lo + 1:hi - 1], cur[:, lo:hi - 2], cur[:, lo + 2:hi], ALU.add, R=["sc2"], W=["sc3"])
                    cur, lo, hi = bb, lo + 1, hi - 1
                if g >= 2:
                    TT(dve, a[:, lo + 2:hi - 2], cur[:, lo:hi - 4], cur[:, lo + 4:hi], ALU.add, R=["sc3"], W=["sc2"])
                    cur, lo, hi = a, lo + 2, hi - 2
                if g >= 3:
                    TT(dve, bb[:, lo + 4:hi - 4], cur[:, lo:hi - 8], cur[:, lo + 8:hi], ALU.add, R=["sc2"], W=["sc3"])
                    cur, lo, hi = bb, lo + 4, hi - 4
                cn_ = "sc2" if cur is a else "sc3"
                TT(dve, sc[4][:, 0:T], cur[:, HALO:HALO + T], rcb, ALU.mult, R=[cn_, "sc8"], W=["sc4"])
                TT(dve, sc[4][:, 0:T], sc[4][:, 0:T], z[:, HALO:HALO + T], ALU.subtract, R=["sc4", "sc0"], W=["sc4"])

                def evbg(i, m, gi, p, pslot, c=c):
                    A(act, sc[5][:, 0:T], p, AF.Silu, R=[pslot], W=["sc5"])
                    STT(dve, yb[:, c, :], sc[4][:, 0:T], vecT[:, V_PS + c:V_PS + c + 1], sc[5][:, 0:T], ALU.mult, ALU.mult,
                        R=["sc4", "vecT", "sc5"], W=["yb"])
                linear(wb_in[l], 16, [C_BGATE + c], hT_k, [(HALO, T)], evbg, ["hT"])
        ys = [ya, yb, yc, yd]
        yn = ["ya", "yb", "yc", "yd"]
        for dch in range(16):
            for i in range(4):
                def evr(ii, m, gi, p, pslot, i=i):
                    A(act, sc[5][:, 0:T], p, AF.Sigmoid, R=[pslot], W=["sc5"])
                linear(wb_in[l], 16, [C_R + i * 16 + dch], hT_k, [(HALO, T)], evr, ["hT"])

                def evo(ii, m, gi, p, pslot, i=i):
                    if i == 0:
                        TT(dve, sc[6][:, 0:T], p, sc[5][:, 0:T], ALU.mult, R=[pslot, "sc5"], W=["sc6"])
                    else:
                        TT(dve, sc[7][:, 0:T], p, sc[5][:, 0:T], ALU.mult, R=[pslot, "sc5"], W=["sc7"])
                        TT(dve, sc[6][:, 0:T], sc[6][:, 0:T], sc[7][:, 0:T], ALU.add, R=["sc6", "sc7"], W=["sc6"])
                linear(wb_ao[l][i], 8, [dch], lambda kc, i=i: ys[i][:, kc, :], [(0, T)], evo, [yn[i]])
            K.op(dve, lambda e: e.tensor_copy(mT[:, dch, :], sc[6][:, 0:T]), R=["sc6"], W=["mT"])
        def evout(i, m, gi, p, pslot):
            A(act, sc[0][:, 0:T], p, AF.Copy, R=[pslot], W=["sc0"])
            for s in range(4):
                TR(psx[:, 0:128], sc[0][:, s * 128:(s + 1) * 128], ident, R=["sc0", "cs"], W=["psx"])
                TT(dve, xt[:, s, i * 128:(i + 1) * 128], xt[:, s, i * 128:(i + 1) * 128], psx[:, 0:128], ALU.add, R=["psx", "xt"], W=["xt"])
        K.dma(xt, src[HALO + r0:HALO + r0 + T, :].rearrange("(s p) d -> p s d", p=128), W=["xt"])
        linear(wb_out[l], 16, list(range(16)), lambda kc: mT[:, kc, :], [(0, T)], evout, ["mT"])
        if not last_layer:
            K.dma(x1[seg][HALO + r0:HALO + r0 + T, :].rearrange("(s p) d -> p s d", p=128), xt, R=["xt"], W=[("x1", seg)])
        else:
            for s in range(4):
                A(act, junk[:, :], xt[:, s, :], AF.Square, R=["xt"], W=["junk", "ss"], accum=ss[:, s:s + 1])
            TS(dve, rstd[:, 0:4], ss[:, 0:4], 1.0 / D, EPS, ALU.mult, ALU.add, R=["ss"], W=["rstd"])
            A(act, rstd[:, 0:4], rstd[:, 0:4], AF.Ln, R=["rstd"], W=["rstd"])
            A(act, rstd[:, 0:4], rstd[:, 0:4], AF.Exp, R=["rstd"], W=["rstd"], scale=-0.5)
            K.dma(fgb, final_g.partition_broadcast(128), W=["fgb"])
            for s in range(4):
                STT(dve, xt[:, s, :], xt[:, s, :], rstd[:, s:s + 1], fgb, ALU.mult, ALU.mult, R=["xt", "rstd", "fgb"], W=["xt"])
            K.dma(yout[seg][r0:r0 + T, :].rearrange("(s p) d -> p s d", p=128), xt, R=["xt"], W=[("yout", seg)])

    for l in range(depth):
        prep_layer(l)
        prep_s5(l, 0)
        for seg in range(nseg):
            nt = segs[seg] // T
            for ti in range(nt):
                passA(l, seg, ti)
        prep_s5(l, 1)
        for seg in range(nseg):
            nt = segs[seg] // T
            for ti in range(nt - 1, -1, -1):
                passB(l, seg, ti, nt, l == depth - 1)
    K.finish()
    return nc


def _consts():
    c = np.zeros((128, 128 + 256 + 256 + 512), np.float32)
    c[:, 0:128] = np.eye(128, dtype=np.float32)
    s = np.arange(128) % 64
    t = np.arange(256) % 64
    c[:, 128:384] = (s[:, None] <= t[None, :]).astype(np.float32)
    c[:, 384:640] = (s[:, None] >= t[None, :]).astype(np.float32)
    rm = np.ones(512, np.float32)
    rm[::64] = 0.0
    c[:, 640:1152] = rm[None, :]
    return c


def _rc(L):
    t = np.arange(L)
    out = np.zeros((4, L), np.float32)
    for g, win in enumerate((2, 4, 8, 16)):
        left = win // 2
        right = win - 1 - left
        lo = np.maximum(t - left, 0)
        hi = np.minimum(t + right, L - 1) + 1
        out[g] = 1.0 / (hi - lo).astype(np.float32)
    return out


_WNAMES = ["norm_g", "w_in", "conv_w", "conv_b", "conv_ln_g", "conv_ln_b", "w_a_out", "pool_w", "pool_scale", "w_b_out",
           "hg_lb", "hg_norm_g", "w_c_out", "s5_a_re", "s5_a_im", "s5_log_dt", "s5_b_re", "s5_b_im", "s5_c_re", "s5_c_im",
           "s5_d", "s5_glu_w", "s5_glu_b", "w_d_out", "w_out", "final_g"]


def run(x_prompt, x_sample, weights, ncores=8):
    Lp, Ls = x_prompt.shape[1], x_sample.shape[1]
    depth = weights["w_in"].shape[0]
    nb_p, nb_s = x_prompt.shape[0], x_sample.shape[0]
    slots = Lp // Ls
    segt = Ls // T
    nc = build([Lp], depth, segt=segt)
    base = {k: np.ascontiguousarray(np.asarray(weights[k], dtype=np.float32)) for k in _WNAMES}
    base["cst"] = _consts()
    rc_p = _rc(Lp)
    rc_s = np.ascontiguousarray(np.tile(_rc(Ls), (1, slots)))
    n_score = ncores - nb_p
    per = -(-nb_s // n_score)
    assert per <= slots
    in_maps, assign = [], []
    for c in range(ncores):
        m = dict(base)
        xp = np.zeros((Lp + 2 * HALO, D), np.float32)
        hm = np.ones((32, 3), np.float32)
        if c < nb_p:
            xp[HALO:HALO + Lp] = x_prompt[c]
            m["rc0"] = rc_p
            cf = 1.0
            assign.append(("p", c))
        else:
            ids = [i for i in range((c - nb_p) * per, min((c - nb_p + 1) * per, nb_s))]
            for k, i in enumerate(ids):
                xp[HALO + k * Ls:HALO + (k + 1) * Ls] = x_sample[i]
            m["rc0"] = rc_s
            cf = 0.0
            assign.append(("s", ids))
        hm[0:16, 1] = cf
        hm[16:32, 2] = cf
        m["x0"] = xp
        m["cfl"] = np.full((128, 1), cf, np.float32)
        m["hmk"] = hm
        in_maps.append(m)
    res = run_bass_kernel_spmd(nc, in_maps, core_ids=list(range(ncores)))
    yp = np.zeros(x_prompt.shape, np.float32)
    ysm = np.zeros(x_sample.shape, np.float32)
    for c in range(ncores):
        kind, ids = assign[c]
        y = res.results[c]["y0"]
        if kind == "p":
            yp[ids] = y
        else:
            for k, i in enumerate(ids):
                ysm[i] = y[k * Ls:(k + 1) * Ls]
    return yp, ysm


def kernel(x_prompt, x_sample, **weights):
    x_prompt = np.asarray(x_prompt, dtype=np.float32)
    x_sample = np.asarray(x_sample, dtype=np.float32)
    return run(x_prompt, x_sample, weights)
```

```python
import numpy as np
import concourse.bass as bass
import concourse.mybir as mybir
from concourse.bass_utils import run_bass_kernel_spmd

F32 = mybir.dt.float32
BF16 = mybir.dt.bfloat16
AF = mybir.ActivationFunctionType
ALU = mybir.AluOpType

D = 2048
WB = 1024
NIN = 20480
T = 512
HALO = 16
TE = T + 2 * HALO
EPS = 1e-6
CH = 64
PAD = 256
POOL_MOD = 1000
SW_S = 2
SW_H = 1
C_AVAL, C_AGLU, C_AGATE, C_BIN, C_BGATE, C_Q, C_FF, C_FB, C_I, C_CGATE, C_DIN, C_DGATE, C_R = (
    0, 8, 16, 24, 32, 40, 48, 56, 64, 72, 80, 88, 96)
V_NG, V_CB, V_LNG, V_LNB, V_PS, V_HG, V_SD, V_GB, V_LB0, V_LB1 = 0, 16, 24, 32, 40, 48, 56, 64, 72, 88
NVEC = 104


class Slot:
    __slots__ = ("w", "r")

    def __init__(self):
        self.w = {}
        self.r = {}


class EngW:
    def __init__(self, nc, eng, name, selfwait):
        self.eng = eng
        self.sem = nc.alloc_semaphore(name)
        self.n = 0
        self.seen = {}
        self.selfwait = selfwait


class Ker:
    def __init__(self, nc):
        self.nc = nc
        self.pe = EngW(nc, nc.tensor, "s_pe", False)
        self.act = EngW(nc, nc.scalar, "s_act", True)
        self.dve = EngW(nc, nc.vector, "s_dve", True)
        self.pool = EngW(nc, nc.gpsimd, "s_pool", True)
        self.sp = EngW(nc, nc.sync, "s_sp", False)
        self.ndsem = 24
        self.dsem = [nc.alloc_semaphore(f"s_dma{i}") for i in range(self.ndsem)]
        self.dval = [0] * self.ndsem
        self.dnext = 0
        self.slots = {}
        self.canon = {"xt": "arA", "ysf": "arA", "ubf": "arA", "zbf": "arA", ("cvt", 0): "arA", ("cvt", 1): "arA",
                      ("cvb", 0): "arA", ("cvb", 1): "arA",
                      "arB": "arB", ("ks", 1): "arB", "ue": "arB", ("cbuf", 0): "arB", ("cbuf", 1): "arB",
                      "hb": "arC", "mT": "arC", "hT": "arD", "fgb": "arD", "junk": "arD", "xh": "arB", ("psb", 0): "psb", ("psb", 1): "psb", "ya": "arY", "yb": "arY", ("kp", 0): "arY", ("kp", 1): "arY"}

    def slot(self, key):
        key = self.canon.get(key, key)
        s = self.slots.get(key)
        if s is None:
            s = Slot()
            self.slots[key] = s
        return s

    def _wait(self, E, R, W):
        deps = {}
        for k in R:
            for i, ev in self.slot(k).w.items():
                if deps.get(i, (None, 0))[1] < ev[1]:
                    deps[i] = ev
        for k in W:
            s = self.slot(k)
            for dd in (s.w, s.r):
                for i, ev in dd.items():
                    if deps.get(i, (None, 0))[1] < ev[1]:
                        deps[i] = ev
        for i, (sem, val) in deps.items():
            if sem is E.sem and not E.selfwait:
                continue
            if E.seen.get(i, 0) >= val:
                continue
            E.eng.wait_ge(sem, val)
            E.seen[i] = val

    def _update(self, ev, R, W):
        i = id(ev[0])
        for k in R:
            self.slot(k).r[i] = ev
        for k in W:
            s = self.slot(k)
            if s.r:
                s.r = {}
                s.w = {}
            s.w[i] = ev

    def op(self, E, fn, R=(), W=()):
        self._wait(E, R, W)
        ins = fn(E.eng)
        E.n += 1
        ins.then_inc(E.sem, 1)
        self._update((E.sem, E.n), R, W)

    def dma(self, out, in_, R=(), W=()):
        E = self.sp
        self._wait(E, R, W)
        k = self.dnext
        self.dnext = (k + 1) % self.ndsem
        sem = self.dsem[k]
        if self.dval[k] > 0 and E.seen.get(id(sem), 0) < self.dval[k]:
            E.eng.wait_ge(sem, self.dval[k])
            E.seen[id(sem)] = self.dval[k]
        E.eng.dma_start(out=out, in_=in_).then_inc(sem, 16)
        self.dval[k] += 16
        self._update((sem, self.dval[k]), R, W)

    def finish(self):
        E = self.sp
        for k in range(self.ndsem):
            if self.dval[k] > 0:
                E.eng.wait_ge(self.dsem[k], self.dval[k])
        for X in (self.pe, self.act, self.dve, self.pool):
            if X.n > 0:
                E.eng.wait_ge(X.sem, X.n)


def build(segs, depth=2, segt=None):
    nc = bass.Bass("TRN2", target_bir_lowering=False)
    K = Ker(nc)
    pe, act, dve, pool = K.pe, K.act, K.dve, K.pool
    nseg = len(segs)

    def din(name, shape, dt=F32):
        return nc.dram_tensor(name, list(shape), dt, kind="ExternalInput").ap()

    def dscr(name, shape, dt=F32):
        return nc.dram_tensor(name, list(shape), dt).ap()

    xin = [din(f"x{s}", [segs[s] + 2 * HALO, D]) for s in range(nseg)]
    yout = [nc.dram_tensor(f"y{s}", [segs[s], D], F32, kind="ExternalOutput").ap() for s in range(nseg)]
    x1 = [dscr(f"x1_{s}", [segs[s] + 2 * HALO, D]) for s in range(nseg)]
    rcin = [din(f"rc{s}", [4, segs[s]]) for s in range(nseg)]
    cst = din("cst", [128, 128 + 256 + 256 + 512])
    cfl_d = din("cfl", [128, 1])
    hmk_d = din("hmk", [32, 3])
    norm_g = din("norm_g", [depth, D]); w_in = din("w_in", [depth, D, NIN])
    conv_w = din("conv_w", [depth, 31, WB]); conv_b = din("conv_b", [depth, WB])
    conv_ln_g = din("conv_ln_g", [depth, WB]); conv_ln_b = din("conv_ln_b", [depth, WB])
    w_a_out = din("w_a_out", [depth, WB, D]); pool_w = din("pool_w", [depth, 4, 256, 256])
    pool_scale = din("pool_scale", [depth, WB]); w_b_out = din("w_b_out", [depth, WB, D])
    hg_lb = din("hg_lb", [depth, 2, WB]); hg_norm_g = din("hg_norm_g", [depth, WB])
    w_c_out = din("w_c_out", [depth, WB, D])
    s5_a_re = din("s5_a_re", [depth, 2, 64, 64]); s5_a_im = din("s5_a_im", [depth, 2, 64, 64])
    s5_log_dt = din("s5_log_dt", [depth, 2, 64])
    s5_b_re = din("s5_b_re", [depth, 2, 64, 64, 16]); s5_b_im = din("s5_b_im", [depth, 2, 64, 64, 16])
    s5_c_re = din("s5_c_re", [depth, 2, 64, 16, 64]); s5_c_im = din("s5_c_im", [depth, 2, 64, 16, 64])
    s5_d = din("s5_d", [depth, WB]); s5_glu_w = din("s5_glu_w", [depth, WB, WB])
    s5_glu_b = din("s5_glu_b", [depth, WB]); w_d_out = din("w_d_out", [depth, WB, D])
    w_out = din("w_out", [depth, D, D]); final_g = din("final_g", [D])

    def wscr(name, Kdim, N):
        return dscr(name, [N // 128, 128, Kdim // 128, 128], BF16)
    wb_in = [wscr(f"wb_in{l}", D, NIN) for l in range(depth)]
    wb_ao = [[wscr(f"wb_o{i}_{l}", WB, D) for i in range(4)] for l in range(depth)]
    wb_out = [wscr(f"wb_out{l}", D, D) for l in range(depth)]
    wb_glu = [wscr(f"wb_glu{l}", WB, WB) for l in range(depth)]
    wb_pool = [[wscr(f"wb_pool{l}_{g}", 256, 256) for g in range(4)] for l in range(depth)]
    cdiag = [dscr(f"cdiag{l}", [8, 128, 31, 128], BF16) for l in range(depth)]
    tabd = [[dscr(f"tabd{l}_{d}", [32, 128, 2, T]) for d in range(2)] for l in range(depth)]
    sp_q = [dscr(f"sp_q{s}", [8, 128, segs[s]]) for s in range(nseg)]
    sp_o = [dscr(f"sp_o{s}", [8, 128, segs[s]]) for s in range(nseg)]
    sp_y = [dscr(f"sp_y{s}", [8, 128, segs[s]]) for s in range(nseg)]
    sp_u = [dscr(f"sp_u{s}", [8, 128, segs[s]], BF16) for s in range(nseg)]
    sp_v = [dscr(f"sp_v{s}", [segs[s], WB], BF16) for s in range(nseg)]

    def sb(name, shape, dt=F32):
        return nc.alloc_sbuf_tensor(name, list(shape), dt)
    cs = sb("cs", [128, 128 + 256 + 256 + 512])
    ident = cs[:, 0:128]
    maskF = cs[:, 128:384]
    maskB = cs[:, 384:640]
    rmask = cs[:, 640:1152]
    identb = sb("identb", [128, 128], BF16)
    onesW = sb("onesW", [128, 128], BF16)
    onesH = sb("onesH", [128, 128], BF16)
    vecT = sb("vecT", [128, NVEC])
    lbv = sb("lbv", [128, 16]); omlb = sb("omlb", [128, 16]); nomlb = sb("nomlb", [128, 16])
    arA = sb("arA", [128, 8192])
    arB = sb("arB", [128, 4352])
    arC = sb("arC", [128, 5120])
    arD = sb("arD", [128, 4352])
    xt = arA[:, :].rearrange("p (s d) -> p s d", d=D)
    ysf = arA[:, 0:4096].rearrange("p (j t) -> p j t", t=T)
    ubf = arA[:, 4096:6144].bitcast(BF16).rearrange("p (j t) -> p j t", t=T)
    zbf = arA[:, 6144:8192].bitcast(BF16).rearrange("p (j t) -> p j t", t=T)
    cvt = [arA[:, i * 2048:(i + 1) * 2048].rearrange("p (k n) -> p k n", n=128) for i in range(2)]
    cvb = [arA[:, 4096 + i * 1024:4096 + (i + 1) * 1024].bitcast(BF16).rearrange("p (k n) -> p k n", n=128) for i in range(2)]
    s5buf = [arB[:, i * 512:(i + 1) * 512] for i in range(4)]
    tbuf = [arB[:, 2048 + i * 1024:2048 + (i + 1) * 1024].rearrange("p (a t) -> p a t", t=T) for i in range(2)]
    ks = [[s5buf[0]]]
    ue = arB[:, 0:2176].bitcast(BF16).rearrange("p (c t) -> p c t", t=TE)
    cb0 = arB[:, 2176:2176 + 1984].bitcast(BF16).rearrange("p (j n) -> p j n", n=128)
    cbuf = [cb0, cb0]
    hb = arC[:, :].bitcast(BF16).rearrange("p (s d) -> p s d", d=D)
    mT = arC[:, 0:4096].bitcast(BF16).rearrange("p (c t) -> p c t", t=T)
    hT = arD[:, :].bitcast(BF16).rearrange("p (c t) -> p c t", t=TE)
    fgb = arD[:, 0:D]
    xh = arB[0:32, 0:D]
    NWB = 3
    wbuf = [sb(f"wbuf{i}", [128, 16, 128], BF16) for i in range(NWB)]
    ss = sb("ss", [128, 8]); rstd = sb("rstd", [128, 8])
    junk = arD[:, 2048:3072].bitcast(BF16)
    NSC = 10
    sc = [sb(f"sc{i}", [128, TE]) for i in range(NSC)]
    bq = sb("bq", [128, T], BF16); bk = sb("bk", [128, T], BF16); bvf = sb("bvf", [128, T], BF16)
    vT = sb("vT", [128, 8, 4, 128], BF16)
    kT = sb("kT", [128, 4, 128], BF16)
    At = sb("At", [128, 4, CH], BF16)
    S32 = sb("S32", [128, 8, 128]); Sbf = sb("Sbf", [128, 8, 128], BF16); Tm = sb("Tm", [128, 128])
    Hb = [[sb(f"Hb{a}{b}", [128, T], BF16) for b in range(2)] for a in range(2)]
    s5p = sb("s5p", [128, 32, 32])
    hc = sb("hc", [128, 2, 32]); cin = sb("cin", [128, 2, 32]); tmp32 = sb("tmp32", [128, 4, 32])
    Bm = sb("Bm", [128, 2, 16, 128], BF16)
    Cm = sb("Cm", [128, 2, 32, 64], BF16)
    stg = sb("stg", [128, 128])
    stgA = sb("stgA", [128, 128]); stgB = sb("stgB", [128, 128])
    stg2 = sb("stg2", [128, 256])
    cfl = sb("cflag", [128, 1]); hmk = sb("hmask", [32, 3])
    tmpi = sb("tmpi", [128, 32], mybir.dt.int32)
    arY = sb("arY", [128, 4096])
    ya = arY[:, 0:2048].bitcast(BF16).rearrange("p (c t) -> p c t", t=T)
    yb = arY[:, 2048:4096].bitcast(BF16).rearrange("p (c t) -> p c t", t=T)
    ks2 = [[arY[:, (2 * a + b) * 1024:(2 * a + b + 1) * 1024] for b in range(2)] for a in range(2)]
    yc = sb("yc", [128, 8, T], BF16); yd = sb("yd", [128, 8, T], BF16)
    rcb = sc[8][:, 0:T]
    print("sbuf remaining", nc.sbuf_bytes_remaining)
    ps = [nc.alloc_psum_tensor(f"ps{i}", [128, T], F32) for i in range(6)]
    psb = nc.alloc_psum_tensor("psb", [128, 2 * T], BF16)
    psx = nc.alloc_psum_tensor("psx", [128, T], F32)
    ps5b = ps[5][:, 256:512].bitcast(BF16)
    pcnt = [0]

    def nps():
        i = pcnt[0] % 2
        pcnt[0] += 1
        return ps[i], ("ps", i)

    def A(E, out, in_, func, R, W, bias=None, scale=None, accum=None):
        kw = {}
        if bias is not None:
            kw["bias"] = bias
        if scale is not None:
            kw["scale"] = scale
        if accum is not None:
            kw["accum_out"] = accum
        K.op(act, lambda e: e.activation(out=out, in_=in_, func=func, **kw), R, W)

    def TS(E, out, in0, s1, s2, op0, op1, R, W):
        if s2 is None:
            K.op(E, lambda e: e.tensor_scalar(out, in0, s1, None, op0), R, W)
        else:
            K.op(E, lambda e: e.tensor_scalar(out, in0, s1, s2, op0, op1), R, W)

    def TT(E, out, in0, in1, op, R, W):
        K.op(E, lambda e: e.tensor_tensor(out, in0, in1, op), R, W)

    def STT(E, out, in0, scalar, in1, op0, op1, R, W):
        K.op(E, lambda e: e.scalar_tensor_tensor(out, in0, scalar, in1, op0, op1), R, W)

    def MM(out, lhsT, rhs, start, stop, R, W):
        K.op(pe, lambda e: e.matmul(out, lhsT, rhs, start=start, stop=stop), R, W)

    def TR(out, in_, idn, R, W):
        K.op(pe, lambda e: e.transpose(out, in_, idn), R, W)

    wcnt = [0]

    def linear(wb, nk, mlist, rhs_fn, ncols_list, evac, Rin):
        n = len(mlist)
        loaded = {}

        def load(i):
            b = wcnt[0] % NWB
            wcnt[0] += 1
            K.dma(wbuf[b][:, 0:nk, :], wb[mlist[i]], R=[], W=[("wbuf", b)])
            loaded[i] = b
        for i in range(min(NWB - 1, n)):
            load(i)
        for i in range(n):
            if i + NWB - 1 < n:
                load(i + NWB - 1)
            b = loaded.pop(i)
            for gi, (c0, ncol) in enumerate(ncols_list):
                p, pslot = nps()
                for kc in range(nk):
                    MM(p[:, 0:ncol], wbuf[b][:, kc, :], rhs_fn(kc)[:, c0:c0 + ncol], kc == 0, kc == nk - 1,
                       R=[("wbuf", b)] + Rin, W=[pslot])
                evac(i, mlist[i], gi, p[:, 0:ncol], pslot)

    ccnt = [0]

    def convert(src, Kdim, N, dst):
        nk = Kdim // 128
        srcv = src.rearrange("(kc kp) n -> kp kc n", kp=128)
        dstv = dst.rearrange("m kp kc j -> kp m kc j")
        for n0 in range(0, N, 128):
            i = ccnt[0] % 2
            ccnt[0] += 1
            K.dma(cvt[i][:, 0:nk, :], srcv[:, :, n0:n0 + 128], R=[], W=[("cvt", i)])
            eng = act if (ccnt[0] % 2 == 0) else dve
            if eng is act:
                A(act, cvb[i][:, 0:nk, :], cvt[i][:, 0:nk, :], AF.Copy, R=[("cvt", i)], W=[("cvb", i)])
            else:
                K.op(dve, lambda e: e.tensor_copy(cvb[i][:, 0:nk, :], cvt[i][:, 0:nk, :]), R=[("cvt", i)], W=[("cvb", i)])
            K.dma(dstv[:, n0 // 128, :, :], cvb[i][:, 0:nk, :], R=[("cvb", i)], W=[("wdram",)])

    K.dma(cs[:], cst[:, :], W=["cs"])
    K.dma(cfl[:], cfl_d[:, :], W=["cfl"])
    K.dma(hmk[:], hmk_d[:, :], W=["hmk"])
    K.op(dve, lambda e: e.tensor_copy(identb[:], ident), R=["cs"], W=["identb"])
    K.op(dve, lambda e: e.memset(ss[:], 1.0), W=["ss"])
    K.op(dve, lambda e: e.memset(onesW[:], 1.0 / 1024.0), W=["onesW"])
    K.op(dve, lambda e: e.memset(onesH[:], 1.0 / 128.0), W=["onesH"])
    for s in range(nseg):
        for off in (0, HALO + segs[s]):
            K.op(dve, lambda e: e.memset(xh[0:HALO, :], 0.0), W=["xh"])
            K.dma(x1[s][off:off + HALO, :], xh[0:HALO, :], R=["xh"], W=[("x1", s)])

    for l in range(depth):
        convert(w_in[l], D, NIN, wb_in[l])
        for i, wsrc in enumerate((w_a_out, w_b_out, w_c_out, w_d_out)):
            convert(wsrc[l], WB, D, wb_ao[l][i])
        convert(w_out[l], D, D, wb_out[l])
        convert(s5_glu_w[l], WB, WB, wb_glu[l])
        for g in range(4):
            convert(pool_w[l, g], 256, 256, wb_pool[l][g])

    def prep_layer(l):
        rows = [(norm_g[l], 16), (conv_b[l], 8), (conv_ln_g[l], 8), (conv_ln_b[l], 8), (pool_scale[l], 8),
                (hg_norm_g[l], 8), (s5_d[l], 8), (s5_glu_b[l], 8), (hg_lb[0, 0], 8), (hg_lb[0, 1], 8),
                (hg_lb[1 if depth > 1 else 0, 0], 8), (hg_lb[1 if depth > 1 else 0, 1], 8)]
        r = 0
        K.op(dve, lambda e: e.memset(stg[:], 0.0), W=["stg"])
        for v, n in rows:
            K.dma(stg[r:r + n, :], v.rearrange("(c p) -> c p", p=128), W=["stg"])
            r += n
        TR(psx[:, 0:128], stg[:, :], ident, R=["stg", "cs"], W=["psx"])
        K.op(dve, lambda e: e.tensor_copy(vecT[:], psx[:, 0:NVEC]), R=["psx"], W=["vecT"])
        if l == 0:
            K.op(dve, lambda e: e.memset(lbv[:], 0.0), W=["lbv"])
        else:
            TT(dve, lbv[:], vecT[:, V_LB1:V_LB1 + 16], vecT[:, V_LB0:V_LB0 + 16], ALU.subtract, R=["vecT"], W=["lbv"])
            A(act, lbv[:], lbv[:], AF.Sigmoid, R=["lbv"], W=["lbv"])
        TS(dve, omlb[:], lbv[:], -1.0, 1.0, ALU.mult, ALU.add, R=["lbv"], W=["omlb"])
        TS(dve, nomlb[:], omlb[:], -1.0, None, ALU.mult, None, R=["omlb"], W=["omlb"])
        for half in range(2):
            nr = 128 if half == 0 else 31 * 8 - 128
            K.op(dve, lambda e: e.memset(stg[:], 0.0), W=["stg"])
            src = conv_w[l].rearrange("j (c p) -> (j c) p", p=128)
            K.dma(stg[0:nr, :], src[half * 128:half * 128 + nr, :], W=["stg"])
            TR(psx[:, 0:128], stg[:, :], ident, R=["stg", "cs"], W=["psx"])
            K.op(dve, lambda e: e.tensor_copy(stg2[:, half * 128:half * 128 + 128], psx[:, 0:128]), R=["psx"], W=["stg2"])
        for c in range(8):
            b = c % 2
            for j in range(31):
                col = j * 8 + c
                TS(dve, cbuf[b][:, j, :], ident, stg2[:, col:col + 1], None, ALU.mult, None, R=["stg2", "cs"], W=[("cbuf", b)])
            K.dma(cdiag[l][c], cbuf[b][:], R=[("cbuf", b)], W=[("cdiag",)])

    def prep_s5(l, d):
        def ldT(src, dstcol):
            K.op(dve, lambda e: e.memset(stg[:], 0.0), W=["stg"])
            K.dma(stg[0:32, :], src.rearrange("(q two) p -> q (two p)", two=2), W=["stg"])
            TR(psx[:, 0:128], stg[:, :], ident, R=["stg", "cs"], W=["psx"])
            K.op(dve, lambda e: e.tensor_copy(tmp32[:, dstcol, :], psx[:, 0:32]), R=["psx"], W=["tmp32"])
        ldT(s5_a_re[l, d], 0)
        ldT(s5_a_im[l, d], 1)
        K.op(dve, lambda e: e.memset(stg[:], 0.0), W=["stg"])
        K.dma(stg2[0:32, 0:2], s5_log_dt[l, d].rearrange("(q two) -> q two", two=2), W=["stg2"])
        for two in range(2):
            TS(dve, stg[0:32, two * 64:(two + 1) * 64], stg[0:32, two * 64:(two + 1) * 64], stg2[0:32, two:two + 1], None,
               ALU.add, None, R=["stg", "stg2"], W=["stg"])
        TR(psx[:, 0:128], stg[:, :], ident, R=["stg", "cs"], W=["psx"])
        A(act, tmp32[:, 2, :], psx[:, 0:32], AF.Exp, R=["psx"], W=["tmp32"])
        are, aim, dt = tmp32[:, 0, :], tmp32[:, 1, :], tmp32[:, 2, :]
        t3 = tmp32[:, 3, :]
        R_, W_ = ["tmp32", "s5p"], ["tmp32", "s5p"]
        mag, ang, sn, cn = s5p[:, :, 27], s5p[:, :, 28], s5p[:, :, 29], s5p[:, :, 30]
        TT(dve, mag, dt, are, ALU.mult, R_, W_)
        A(act, mag, mag, AF.Exp, R_, W_)
        TT(dve, ang, dt, aim, ALU.mult, R_, W_)
        TWO_PI = 2.0 * np.pi

        def sin_of(dst, phase):
            TS(dve, dst, ang, 1.0 / TWO_PI, phase / TWO_PI, ALU.mult, ALU.add, R_, W_)
            K.op(dve, lambda e: e.tensor_copy(tmpi[:], dst), R_ + ["tmpi"], W_ + ["tmpi"])
            K.op(dve, lambda e: e.tensor_copy(t3, tmpi[:]), R_ + ["tmpi"], W_)
            TT(dve, dst, dst, t3, ALU.subtract, R_, W_)
            TS(dve, t3, dst, 0.5, None, ALU.is_gt, None, R_, W_)
            TT(dve, dst, dst, t3, ALU.subtract, R_, W_)
            TS(dve, t3, dst, -0.5, None, ALU.is_lt, None, R_, W_)
            TT(dve, dst, dst, t3, ALU.add, R_, W_)
            A(act, dst, dst, AF.Sin, R_, W_, scale=TWO_PI)
        sin_of(sn, 0.0)
        sin_of(cn, np.pi / 2)
        lre, lim = s5p[:, :, 1], s5p[:, :, 2]
        TT(dve, lre, mag, cn, ALU.mult, R_, W_)
        TT(dve, lim, mag, sn, ALU.mult, R_, W_)
        K.op(dve, lambda e: e.tensor_copy(s5p[:, :, 0], mag), R_, W_)
        K.op(dve, lambda e: e.tensor_copy(s5p[:, :, 8], cn), R_, W_)
        K.op(dve, lambda e: e.tensor_copy(s5p[:, :, 9], sn), R_, W_)
        den, xm, gre, gim = s5p[:, :, 27], s5p[:, :, 28], s5p[:, :, 29], s5p[:, :, 30]
        TT(dve, den, are, are, ALU.mult, R_, W_)
        TT(dve, t3, aim, aim, ALU.mult, R_, W_)
        TT(dve, den, den, t3, ALU.add, R_, W_)
        K.op(dve, lambda e: e.reciprocal(den, den), R_, W_)
        TS(dve, xm, lre, -1.0, None, ALU.add, None, R_, W_)
        TT(dve, gre, xm, are, ALU.mult, R_, W_)
        TT(dve, t3, lim, aim, ALU.mult, R_, W_)
        TT(dve, gre, gre, t3, ALU.add, R_, W_)
        TT(dve, gre, gre, den, ALU.mult, R_, W_)
        TT(dve, gim, lim, are, ALU.mult, R_, W_)
        TT(dve, t3, xm, aim, ALU.mult, R_, W_)
        TT(dve, gim, gim, t3, ALU.subtract, R_, W_)
        TT(dve, gim, gim, den, ALU.mult, R_, W_)
        for k in range(1, 10):
            pr, pi_, qr, qi = s5p[:, :, 6 + 2 * k], s5p[:, :, 7 + 2 * k], s5p[:, :, 8 + 2 * k], s5p[:, :, 9 + 2 * k]
            TT(dve, qr, pr, pr, ALU.mult, R_, W_)
            TT(dve, t3, pi_, pi_, ALU.mult, R_, W_)
            TT(dve, qr, qr, t3, ALU.subtract, R_, W_)
            TT(dve, qi, pr, pi_, ALU.mult, R_, W_)
            TS(dve, qi, qi, 2.0, None, ALU.mult, None, R_, W_)
        TS(dve, s5p[:, :, 3], s5p[:, :, 9], -1.0, None, ALU.mult, None, R_, W_)
        Cc = arA[:, 0:4096].rearrange("p (q t) -> p q t", t=T)
        Ss = arA[:, 4096:8192].rearrange("p (q t) -> p q t", t=T)
        tg = arB[:, 0:2048].rearrange("p (q t) -> p q t", t=256)
        RW = ["arA", "arB", "s5p"]
        for qg in range(4):
            q0 = 8 * qg
            K.op(dve, lambda e: e.memset(Cc[:, :, 0:1], 1.0), R=RW, W=RW[:2])
            K.op(dve, lambda e: e.memset(Ss[:, :, 0:1], 0.0), R=RW, W=RW[:2])
            for k in range(9):
                n = 1 << k
                ur = s5p[:, q0:q0 + 8, 8 + 2 * k:9 + 2 * k].to_broadcast([128, 8, n])
                ui = s5p[:, q0:q0 + 8, 9 + 2 * k:10 + 2 * k].to_broadcast([128, 8, n])
                TT(dve, Cc[:, :, n:2 * n], Cc[:, :, 0:n], ur, ALU.mult, RW, RW[:2])
                TT(dve, tg[:, :, 0:n], Ss[:, :, 0:n], ui, ALU.mult, RW, RW[:2])
                TT(dve, Cc[:, :, n:2 * n], Cc[:, :, n:2 * n], tg[:, :, 0:n], ALU.subtract, RW, RW[:2])
                TT(dve, Ss[:, :, n:2 * n], Cc[:, :, 0:n], ui, ALU.mult, RW, RW[:2])
                TT(dve, tg[:, :, 0:n], Ss[:, :, 0:n], ur, ALU.mult, RW, RW[:2])
                TT(dve, Ss[:, :, n:2 * n], Ss[:, :, n:2 * n], tg[:, :, 0:n], ALU.add, RW, RW[:2])
            dstv = tabd[l][d][q0:q0 + 8].rearrange("q p a t -> p q a t")
            K.dma(dstv[:, :, 0, :], Cc, R=RW[:2], W=[("tabd",)])
            K.dma(dstv[:, :, 1, :], Ss, R=RW[:2], W=[("tabd",)])
        K.op(dve, lambda e: e.memset(Bm[:], 0.0), W=["Bm"])
        K.op(dve, lambda e: e.memset(Cm[:], 0.0), W=["Cm"])
        for j in range(8):
            K.op(dve, lambda e: e.memset(stgA[:], 0.0), W=["stgA"])
            K.op(dve, lambda e: e.memset(stgB[:], 0.0), W=["stgB"])
            for qq in range(4):
                q = 4 * j + qq
                base = 32 * qq
                K.dma(stg2[:, 0:16], s5_b_re[l, d, 2 * q:2 * q + 2].rearrange("g p c -> (g p) c"), W=["stg2"])
                K.dma(stg2[:, 16:32], s5_b_im[l, d, 2 * q:2 * q + 2].rearrange("g p c -> (g p) c"), W=["stg2"])
                br, bi = stg2[:, 0:16], stg2[:, 16:32]
                o1, o2 = stg2[:, 32:48], stg2[:, 48:64]
                grq, giq = s5p[:, q, 29:30], s5p[:, q, 30:31]
                Rq, Wq = ["stg2", "s5p"], ["stg2"]
                TS(dve, o1, bi, giq, -1.0, ALU.mult, ALU.mult, Rq, Wq)
                STT(dve, o1, br, grq, o1, ALU.mult, ALU.add, Rq, Wq)
                TS(dve, o2, br, giq, None, ALU.mult, None, Rq, Wq)
                STT(dve, o2, bi, grq, o2, ALU.mult, ALU.add, Rq, Wq)
                for o, st, sn_ in ((o1, stgA, "stgA"), (o2, stgB, "stgB")):
                    K.op(dve, lambda e: e.tensor_copy(st[0:64, base:base + 16], o[0:64, :]), R=["stg2"], W=[sn_])
                    K.op(dve, lambda e: e.tensor_copy(st[64:128, base + 16:base + 32], o[64:128, :]), R=["stg2"], W=[sn_])
            for ri, (st, sn_) in enumerate(((stgA, "stgA"), (stgB, "stgB"))):
                TR(psx[:, 0:128], st[:, :], ident, R=[sn_, "cs"], W=["psx"])
                K.op(dve, lambda e: e.tensor_copy(Bm[:, ri, j, :], psx[:, 0:128]), R=["psx"], W=["Bm"])
                K.op(dve, lambda e: e.tensor_copy(Bm[:, ri, 8 + j, :], psx[:, 0:128]), R=["psx"], W=["Bm"])
                K.op(dve, lambda e: e.memset(Bm[64:96, ri, 8 + j, :], 0.0), W=["Bm"])
            for ri, csrc in enumerate((s5_c_re, s5_c_im)):
                K.op(dve, lambda e: e.memset(stgA[:], 0.0), W=["stgA"])
                for qq in range(4):
                    q = 4 * j + qq
                    K.dma(stgA[32 * qq:32 * qq + 16, 0:64], csrc[l, d, 2 * q], W=["stgA"])
                    K.dma(stgA[32 * qq + 16:32 * qq + 32, 64:128], csrc[l, d, 2 * q + 1], W=["stgA"])
                TR(psx[:, 0:128], stgA[:, :], ident, R=["stgA", "cs"], W=["psx"])
                for qq in range(4):
                    co = 32 if qq == 3 else 0
                    cdst = Cm[:, ri, 4 * j + qq, co:co + 32]
                    TS(dve, cdst, psx[:, 32 * qq:32 * qq + 32], 1.0 if ri == 0 else -1.0, None, ALU.mult, None, R=["psx"], W=["Cm"])

    def load_norm(src, seg, r0, l, ext, hm=0):
        K.dma(xt, src[HALO + r0:HALO + r0 + T, :].rearrange("(s p) d -> p s d", p=128), W=["xt"])
        nsub = 4
        if ext:
            K.dma(xh[0:HALO, :], src[r0:r0 + HALO, :], W=["xh"])
            K.dma(xh[HALO:2 * HALO, :], src[HALO + r0 + T:HALO + r0 + T + HALO, :], W=["xh"])
            nsub = 5
        for s in range(nsub):
            np_ = 128 if s < 4 else 32
            xin_ = xt[:, s, :] if s < 4 else xh[:, :]
            rs = ["xt"] if s < 4 else ["xh"]
            A(act, junk[0:np_, :], xin_, AF.Square, R=rs, W=["junk", "ss"], accum=ss[0:np_, s:s + 1])
        TS(dve, rstd[:, 0:nsub], ss[:, 0:nsub], 1.0 / D, EPS, ALU.mult, ALU.add, R=["ss"], W=["rstd"])
        A(act, rstd[:, 0:nsub], rstd[:, 0:nsub], AF.Ln, R=["rstd"], W=["rstd"])
        A(act, rstd[:, 0:nsub], rstd[:, 0:nsub], AF.Exp, R=["rstd"], W=["rstd"], scale=-0.5)
        if ext and hm != 0:
            TT(dve, rstd[0:32, 4:5], rstd[0:32, 4:5], hmk[0:32, hm:hm + 1], ALU.mult, R=["rstd", "hmk"], W=["rstd"])
        for s in range(nsub):
            np_ = 128 if s < 4 else 32
            xin_ = xt[:, s, :] if s < 4 else xh[:, :]
            rs = ["xt"] if s < 4 else ["xh"]
            A(act, hb[0:np_, s, :], xin_, AF.Copy, R=rs + ["rstd"], W=["hb"], scale=rstd[0:np_, s:s + 1])
        for dc in range(16):
            half = dc % 2
            pb = psb[:, 0:T] if half == 0 else ps[5][:, 0:256].bitcast(BF16)
            pbn = "psb" if half == 0 else ("ps", 5)
            for s in range(4):
                TR(pb[:, s * 128:(s + 1) * 128], hb[:, s, dc * 128:(dc + 1) * 128], identb[:], R=["hb", "identb"], W=[pbn])
            A(act, hT[:, dc, HALO:HALO + T], pb, AF.Copy, R=[pbn, "vecT"], W=["hT"], scale=vecT[:, V_NG + dc:V_NG + dc + 1])
            if ext:
                TR(psx[:, 0:32].bitcast(BF16)[:, 0:32], hb[0:32, 4, dc * 128:(dc + 1) * 128], identb[0:32, 0:32], R=["hb", "identb"], W=["psx"])
                pxb = psx[:, 0:32].bitcast(BF16)
                A(act, hT[:, dc, 0:HALO], pxb[:, 0:HALO], AF.Copy, R=["psx", "vecT"], W=["hT"], scale=vecT[:, V_NG + dc:V_NG + dc + 1])
                A(act, hT[:, dc, HALO + T:TE], pxb[:, HALO:2 * HALO], AF.Copy, R=["psx", "vecT"], W=["hT"], scale=vecT[:, V_NG + dc:V_NG + dc + 1])

    def hT_k(kc):
        return hT[:, kc, :]

    def reset_states():
        K.op(dve, lambda e: e.memset(S32[:], 0.0), W=["S32"])
        K.op(dve, lambda e: e.memset(Sbf[:], 0.0), W=["Sbf"])
        K.op(dve, lambda e: e.memset(hc[:], 0.0), W=["hc"])

    def scale_states():
        TS(dve, S32[:].rearrange("p h k -> p (h k)"), S32[:].rearrange("p h k -> p (h k)"), cfl[:, 0:1], None, ALU.mult, None, R=["S32", "cfl"], W=["S32"])
        A(act, Sbf[:].rearrange("p h k -> p (h k)"), S32[:].rearrange("p h k -> p (h k)"), AF.Copy, R=["S32"], W=["Sbf"])
        TS(dve, hc[:].rearrange("p a q -> p (a q)"), hc[:].rearrange("p a q -> p (a q)"), cfl[:, 0:1], None, ALU.mult, None, R=["hc", "cfl"], W=["hc"])

    def soft_left(ti, nt):
        return segt is not None and ti > 0 and ti % segt == 0

    def soft_right(ti, nt):
        return segt is not None and ti < nt - 1 and ti % segt == segt - 1

    def hg_head(hd, d):
        q_, sg, f_, lf, b_, e1, e2, k_ = (sc[i][:, 0:T] for i in range(8))
        lcol = hd if d == 0 else 8 + hd
        A(act, f_, sg, AF.Identity, R=["sc1", "omlb", "lbv"], W=["sc2"], scale=omlb[:, lcol:lcol + 1], bias=lbv[:, lcol:lcol + 1])
        A(act, lf, f_, AF.Ln, R=["sc2"], W=["sc3"])
        A(act, k_, sg, AF.Identity, R=["sc1", "omlb"], W=["sc7"], scale=nomlb[:, lcol:lcol + 1], bias=omlb[:, lcol:lcol + 1])
        yield
        K.op(dve, lambda e: e.tensor_tensor_scan(b_, rmask, lf, 0.0, ALU.mult, ALU.add), R=["cs", "sc3"], W=["sc4"])
        yield
        bend = sc[4][:, 0:T].rearrange("p (c t) -> p c t", t=CH)[:, :, CH - 1]
        A(act, sc[8][:, 0:8], bend, AF.Exp, R=["sc4"], W=["sc8"])
        if d == 0:
            A(act, e1, b_, AF.Exp, R=["sc4"], W=["sc5"])
            A(act, e2, b_, AF.Exp, R=["sc4"], W=["sc6"], scale=-1.0)
        else:
            TT(dve, b_, b_, lf, ALU.subtract, R=["sc4", "sc3"], W=["sc4"])
            A(act, e1, b_, AF.Exp, R=["sc4"], W=["sc5"], scale=-1.0)
            A(act, e2, b_, AF.Exp, R=["sc4"], W=["sc6"])
        yield
        TT(dve, bq[:], q_, e1, ALU.mult, R=["sc0", "sc5"], W=["bq"])
        TT(dve, bk[:], k_, e2, ALU.mult, R=["sc7", "sc6"], W=["bk"])
        for s in range(4):
            TR(ps5b[:, s * 128:(s + 1) * 128], bk[:, s * 128:(s + 1) * 128], identb[:], R=["bk", "identb"], W=[("ps", 5)])
        K.op(dve, lambda e: e.tensor_copy(kT[:], ps5b.rearrange("p (s k) -> p s k", k=128)), R=[("ps", 5)], W=["kT"])
        yield
        for c in range(8):
            h64 = (c % 2) * 64
            MM(ps[5][h64:h64 + 64, (c // 2) * CH:(c // 2 + 1) * CH], bk[:, c * CH:(c + 1) * CH], bq[:, c * CH:(c + 1) * CH],
               True, True, R=["bk", "bq"], W=[("ps", 5)])
        yield
        msk = maskF if d == 0 else maskB
        TT(dve, At[:], ps[5][:, 0:256].rearrange("p (a t) -> p a t", t=CH), msk.rearrange("p (a t) -> p a t", t=CH),
           ALU.mult, R=[("ps", 5), "cs"], W=["At"])
        order = range(8) if d == 0 else range(7, -1, -1)
        for c in order:
            h64 = (c % 2) * 64
            et = sc[8][:, c:c + 1]
            if d == 1:
                TS(dve, S32[:, hd, :], S32[:, hd, :], et, None, ALU.mult, None, R=["S32", "sc8"], W=["S32"])
                A(act, Sbf[:, hd, :], S32[:, hd, :], AF.Copy, R=["S32"], W=["Sbf"])
            oc = ps[4][:, c * CH:(c + 1) * CH]
            MM(oc, vT[h64:h64 + 64, hd, c // 2, :], At[h64:h64 + 64, c // 2, :], True, False, R=["vT", "At"], W=[("ps", 4)])
            MM(oc, Sbf[:, hd, :], bq[:, c * CH:(c + 1) * CH], False, True, R=["Sbf", "bq"], W=[("ps", 4)])
            MM(psx[:, 0:128], kT[h64:h64 + 64, c // 2, :], vT[h64:h64 + 64, hd, c // 2, :], True, True, R=["kT", "vT"], W=["psx"])
            yield
            if d == 0:
                TT(dve, Tm[:], psx[:, 0:128], S32[:, hd, :], ALU.add, R=["psx", "S32"], W=["Tm"])
                TS(dve, S32[:, hd, :], Tm[:], et, None, ALU.mult, None, R=["Tm", "sc8"], W=["S32"])
                A(act, Sbf[:, hd, :], S32[:, hd, :], AF.Copy, R=["S32"], W=["Sbf"])
            else:
                TT(dve, S32[:, hd, :], psx[:, 0:128], S32[:, hd, :], ALU.add, R=["psx", "S32"], W=["S32"])
            yield

    def run_streams(gens, weights):
        alive = [True] * len(gens)
        while any(alive):
            for gi, g in enumerate(gens):
                if not alive[gi]:
                    continue
                for _ in range(weights[gi]):
                    try:
                        next(g)
                    except StopIteration:
                        alive[gi] = False
                        break

    def s5_dir(l, d):
        ETc, ETs = s5p[:, :, 26], s5p[:, :, 27]
        R_, W_ = ["s5p", "hc", "cin", "tmp32"], ["cin", "tmp32"]
        TT(dve, cin[:, 0, :], ETc, hc[:, 0, :], ALU.mult, R_, W_)
        TT(dve, tmp32[:, 0, :], ETs, hc[:, 1, :], ALU.mult, R_, W_)
        TT(dve, cin[:, 0, :], cin[:, 0, :], tmp32[:, 0, :], ALU.subtract, R_, W_)
        TT(dve, cin[:, 1, :], ETc, hc[:, 1, :], ALU.mult, R_, W_)
        TT(dve, tmp32[:, 0, :], ETs, hc[:, 0, :], ALU.mult, R_, W_)
        TT(dve, cin[:, 1, :], cin[:, 1, :], tmp32[:, 0, :], ALU.add, R_, W_)
        bA, bB, bC, bD = s5buf
        psI = psb[:, :].bitcast(F32)
        K.dma(tbuf[0], tabd[l][d][0], R=[("tabd",)], W=["arB", ("tb", 0)])
        pending = []
        for q in range(32):
            j, base = q // 4, 32 * (q % 4)
            tb = tbuf[q % 2]
            tn = ("tb", q % 2)
            if q + 1 < 32:
                K.dma(tbuf[(q + 1) % 2], tabd[l][d][q + 1], R=[("tabd",)], W=[("tb", (q + 1) % 2)])
            hsel = q % 2
            hbq = Hb[hsel]
            hbn = ("Hb", hsel)
            for ri, (pt, pn) in enumerate(((ps[2][:, :], ("ps", 2)), (psI, "psb"))):
                if base == 96:
                    MM(pt, Bm[64:128, ri, 8 + j, :], ubf[64:128, j, :], True, True, R=["Bm", "ubf"], W=[pn])
                else:
                    MM(pt, Bm[base:base + 32, ri, j, :], ubf[base:base + 32, j, :], True, True, R=["Bm", "ubf"], W=[pn])
            if d == 0:
                vr, vi = ps[2][:, :], psI
                hro, hio = hbq[0][:], hbq[1][:]
            else:
                vr, vi = ps[2][:, ::-1], psI[:, ::-1]
                hro, hio = hbq[0][:, ::-1], hbq[1][:, ::-1]
            cc, ssn = tb[:, 0, :], tb[:, 1, :]
            TT(dve, bA, cc, vr, ALU.mult, R=[tn, ("ps", 2)], W=["s5A"])
            TT(dve, bB, ssn, vi, ALU.mult, R=[tn, "psb"], W=["s5B"])
            TT(dve, bC, cc, vi, ALU.mult, R=[tn, "psb"], W=["s5C"])
            TT(dve, bD, ssn, vr, ALU.mult, R=[tn, ("ps", 2)], W=["s5D"])
            TT(dve, bA, bA, bB, ALU.add, R=["s5A", "s5B"], W=["s5A"])
            TT(dve, bC, bC, bD, ALU.subtract, R=["s5C", "s5D"], W=["s5C"])
            yield
            rho_bc = s5p[:, q, 0:1].to_broadcast([128, T])
            K.op(dve, lambda e: e.tensor_tensor_scan(bB, rho_bc, bA, cin[:, 0, q:q + 1], ALU.mult, ALU.add), R=["s5A", "s5p", "cin"], W=["s5B"])
            K.op(dve, lambda e: e.tensor_tensor_scan(bD, rho_bc, bC, cin[:, 1, q:q + 1], ALU.mult, ALU.add), R=["s5C", "s5p", "cin"], W=["s5D"])
            yield
            TT(dve, bA, cc, bB, ALU.mult, R=[tn, "s5B"], W=["s5A"])
            TT(dve, bC, ssn, bD, ALU.mult, R=[tn, "s5D"], W=["s5C"])
            TT(dve, hro, bA, bC, ALU.subtract, R=["s5A", "s5C", "arB"], W=[hbn])
            TT(dve, bA, cc, bD, ALU.mult, R=[tn, "s5D"], W=["s5A"])
            TT(dve, bC, ssn, bB, ALU.mult, R=[tn, "s5B"], W=["s5C"])
            TT(dve, hio, bA, bC, ALU.add, R=["s5A", "s5C", "arB"], W=[hbn])
            A(act, hc[:, 0, q:q + 1], bB[:, T - 1:T], AF.Copy, R=["s5B", "cin"], W=["hc"])
            A(act, hc[:, 1, q:q + 1], bD[:, T - 1:T], AF.Copy, R=["s5D", "cin"], W=["hc"])

            def cmm(q=q, j=j, base=base, hbq=hbq, hbn=hbn):
                if base < 64:
                    MM(ps[3][base:base + 32, :], Cm[:, 0, q, 0:32], hbq[0][:], True, False, R=["Cm", hbn], W=[("psY",)])
                    MM(ps[3][base:base + 32, :], Cm[:, 1, q, 0:32], hbq[1][:], False, True, R=["Cm", hbn], W=[("psY",)])
                else:
                    MM(ps[3][64:128, :], Cm[:, 0, q, :], hbq[0][:], base == 64, False, R=["Cm", hbn], W=[("psY",)])
                    MM(ps[3][64:128, :], Cm[:, 1, q, :], hbq[1][:], False, base == 96, R=["Cm", hbn], W=[("psY",)])
                if q % 4 == 3:
                    TT(dve, ysf[:, j, :], ysf[:, j, :], ps[3][:, :], ALU.add, R=[("psY",), "ysf"], W=["ysf"])
            pending.append(cmm)
            if len(pending) > 1:
                pending.pop(0)()
            yield
        while pending:
            pending.pop(0)()
        yield

    def spill(dst, src_ap, R):
        K.dma(dst, src_ap, R=R, W=[("spill",)])

    def passA(l, seg, ti):
        r0 = ti * T
        src = xin[seg] if l == 0 else x1[seg]
        load_norm(src, seg, r0, l, False)
        if ti == 0:
            reset_states()
        elif soft_left(ti, segs[seg] // T):
            scale_states()
        def streamH():
            for hd in range(8):
                def ev(i, m, gi, p, pslot, hd=hd):
                    if i == 0:
                        A(act, sc[0][:, 0:T], p, AF.Copy, R=[pslot], W=["sc0"])
                        spill(sp_q[seg][hd, :, r0:r0 + T], sc[0][:, 0:T], R=["sc0"])
                    elif i == 1:
                        A(act, sc[1][:, 0:T], p, AF.Sigmoid, R=[pslot], W=["sc1"])
                    else:
                        A(act, bvf[:], p, AF.Copy, R=[pslot], W=["bvf"])
                        for s in range(4):
                            TR(ps5b[:, s * 128:(s + 1) * 128], bvf[:, s * 128:(s + 1) * 128], identb[:], R=["bvf", "identb"], W=[("ps", 5)])
                        K.op(dve, lambda e: e.tensor_copy(vT[:, hd, :, :], ps5b.rearrange("p (s k) -> p s k", k=128)), R=[("ps", 5)], W=["vT"])
                        spill(sp_v[seg][r0:r0 + T, hd * 128:(hd + 1) * 128].rearrange("(s p) v -> p s v", p=128), vT[:, hd, :, :], R=["vT"])
                linear(wb_in[l], 16, [C_Q + hd, C_FF + hd, C_I + hd], hT_k, [(HALO, T)], ev, ["hT"])
                yield
                yield from hg_head(hd, 0)
                A(act, sc[9][:, 0:T], ps[4][:, :], AF.Copy, R=[("ps", 4)], W=["sc9"])
                spill(sp_o[seg][hd, :, r0:r0 + T], sc[9][:, 0:T], R=["sc9"])
                yield

        def streamS():
            def evu(i, m, gi, p, pslot):
                A(act, ubf[:, i, :], p, AF.Copy, R=[pslot], W=["ubf"])
                A(act, ysf[:, i, :], p, AF.Copy, R=[pslot, "vecT"], W=["ysf"], scale=vecT[:, V_SD + i:V_SD + i + 1])
            linear(wb_in[l], 16, [C_DIN + j for j in range(8)], hT_k, [(HALO, T)], evu, ["hT"])
            spill(sp_u[seg][:, :, r0:r0 + T].rearrange("j p t -> p j t"), ubf, R=["ubf"])
            yield
            yield from s5_dir(l, 0)
            spill(sp_y[seg][:, :, r0:r0 + T].rearrange("j p t -> p j t"), ysf, R=["ysf"])
        run_streams([streamS(), streamH()], [SW_S, SW_H])

    def rstd_from(ps_ms, out_sc, R, W):
        TS(dve, out_sc, ps_ms, EPS, None, ALU.add, None, R=R, W=W)
        A(act, out_sc, out_sc, AF.Ln, R=W, W=W)
        A(act, out_sc, out_sc, AF.Exp, R=W, W=W, scale=-0.5)

    def passB(l, seg, ti, ntiles, last_layer):
        r0 = ti * T
        src = xin[seg] if l == 0 else x1[seg]
        load_norm(src, seg, r0, l, True, hm=(1 if soft_left(ti, ntiles) else (2 if soft_right(ti, ntiles) else 0)))
        if ti == ntiles - 1:
            reset_states()
        elif soft_right(ti, ntiles):
            scale_states()
        def streamD():
            K.dma(ubf, sp_u[seg][:, :, r0:r0 + T].rearrange("j p t -> p j t"), R=[("spill",)], W=["ubf"])
            K.dma(ysf, sp_y[seg][:, :, r0:r0 + T].rearrange("j p t -> p j t"), R=[("spill",)], W=["ysf"])
            yield
            yield from s5_dir(l, 1)
            for j in range(8):
                A(act, ysf[:, j, :], ysf[:, j, :], AF.Gelu, R=["ysf"], W=["ysf"])
                K.op(dve, lambda e: e.tensor_copy(zbf[:, j, :], ysf[:, j, :]), R=["ysf"], W=["zbf"])

            def evglu(i, m, gi, p, pslot):
                A(act, s5buf[0], p, AF.Sigmoid, R=[pslot, "vecT"], W=["arB"], bias=vecT[:, V_GB + i:V_GB + i + 1])
                TT(dve, ysf[:, i, :], ysf[:, i, :], s5buf[0], ALU.mult, R=["ysf", "arB"], W=["ysf"])
            linear(wb_glu[l], 8, list(range(8)), lambda kc: zbf[:, kc, :], [(0, T)], evglu, ["zbf"])

            def evdg(i, m, gi, p, pslot):
                A(act, s5buf[0], p, AF.Silu, R=[pslot], W=["arB"])
                TT(dve, yd[:, i, :], ysf[:, i, :], s5buf[0], ALU.mult, R=["ysf", "arB"], W=["yd"])
            linear(wb_in[l], 16, [C_DGATE + j for j in range(8)], hT_k, [(HALO, T)], evdg, ["hT"])
            yield

        def streamC():
            for hd in range(8):
                K.dma(sc[0][:, 0:T], sp_q[seg][hd, :, r0:r0 + T], R=[("spill",)], W=["sc0"])
                K.dma(vT[:, hd, :, :], sp_v[seg][r0:r0 + T, hd * 128:(hd + 1) * 128].rearrange("(s p) v -> p s v", p=128), R=[("spill",)], W=["vT"])
                K.dma(sc[9][:, 0:T], sp_o[seg][hd, :, r0:r0 + T], R=[("spill",)], W=["sc9"])

                def ev(i, m, gi, p, pslot):
                    A(act, sc[1][:, 0:T], p, AF.Sigmoid, R=[pslot], W=["sc1"])
                linear(wb_in[l], 16, [C_FB + hd], hT_k, [(HALO, T)], ev, ["hT"])
                yield
                yield from hg_head(hd, 1)
                TT(dve, sc[9][:, 0:T], ps[4][:, :], sc[9][:, 0:T], ALU.add, R=[("ps", 4), "sc9"], W=["sc9"])
                A(act, bq[:], sc[9][:, 0:T], AF.Square, R=["sc9"], W=["bq"])
                MM(ps[5][:, :], onesH[:], bq[:], True, True, R=["onesH", "bq"], W=[("ps", 5)])
                rstd_from(ps[5][:, :], sc[2][:, 0:T], R=[("ps", 5)], W=["sc2"])
                STT(dve, sc[3][:, 0:T], sc[9][:, 0:T], vecT[:, V_HG + hd:V_HG + hd + 1], sc[2][:, 0:T], ALU.mult, ALU.mult,
                    R=["sc9", "vecT", "sc2"], W=["sc3"])

                def evg(i, m, gi, p, pslot, hd=hd):
                    A(act, sc[5][:, 0:T], p, AF.Silu, R=[pslot], W=["sc5"])
                    TT(dve, yc[:, hd, :], sc[3][:, 0:T], sc[5][:, 0:T], ALU.mult, R=["sc3", "sc5"], W=["yc"])
                linear(wb_in[l], 16, [C_CGATE + hd], hT_k, [(HALO, T)], evg, ["hT"])
                yield
            yield

        run_streams([streamD(), streamC()], [SW_S, SW_H])
        CG = [(0, T), (T, 2 * HALO)]
        for c in range(8):
            def eva(i, m, gi, p, pslot, c=c):
                c0, ncol = CG[gi]
                if i == 0:
                    A(act, sc[0][:, c0:c0 + ncol], p, AF.Copy, R=[pslot], W=["sc0"])
                else:
                    A(act, sc[1][:, c0:c0 + ncol], p, AF.Sigmoid, R=[pslot], W=["sc1"])
                    TT(dve, ue[:, c, c0:c0 + ncol], sc[0][:, c0:c0 + ncol], sc[1][:, c0:c0 + ncol], ALU.mult, R=["sc0", "sc1"], W=["ue"])
            linear(wb_in[l], 16, [C_AVAL + c, C_AGLU + c], hT_k, CG, eva, ["hT"])
        for c in range(8):
            b = c % 2
            K.dma(cbuf[b][:], cdiag[l][c], R=[("cdiag",)], W=[("cbuf", b)])
            p, pslot = nps()
            for j in range(31):
                MM(p[:, :], cbuf[b][:, j, :], ue[:, c, j + 1:j + 1 + T], j == 0, j == 30, R=[("cbuf", b), "ue"], W=[pslot])
            A(act, ysf[:, c, :], p[:, :], AF.Identity, R=[pslot, "vecT"], W=["ysf"], bias=vecT[:, V_CB + c:V_CB + c + 1])
            K.op(dve, lambda e: e.tensor_copy(zbf[:, c, :], ysf[:, c, :]), R=["ysf"], W=["zbf"])
        for c in range(8):
            MM(ps[4][:, :], onesW[:], zbf[:, c, :], c == 0, c == 7, R=["onesW", "zbf"], W=[("ps", 4)])
        for c in range(8):
            A(act, bq[:], ysf[:, c, :], AF.Square, R=["ysf"], W=["bq"])
            MM(ps[5][:, :], onesW[:], bq[:], c == 0, c == 7, R=["onesW", "bq"], W=[("ps", 5)])
        A(act, sc[0][:, 0:T], ps[4][:, :], AF.Copy, R=[("ps", 4)], W=["sc0"])
        TT(dve, sc[1][:, 0:T], sc[0][:, 0:T], sc[0][:, 0:T], ALU.mult, R=["sc0"], W=["sc1"])
        TT(dve, sc[1][:, 0:T], ps[5][:, :], sc[1][:, 0:T], ALU.subtract, R=[("ps", 5), "sc1"], W=["sc1"])
        rstd_from(sc[1][:, 0:T], sc[2][:, 0:T], R=["sc1"], W=["sc2"])
        for c in range(8):
            TT(dve, sc[3][:, 0:T], ysf[:, c, :], sc[0][:, 0:T], ALU.subtract, R=["ysf", "sc0"], W=["sc3"])
            STT(dve, sc[3][:, 0:T], sc[3][:, 0:T], vecT[:, V_LNG + c:V_LNG + c + 1], sc[2][:, 0:T], ALU.mult, ALU.mult,
                R=["sc3", "vecT", "sc2"], W=["sc3"])
            A(act, sc[4][:, 0:T], sc[3][:, 0:T], AF.Silu, R=["sc3", "vecT"], W=["sc4"], bias=vecT[:, V_LNB + c:V_LNB + c + 1])

            def evag(i, m, gi, p, pslot, c=c):
                A(act, sc[5][:, 0:T], p, AF.Silu, R=[pslot], W=["sc5"])
                TT(dve, ya[:, c, :], sc[4][:, 0:T], sc[5][:, 0:T], ALU.mult, R=["sc4", "sc5"], W=["ya"])
            linear(wb_in[l], 16, [C_AGATE + c], hT_k, [(HALO, T)], evag, ["hT"])
        def evb(i, m, gi, p, pslot):
            c0, ncol = CG[gi]
            A(act, ue[:, i, c0:c0 + ncol], p, AF.Copy, R=[pslot], W=["ue"])
        linear(wb_in[l], 16, [C_BIN + c for c in range(8)], hT_k, CG, evb, ["hT"])
        for g in range(4):
            K.dma(rcb, rcin[seg][g, r0:r0 + T].partition_broadcast(128), W=["sc8"])
            for mm in range(2):
                c = 2 * g + mm

                def evz(i, m, gi, p, pslot):
                    c0, ncol = CG[gi]
                    A(act, sc[0][:, c0:c0 + ncol], p, AF.Copy, R=[pslot], W=["sc0"])
                linear(wb_pool[l][g], 2, [mm], lambda kc, g=g: ue[:, 2 * g + kc, :], CG, evz, ["ue"])
                z = sc[0]
                a, bb = sc[2], sc[3]
                TT(dve, a[:, 1:TE], z[:, 0:TE - 1], z[:, 1:TE], ALU.add, R=["sc0"], W=["sc2"])
                cur, lo, hi = a, 1, TE
                if g >= 1:
                    TT(dve, bb[:, lo + 1:hi - 1], cur[:, lo:hi - 2], cur[:, lo + 2:hi], ALU.add, R=["sc2"], W=["sc3"])
                    cur, lo, hi = bb, lo + 1, hi - 1
                if g >= 2:
                    TT(dve, a[:, lo + 2:hi - 2], cur[:, lo:hi - 4], cur[:, lo + 4:hi], ALU.add, R=["sc3"], W=["sc2"])
                    cur, lo, hi = a, lo + 2, hi - 2
                if g >= 3:
                    TT(dve, bb[:, lo + 4:hi - 4], cur[:, lo:hi - 8], cur[:, lo + 8:hi], ALU.add, R=["sc2"], W=["sc3"])
                    cur, lo, hi = bb, lo + 4, hi - 4
                cn_ = "sc2" if cur is a else "sc3"
                TT(dve, sc[4][:, 0:T], cur[:, HALO:HALO + T], rcb, ALU.mult, R=[cn_, "sc8"], W=["sc4"])
                TT(dve, sc[4][:, 0:T], sc[4][:, 0:T], z[:, HALO:HALO + T], ALU.subtract, R=["sc4", "sc0"], W=["sc4"])

                def evbg(i, m, gi, p, pslot, c=c):
                    A(act, sc[5][:, 0:T], p, AF.Silu, R=[pslot], W=["sc5"])
                    STT(dve, yb[:, c, :], sc[4][:, 0:T], vecT[:, V_PS + c:V_PS + c + 1], sc[5][:, 0:T], ALU.mult, ALU.mult,
                        R=["sc4", "vecT", "sc5"], W=["yb"])
                linear(wb_in[l], 16, [C_BGATE + c], hT_k, [(HALO, T)], evbg, ["hT"])
        ys = [ya, yb, yc, yd]
        yn = ["ya", "yb", "yc", "yd"]
        for dch in range(16):
            for i in range(4):
                def evr(ii, m, gi, p, pslot, i=i):
                    A(act, sc[5][:, 0:T], p, AF.Sigmoid, R=[pslot], W=["sc5"])
                linear(wb_in[l], 16, [C_R + i * 16 + dch], hT_k, [(HALO, T)], evr, ["hT"])

                def evo(ii, m, gi, p, pslot, i=i):
                    if i == 0:
                        TT(dve, sc[6][:, 0:T], p, sc[5][:, 0:T], ALU.mult, R=[pslot, "sc5"], W=["sc6"])
                    else:
                        TT(dve, sc[7][:, 0:T], p, sc[5][:, 0:T], ALU.mult, R=[pslot, "sc5"], W=["sc7"])
                        TT(dve, sc[6][:, 0:T], sc[6][:, 0:T], sc[7][:, 0:T], ALU.add, R=["sc6", "sc7"], W=["sc6"])
                linear(wb_ao[l][i], 8, [dch], lambda kc, i=i: ys[i][:, kc, :], [(0, T)], evo, [yn[i]])
            K.op(dve, lambda e: e.tensor_copy(mT[:, dch, :], sc[6][:, 0:T]), R=["sc6"], W=["mT"])
        def evout(i, m, gi, p, pslot):
            A(act, sc[0][:, 0:T], p, AF.Copy, R=[pslot], W=["sc0"])
            for s in range(4):
                TR(psx[:, 0:128], sc[0][:, s * 128:(s + 1) * 128], ident, R=["sc0", "cs"], W=["psx"])
                TT(dve, xt[:, s, i * 128:(i + 1) * 128], xt[:, s, i * 128:(i + 1) * 128], psx[:, 0:128], ALU.add, R=["psx", "xt"], W=["xt"])
        K.dma(xt, src[HALO + r0:HALO + r0 + T, :].rearrange("(s p) d -> p s d", p=128), W=["xt"])
        linear(wb_out[l], 16, list(range(16)), lambda kc: mT[:, kc, :], [(0, T)], evout, ["mT"])
        if not last_layer:
            K.dma(x1[seg][HALO + r0:HALO + r0 + T, :].rearrange("(s p) d -> p s d", p=128), xt, R=["xt"], W=[("x1", seg)])
        else:
            for s in range(4):
                A(act, junk[:, :], xt[:, s, :], AF.Square, R=["xt"], W=["junk", "ss"], accum=ss[:, s:s + 1])
            TS(dve, rstd[:, 0:4], ss[:, 0:4], 1.0 / D, EPS, ALU.mult, ALU.add, R=["ss"], W=["rstd"])
            A(act, rstd[:, 0:4], rstd[:, 0:4], AF.Ln, R=["rstd"], W=["rstd"])
            A(act, rstd[:, 0:4], rstd[:, 0:4], AF.Exp, R=["rstd"], W=["rstd"], scale=-0.5)
            K.dma(fgb, final_g.partition_broadcast(128), W=["fgb"])
            for s in range(4):
                STT(dve, xt[:, s, :], xt[:, s, :], rstd[:, s:s + 1], fgb, ALU.mult, ALU.mult, R=["xt", "rstd", "fgb"], W=["xt"])
            K.dma(yout[seg][r0:r0 + T, :].rearrange("(s p) d -> p s d", p=128), xt, R=["xt"], W=[("yout", seg)])

    for l in range(depth):
        prep_layer(l)
        prep_s5(l, 0)
        for seg in range(nseg):
            nt = segs[seg] // T
            for ti in range(nt):
                passA(l, seg, ti)
        prep_s5(l, 1)
        for seg in range(nseg):
            nt = segs[seg] // T
            for ti in range(nt - 1, -1, -1):
                passB(l, seg, ti, nt, l == depth - 1)
    K.finish()
    return nc


def _consts():
    c = np.zeros((128, 128 + 256 + 256 + 512), np.float32)
    c[:, 0:128] = np.eye(128, dtype=np.float32)
    s = np.arange(128) % 64
    t = np.arange(256) % 64
    c[:, 128:384] = (s[:, None] <= t[None, :]).astype(np.float32)
    c[:, 384:640] = (s[:, None] >= t[None, :]).astype(np.float32)
    rm = np.ones(512, np.float32)
    rm[::64] = 0.0
    c[:, 640:1152] = rm[None, :]
    return c


def _rc(L):
    t = np.arange(L)
    out = np.zeros((4, L), np.float32)
    for g, win in enumerate((2, 4, 8, 16)):
        left = win // 2
        right = win - 1 - left
        lo = np.maximum(t - left, 0)
        hi = np.minimum(t + right, L - 1) + 1
        out[g] = 1.0 / (hi - lo).astype(np.float32)
    return out


_WNAMES = ["norm_g", "w_in", "conv_w", "conv_b", "conv_ln_g", "conv_ln_b", "w_a_out", "pool_w", "pool_scale", "w_b_out",
           "hg_lb", "hg_norm_g", "w_c_out", "s5_a_re", "s5_a_im", "s5_log_dt", "s5_b_re", "s5_b_im", "s5_c_re", "s5_c_im",
           "s5_d", "s5_glu_w", "s5_glu_b", "w_d_out", "w_out", "final_g"]


def run(x_prompt, x_sample, weights, ncores=8):
    Lp, Ls = x_prompt.shape[1], x_sample.shape[1]
    depth = weights["w_in"].shape[0]
    nb_p, nb_s = x_prompt.shape[0], x_sample.shape[0]
    slots = Lp // Ls
    segt = Ls // T
    nc = build([Lp], depth, segt=segt)
    base = {k: np.ascontiguousarray(np.asarray(weights[k], dtype=np.float32)) for k in _WNAMES}
    base["cst"] = _consts()
    rc_p = _rc(Lp)
    rc_s = np.ascontiguousarray(np.tile(_rc(Ls), (1, slots)))
    n_score = ncores - nb_p
    per = -(-nb_s // n_score)
    assert per <= slots
    in_maps, assign = [], []
    for c in range(ncores):
        m = dict(base)
        xp = np.zeros((Lp + 2 * HALO, D), np.float32)
        hm = np.ones((32, 3), np.float32)
        if c < nb_p:
            xp[HALO:HALO + Lp] = x_prompt[c]
            m["rc0"] = rc_p
            cf = 1.0
            assign.append(("p", c))
        else:
            ids = [i for i in range((c - nb_p) * per, min((c - nb_p + 1) * per, nb_s))]
            for k, i in enumerate(ids):
                xp[HALO + k * Ls:HALO + (k + 1) * Ls] = x_sample[i]
            m["rc0"] = rc_s
            cf = 0.0
            assign.append(("s", ids))
        hm[0:16, 1] = cf
        hm[16:32, 2] = cf
        m["x0"] = xp
        m["cfl"] = np.full((128, 1), cf, np.float32)
        m["hmk"] = hm
        in_maps.append(m)
    res = run_bass_kernel_spmd(nc, in_maps, core_ids=list(range(ncores)))
    yp = np.zeros(x_prompt.shape, np.float32)
    ysm = np.zeros(x_sample.shape, np.float32)
    for c in range(ncores):
        kind, ids = assign[c]
        y = res.results[c]["y0"]
        if kind == "p":
            yp[ids] = y
        else:
            for k, i in enumerate(ids):
                ysm[i] = y[k * Ls:(k + 1) * Ls]
    return yp, ysm


def kernel(x_prompt, x_sample, **weights):
    x_prompt = np.asarray(x_prompt, dtype=np.float32)
    x_sample = np.asarray(x_sample, dtype=np.float32)
    return run(x_prompt, x_sample, weights)
```

```python
import numpy as np
import concourse.bass as bass
import concourse.mybir as mybir
from concourse.bass_utils import run_bass_kernel_spmd

F32 = mybir.dt.float32
BF16 = mybir.dt.bfloat16
AF = mybir.ActivationFunctionType
ALU = mybir.AluOpType

D = 2048
WB = 1024
NIN = 20480
T = 512
HALO = 16
TE = T + 2 * HALO
EPS = 1e-6
CH = 64
PAD = 256
POOL_MOD = 1000
SW_S = 1
SW_H = 1
C_AVAL, C_AGLU, C_AGATE, C_BIN, C_BGATE, C_Q, C_FF, C_FB, C_I, C_CGATE, C_DIN, C_DGATE, C_R = (
    0, 8, 16, 24, 32, 40, 48, 56, 64, 72, 80, 88, 96)
V_NG, V_CB, V_LNG, V_LNB, V_PS, V_HG, V_SD, V_GB, V_LB0, V_LB1 = 0, 16, 24, 32, 40, 48, 56, 64, 72, 88
NVEC = 104


class Slot:
    __slots__ = ("w", "r")

    def __init__(self):
        self.w = {}
        self.r = {}


class EngW:
    def __init__(self, nc, eng, name, selfwait):
        self.eng = eng
        self.sem = nc.alloc_semaphore(name)
        self.n = 0
        self.seen = {}
        self.selfwait = selfwait


class Ker:
    def __init__(self, nc):
        self.nc = nc
        self.pe = EngW(nc, nc.tensor, "s_pe", False)
        self.act = EngW(nc, nc.scalar, "s_act", True)
        self.dve = EngW(nc, nc.vector, "s_dve", True)
        self.pool = EngW(nc, nc.gpsimd, "s_pool", True)
        self.sp = EngW(nc, nc.sync, "s_sp", False)
        self.ndsem = 24
        self.dsem = [nc.alloc_semaphore(f"s_dma{i}") for i in range(self.ndsem)]
        self.dval = [0] * self.ndsem
        self.dnext = 0
        self.slots = {}
        self.canon = {"xt": "arA", "ysf": "arA", "ubf": "arA", "zbf": "arA", ("cvt", 0): "arA", ("cvt", 1): "arA",
                      ("cvb", 0): "arA", ("cvb", 1): "arA",
                      "arB": "arB", ("ks", 1): "arB", "ue": "arB", ("cbuf", 0): "arB", ("cbuf", 1): "arB",
                      "hb": "arC", "mT": "arC", "hT": "arD", "fgb": "arD", "junk": "arD", "xh": "arB", ("psb", 0): "psb", ("psb", 1): "psb", "ya": "arY", "yb": "arY", ("kp", 0): "arY", ("kp", 1): "arY"}

    def slot(self, key):
        key = self.canon.get(key, key)
        s = self.slots.get(key)
        if s is None:
            s = Slot()
            self.slots[key] = s
        return s

    def _wait(self, E, R, W):
        deps = {}
        for k in R:
            for i, ev in self.slot(k).w.items():
                if deps.get(i, (None, 0))[1] < ev[1]:
                    deps[i] = ev
        for k in W:
            s = self.slot(k)
            for dd in (s.w, s.r):
                for i, ev in dd.items():
                    if deps.get(i, (None, 0))[1] < ev[1]:
                        deps[i] = ev
        for i, (sem, val) in deps.items():
            if sem is E.sem and not E.selfwait:
                continue
            if E.seen.get(i, 0) >= val:
                continue
            E.eng.wait_ge(sem, val)
            E.seen[i] = val

    def _update(self, ev, R, W):
        i = id(ev[0])
        for k in R:
            self.slot(k).r[i] = ev
        for k in W:
            s = self.slot(k)
            if s.r:
                s.r = {}
                s.w = {}
            s.w[i] = ev

    def op(self, E, fn, R=(), W=()):
        self._wait(E, R, W)
        ins = fn(E.eng)
        E.n += 1
        ins.then_inc(E.sem, 1)
        self._update((E.sem, E.n), R, W)

    def dma(self, out, in_, R=(), W=()):
        E = self.sp
        self._wait(E, R, W)
        k = self.dnext
        self.dnext = (k + 1) % self.ndsem
        sem = self.dsem[k]
        if self.dval[k] > 0 and E.seen.get(id(sem), 0) < self.dval[k]:
            E.eng.wait_ge(sem, self.dval[k])
            E.seen[id(sem)] = self.dval[k]
        E.eng.dma_start(out=out, in_=in_).then_inc(sem, 16)
        self.dval[k] += 16
        self._update((sem, self.dval[k]), R, W)

    def finish(self):
        E = self.sp
        for k in range(self.ndsem):
            if self.dval[k] > 0:
                E.eng.wait_ge(self.dsem[k], self.dval[k])
        for X in (self.pe, self.act, self.dve, self.pool):
            if X.n > 0:
                E.eng.wait_ge(X.sem, X.n)


def build(segs, depth=2, segt=None):
    nc = bass.Bass("TRN2", target_bir_lowering=False)
    K = Ker(nc)
    pe, act, dve, pool = K.pe, K.act, K.dve, K.pool
    nseg = len(segs)

    def din(name, shape, dt=F32):
        return nc.dram_tensor(name, list(shape), dt, kind="ExternalInput").ap()

    def dscr(name, shape, dt=F32):
        return nc.dram_tensor(name, list(shape), dt).ap()

    xin = [din(f"x{s}", [segs[s] + 2 * HALO, D]) for s in range(nseg)]
    yout = [nc.dram_tensor(f"y{s}", [segs[s], D], F32, kind="ExternalOutput").ap() for s in range(nseg)]
    x1 = [dscr(f"x1_{s}", [segs[s] + 2 * HALO, D]) for s in range(nseg)]
    rcin = [din(f"rc{s}", [4, segs[s]]) for s in range(nseg)]
    cst = din("cst", [128, 128 + 256 + 256 + 512])
    cfl_d = din("cfl", [128, 1])
    hmk_d = din("hmk", [32, 3])
    norm_g = din("norm_g", [depth, D]); w_in = din("w_in", [depth, D, NIN])
    conv_w = din("conv_w", [depth, 31, WB]); conv_b = din("conv_b", [depth, WB])
    conv_ln_g = din("conv_ln_g", [depth, WB]); conv_ln_b = din("conv_ln_b", [depth, WB])
    w_a_out = din("w_a_out", [depth, WB, D]); pool_w = din("pool_w", [depth, 4, 256, 256])
    pool_scale = din("pool_scale", [depth, WB]); w_b_out = din("w_b_out", [depth, WB, D])
    hg_lb = din("hg_lb", [depth, 2, WB]); hg_norm_g = din("hg_norm_g", [depth, WB])
    w_c_out = din("w_c_out", [depth, WB, D])
    s5_a_re = din("s5_a_re", [depth, 2, 64, 64]); s5_a_im = din("s5_a_im", [depth, 2, 64, 64])
    s5_log_dt = din("s5_log_dt", [depth, 2, 64])
    s5_b_re = din("s5_b_re", [depth, 2, 64, 64, 16]); s5_b_im = din("s5_b_im", [depth, 2, 64, 64, 16])
    s5_c_re = din("s5_c_re", [depth, 2, 64, 16, 64]); s5_c_im = din("s5_c_im", [depth, 2, 64, 16, 64])
    s5_d = din("s5_d", [depth, WB]); s5_glu_w = din("s5_glu_w", [depth, WB, WB])
    s5_glu_b = din("s5_glu_b", [depth, WB]); w_d_out = din("w_d_out", [depth, WB, D])
    w_out = din("w_out", [depth, D, D]); final_g = din("final_g", [D])

    def wscr(name, Kdim, N):
        return dscr(name, [N // 128, 128, Kdim // 128, 128], BF16)
    wb_in = [wscr(f"wb_in{l}", D, NIN) for l in range(depth)]
    wb_ao = [[wscr(f"wb_o{i}_{l}", WB, D) for i in range(4)] for l in range(depth)]
    wb_out = [wscr(f"wb_out{l}", D, D) for l in range(depth)]
    wb_glu = [wscr(f"wb_glu{l}", WB, WB) for l in range(depth)]
    wb_pool = [[wscr(f"wb_pool{l}_{g}", 256, 256) for g in range(4)] for l in range(depth)]
    cdiag = [dscr(f"cdiag{l}", [8, 128, 31, 128], BF16) for l in range(depth)]
    tabd = [[dscr(f"tabd{l}_{d}", [32, 128, 2, T]) for d in range(2)] for l in range(depth)]
    sp_q = [dscr(f"sp_q{s}", [8, 128, segs[s]]) for s in range(nseg)]
    sp_o = [dscr(f"sp_o{s}", [8, 128, segs[s]]) for s in range(nseg)]
    sp_y = [dscr(f"sp_y{s}", [8, 128, segs[s]]) for s in range(nseg)]
    sp_u = [dscr(f"sp_u{s}", [8, 128, segs[s]], BF16) for s in range(nseg)]
    sp_v = [dscr(f"sp_v{s}", [segs[s], WB], BF16) for s in range(nseg)]

    def sb(name, shape, dt=F32):
        return nc.alloc_sbuf_tensor(name, list(shape), dt)
    cs = sb("cs", [128, 128 + 256 + 256 + 512])
    ident = cs[:, 0:128]
    maskF = cs[:, 128:384]
    maskB = cs[:, 384:640]
    rmask = cs[:, 640:1152]
    identb = sb("identb", [128, 128], BF16)
    onesW = sb("onesW", [128, 128], BF16)
    onesH = sb("onesH", [128, 128], BF16)
    vecT = sb("vecT", [128, NVEC])
    lbv = sb("lbv", [128, 16]); omlb = sb("omlb", [128, 16]); nomlb = sb("nomlb", [128, 16])
    arA = sb("arA", [128, 8192])
    arB = sb("arB", [128, 4352])
    arC = sb("arC", [128, 5120])
    arD = sb("arD", [128, 4352])
    xt = arA[:, :].rearrange("p (s d) -> p s d", d=D)
    ysf = arA[:, 0:4096].rearrange("p (j t) -> p j t", t=T)
    ubf = arA[:, 4096:6144].bitcast(BF16).rearrange("p (j t) -> p j t", t=T)
    zbf = arA[:, 6144:8192].bitcast(BF16).rearrange("p (j t) -> p j t", t=T)
    cvt = [arA[:, i * 2048:(i + 1) * 2048].rearrange("p (k n) -> p k n", n=128) for i in range(2)]
    cvb = [arA[:, 4096 + i * 1024:4096 + (i + 1) * 1024].bitcast(BF16).rearrange("p (k n) -> p k n", n=128) for i in range(2)]
    s5buf = [arB[:, i * 512:(i + 1) * 512] for i in range(4)]
    tbuf = [arB[:, 2048 + i * 1024:2048 + (i + 1) * 1024].rearrange("p (a t) -> p a t", t=T) for i in range(2)]
    ks = [[s5buf[0]]]
    ue = arB[:, 0:2176].bitcast(BF16).rearrange("p (c t) -> p c t", t=TE)
    cb0 = arB[:, 2176:2176 + 1984].bitcast(BF16).rearrange("p (j n) -> p j n", n=128)
    cbuf = [cb0, cb0]
    hb = arC[:, :].bitcast(BF16).rearrange("p (s d) -> p s d", d=D)
    mT = arC[:, 0:4096].bitcast(BF16).rearrange("p (c t) -> p c t", t=T)
    hT = arD[:, :].bitcast(BF16).rearrange("p (c t) -> p c t", t=TE)
    fgb = arD[:, 0:D]
    xh = arB[0:32, 0:D]
    NWB = 3
    wbuf = [sb(f"wbuf{i}", [128, 16, 128], BF16) for i in range(NWB)]
    ss = sb("ss", [128, 8]); rstd = sb("rstd", [128, 8])
    junk = arD[:, 2048:3072].bitcast(BF16)
    NSC = 10
    sc = [sb(f"sc{i}", [128, TE]) for i in range(NSC)]
    bq = sb("bq", [128, T], BF16); bk = sb("bk", [128, T], BF16); bvf = sb("bvf", [128, T], BF16)
    vT = sb("vT", [128, 8, 4, 128], BF16)
    kT = sb("kT", [128, 4, 128], BF16)
    At = sb("At", [128, 4, CH], BF16)
    S32 = sb("S32", [128, 8, 128]); Sbf = sb("Sbf", [128, 8, 128], BF16); Tm = sb("Tm", [128, 128])
    Hb = [[sb(f"Hb{a}{b}", [128, T], BF16) for b in range(2)] for a in range(2)]
    s5p = sb("s5p", [128, 32, 32])
    hc = sb("hc", [128, 2, 32]); cin = sb("cin", [128, 2, 32]); tmp32 = sb("tmp32", [128, 4, 32])
    Bm = sb("Bm", [128, 2, 16, 128], BF16)
    Cm = sb("Cm", [128, 2, 32, 64], BF16)
    stg = sb("stg", [128, 128])
    stgA = sb("stgA", [128, 128]); stgB = sb("stgB", [128, 128])
    stg2 = sb("stg2", [128, 256])
    cfl = sb("cflag", [128, 1]); hmk = sb("hmask", [32, 3])
    tmpi = sb("tmpi", [128, 32], mybir.dt.int32)
    arY = sb("arY", [128, 4096])
    ya = arY[:, 0:2048].bitcast(BF16).rearrange("p (c t) -> p c t", t=T)
    yb = arY[:, 2048:4096].bitcast(BF16).rearrange("p (c t) -> p c t", t=T)
    ks2 = [[arY[:, (2 * a + b) * 1024:(2 * a + b + 1) * 1024] for b in range(2)] for a in range(2)]
    yc = sb("yc", [128, 8, T], BF16); yd = sb("yd", [128, 8, T], BF16)
    rcb = sc[8][:, 0:T]
    print("sbuf remaining", nc.sbuf_bytes_remaining)
    ps = [nc.alloc_psum_tensor(f"ps{i}", [128, T], F32) for i in range(6)]
    psb = nc.alloc_psum_tensor("psb", [128, 2 * T], BF16)
    psx = nc.alloc_psum_tensor("psx", [128, T], F32)
    ps5b = ps[5][:, 256:512].bitcast(BF16)
    pcnt = [0]

    def nps():
        i = pcnt[0] % 2
        pcnt[0] += 1
        return ps[i], ("ps", i)

    def A(E, out, in_, func, R, W, bias=None, scale=None, accum=None):
        kw = {}
        if bias is not None:
            kw["bias"] = bias
        if scale is not None:
            kw["scale"] = scale
        if accum is not None:
            kw["accum_out"] = accum
        K.op(act, lambda e: e.activation(out=out, in_=in_, func=func, **kw), R, W)

    def TS(E, out, in0, s1, s2, op0, op1, R, W):
        if s2 is None:
            K.op(E, lambda e: e.tensor_scalar(out, in0, s1, None, op0), R, W)
        else:
            K.op(E, lambda e: e.tensor_scalar(out, in0, s1, s2, op0, op1), R, W)

    def TT(E, out, in0, in1, op, R, W):
        K.op(E, lambda e: e.tensor_tensor(out, in0, in1, op), R, W)

    def STT(E, out, in0, scalar, in1, op0, op1, R, W):
        K.op(E, lambda e: e.scalar_tensor_tensor(out, in0, scalar, in1, op0, op1), R, W)

    def MM(out, lhsT, rhs, start, stop, R, W):
        K.op(pe, lambda e: e.matmul(out, lhsT, rhs, start=start, stop=stop), R, W)

    def TR(out, in_, idn, R, W):
        K.op(pe, lambda e: e.transpose(out, in_, idn), R, W)

    wcnt = [0]

    def linear(wb, nk, mlist, rhs_fn, ncols_list, evac, Rin):
        n = len(mlist)
        loaded = {}

        def load(i):
            b = wcnt[0] % NWB
            wcnt[0] += 1
            K.dma(wbuf[b][:, 0:nk, :], wb[mlist[i]], R=[], W=[("wbuf", b)])
            loaded[i] = b
        for i in range(min(NWB - 1, n)):
            load(i)
        for i in range(n):
            if i + NWB - 1 < n:
                load(i + NWB - 1)
            b = loaded.pop(i)
            for gi, (c0, ncol) in enumerate(ncols_list):
                p, pslot = nps()
                for kc in range(nk):
                    MM(p[:, 0:ncol], wbuf[b][:, kc, :], rhs_fn(kc)[:, c0:c0 + ncol], kc == 0, kc == nk - 1,
                       R=[("wbuf", b)] + Rin, W=[pslot])
                evac(i, mlist[i], gi, p[:, 0:ncol], pslot)

    ccnt = [0]

    def convert(src, Kdim, N, dst):
        nk = Kdim // 128
        srcv = src.rearrange("(kc kp) n -> kp kc n", kp=128)
        dstv = dst.rearrange("m kp kc j -> kp m kc j")
        for n0 in range(0, N, 128):
            i = ccnt[0] % 2
            ccnt[0] += 1
            K.dma(cvt[i][:, 0:nk, :], srcv[:, :, n0:n0 + 128], R=[], W=[("cvt", i)])
            eng = act if (ccnt[0] % 2 == 0) else dve
            if eng is act:
                A(act, cvb[i][:, 0:nk, :], cvt[i][:, 0:nk, :], AF.Copy, R=[("cvt", i)], W=[("cvb", i)])
            else:
                K.op(dve, lambda e: e.tensor_copy(cvb[i][:, 0:nk, :], cvt[i][:, 0:nk, :]), R=[("cvt", i)], W=[("cvb", i)])
            K.dma(dstv[:, n0 // 128, :, :], cvb[i][:, 0:nk, :], R=[("cvb", i)], W=[("wdram",)])

    K.dma(cs[:], cst[:, :], W=["cs"])
    K.dma(cfl[:], cfl_d[:, :], W=["cfl"])
    K.dma(hmk[:], hmk_d[:, :], W=["hmk"])
    K.op(dve, lambda e: e.tensor_copy(identb[:], ident), R=["cs"], W=["identb"])
    K.op(dve, lambda e: e.memset(ss[:], 1.0), W=["ss"])
    K.op(dve, lambda e: e.memset(onesW[:], 1.0 / 1024.0), W=["onesW"])
    K.op(dve, lambda e: e.memset(onesH[:], 1.0 / 128.0), W=["onesH"])
    for s in range(nseg):
        for off in (0, HALO + segs[s]):
            K.op(dve, lambda e: e.memset(xh[0:HALO, :], 0.0), W=["xh"])
            K.dma(x1[s][off:off + HALO, :], xh[0:HALO, :], R=["xh"], W=[("x1", s)])

    for l in range(depth):
        convert(w_in[l], D, NIN, wb_in[l])
        for i, wsrc in enumerate((w_a_out, w_b_out, w_c_out, w_d_out)):
            convert(wsrc[l], WB, D, wb_ao[l][i])
        convert(w_out[l], D, D, wb_out[l])
        convert(s5_glu_w[l], WB, WB, wb_glu[l])
        for g in range(4):
            convert(pool_w[l, g], 256, 256, wb_pool[l][g])

    def prep_layer(l):
        rows = [(norm_g[l], 16), (conv_b[l], 8), (conv_ln_g[l], 8), (conv_ln_b[l], 8), (pool_scale[l], 8),
                (hg_norm_g[l], 8), (s5_d[l], 8), (s5_glu_b[l], 8), (hg_lb[0, 0], 8), (hg_lb[0, 1], 8),
                (hg_lb[1 if depth > 1 else 0, 0], 8), (hg_lb[1 if depth > 1 else 0, 1], 8)]
        r = 0
        K.op(dve, lambda e: e.memset(stg[:], 0.0), W=["stg"])
        for v, n in rows:
            K.dma(stg[r:r + n, :], v.rearrange("(c p) -> c p", p=128), W=["stg"])
            r += n
        TR(psx[:, 0:128], stg[:, :], ident, R=["stg", "cs"], W=["psx"])
        K.op(dve, lambda e: e.tensor_copy(vecT[:], psx[:, 0:NVEC]), R=["psx"], W=["vecT"])
        if l == 0:
            K.op(dve, lambda e: e.memset(lbv[:], 0.0), W=["lbv"])
        else:
            TT(dve, lbv[:], vecT[:, V_LB1:V_LB1 + 16], vecT[:, V_LB0:V_LB0 + 16], ALU.subtract, R=["vecT"], W=["lbv"])
            A(act, lbv[:], lbv[:], AF.Sigmoid, R=["lbv"], W=["lbv"])
        TS(dve, omlb[:], lbv[:], -1.0, 1.0, ALU.mult, ALU.add, R=["lbv"], W=["omlb"])
        TS(dve, nomlb[:], omlb[:], -1.0, None, ALU.mult, None, R=["omlb"], W=["nomlb"])
        for half in range(2):
            nr = 128 if half == 0 else 31 * 8 - 128
            K.op(dve, lambda e: e.memset(stg[:], 0.0), W=["stg"])
            src = conv_w[l].rearrange("j (c p) -> (j c) p", p=128)
            K.dma(stg[0:nr, :], src[half * 128:half * 128 + nr, :], W=["stg"])
            TR(psx[:, 0:128], stg[:, :], ident, R=["stg", "cs"], W=["psx"])
            K.op(dve, lambda e: e.tensor_copy(stg2[:, half * 128:half * 128 + 128], psx[:, 0:128]), R=["psx"], W=["stg2"])
        for c in range(8):
            b = c % 2
            for j in range(31):
                col = j * 8 + c
                TS(dve, cbuf[b][:, j, :], ident, stg2[:, col:col + 1], None, ALU.mult, None, R=["stg2", "cs"], W=[("cbuf", b)])
            K.dma(cdiag[l][c], cbuf[b][:], R=[("cbuf", b)], W=[("cdiag",)])

    def prep_s5(l, d):
        def ldT(src, dstcol):
            K.op(dve, lambda e: e.memset(stg[:], 0.0), W=["stg"])
            K.dma(stg[0:32, :], src.rearrange("(q two) p -> q (two p)", two=2), W=["stg"])
            TR(psx[:, 0:128], stg[:, :], ident, R=["stg", "cs"], W=["psx"])
            K.op(dve, lambda e: e.tensor_copy(tmp32[:, dstcol, :], psx[:, 0:32]), R=["psx"], W=["tmp32"])
        ldT(s5_a_re[l, d], 0)
        ldT(s5_a_im[l, d], 1)
        K.op(dve, lambda e: e.memset(stg[:], 0.0), W=["stg"])
        K.dma(stg2[0:32, 0:2], s5_log_dt[l, d].rearrange("(q two) -> q two", two=2), W=["stg2"])
        for two in range(2):
            TS(dve, stg[0:32, two * 64:(two + 1) * 64], stg[0:32, two * 64:(two + 1) * 64], stg2[0:32, two:two + 1], None,
               ALU.add, None, R=["stg", "stg2"], W=["stg"])
        TR(psx[:, 0:128], stg[:, :], ident, R=["stg", "cs"], W=["psx"])
        A(act, tmp32[:, 2, :], psx[:, 0:32], AF.Exp, R=["psx"], W=["tmp32"])
        are, aim, dt = tmp32[:, 0, :], tmp32[:, 1, :], tmp32[:, 2, :]
        t3 = tmp32[:, 3, :]
        R_, W_ = ["tmp32", "s5p"], ["tmp32", "s5p"]
        mag, ang, sn, cn = s5p[:, :, 27], s5p[:, :, 28], s5p[:, :, 29], s5p[:, :, 30]
        TT(dve, mag, dt, are, ALU.mult, R_, W_)
        A(act, mag, mag, AF.Exp, R_, W_)
        TT(dve, ang, dt, aim, ALU.mult, R_, W_)
        TWO_PI = 2.0 * np.pi

        def sin_of(dst, phase):
            TS(dve, dst, ang, 1.0 / TWO_PI, phase / TWO_PI, ALU.mult, ALU.add, R_, W_)
            K.op(dve, lambda e: e.tensor_copy(tmpi[:], dst), R_ + ["tmpi"], W_ + ["tmpi"])
            K.op(dve, lambda e: e.tensor_copy(t3, tmpi[:]), R_ + ["tmpi"], W_)
            TT(dve, dst, dst, t3, ALU.subtract, R_, W_)
            TS(dve, t3, dst, 0.5, None, ALU.is_gt, None, R_, W_)
            TT(dve, dst, dst, t3, ALU.subtract, R_, W_)
            TS(dve, t3, dst, -0.5, None, ALU.is_lt, None, R_, W_)
            TT(dve, dst, dst, t3, ALU.add, R_, W_)
            A(act, dst, dst, AF.Sin, R_, W_, scale=TWO_PI)
        sin_of(sn, 0.0)
        sin_of(cn, np.pi / 2)
        lre, lim = s5p[:, :, 1], s5p[:, :, 2]
        TT(dve, lre, mag, cn, ALU.mult, R_, W_)
        TT(dve, lim, mag, sn, ALU.mult, R_, W_)
        K.op(dve, lambda e: e.tensor_copy(s5p[:, :, 0], mag), R_, W_)
        K.op(dve, lambda e: e.tensor_copy(s5p[:, :, 8], cn), R_, W_)
        K.op(dve, lambda e: e.tensor_copy(s5p[:, :, 9], sn), R_, W_)
        den, xm, gre, gim = s5p[:, :, 27], s5p[:, :, 28], s5p[:, :, 29], s5p[:, :, 30]
        TT(dve, den, are, are, ALU.mult, R_, W_)
        TT(dve, t3, aim, aim, ALU.mult, R_, W_)
        TT(dve, den, den, t3, ALU.add, R_, W_)
        K.op(dve, lambda e: e.reciprocal(den, den), R_, W_)
        TS(dve, xm, lre, -1.0, None, ALU.add, None, R_, W_)
        TT(dve, gre, xm, are, ALU.mult, R_, W_)
        TT(dve, t3, lim, aim, ALU.mult, R_, W_)
        TT(dve, gre, gre, t3, ALU.add, R_, W_)
        TT(dve, gre, gre, den, ALU.mult, R_, W_)
        TT(dve, gim, lim, are, ALU.mult, R_, W_)
        TT(dve, t3, xm, aim, ALU.mult, R_, W_)
        TT(dve, gim, gim, t3, ALU.subtract, R_, W_)
        TT(dve, gim, gim, den, ALU.mult, R_, W_)
        for k in range(1, 10):
            pr, pi_, qr, qi = s5p[:, :, 6 + 2 * k], s5p[:, :, 7 + 2 * k], s5p[:, :, 8 + 2 * k], s5p[:, :, 9 + 2 * k]
            TT(dve, qr, pr, pr, ALU.mult, R_, W_)
            TT(dve, t3, pi_, pi_, ALU.mult, R_, W_)
            TT(dve, qr, qr, t3, ALU.subtract, R_, W_)
            TT(dve, qi, pr, pi_, ALU.mult, R_, W_)
            TS(dve, qi, qi, 2.0, None, ALU.mult, None, R_, W_)
        TS(dve, s5p[:, :, 3], s5p[:, :, 9], -1.0, None, ALU.mult, None, R_, W_)
        for q in range(32):
            tb = tbuf[q % 2]
            tn = ("tb", q % 2)
            cc, ssn = tb[:, 0, :], tb[:, 1, :]
            K.op(dve, lambda e: e.memset(cc[:, 0:1], 1.0), W=[tn, "arB"])
            K.op(dve, lambda e: e.memset(ssn[:, 0:1], 0.0), W=[tn])
            for k in range(9):
                n = 1 << k
                ur, ui = s5p[:, q, 8 + 2 * k:9 + 2 * k], s5p[:, q, 9 + 2 * k:10 + 2 * k]
                TS(dve, cc[:, n:2 * n], cc[:, 0:n], ur, None, ALU.mult, None, R=[tn, "s5p"], W=[tn])
                TS(dve, ssn[:, n:2 * n], cc[:, 0:n], ui, None, ALU.mult, None, R=[tn, "s5p"], W=[tn])
                TS(dve, sc[0][:, 0:n], ssn[:, 0:n], ui, None, ALU.mult, None, R=[tn, "s5p"], W=["sc0"])
                TT(dve, cc[:, n:2 * n], cc[:, n:2 * n], sc[0][:, 0:n], ALU.subtract, R=[tn, "sc0"], W=[tn])
                STT(dve, ssn[:, n:2 * n], ssn[:, 0:n], ur, ssn[:, n:2 * n], ALU.mult, ALU.add, R=[tn, "s5p"], W=[tn])
            K.dma(tabd[l][d][q], tb, R=[tn, "arB"], W=[("tabd",)])
        K.op(dve, lambda e: e.memset(Bm[:], 0.0), W=["Bm"])
        K.op(dve, lambda e: e.memset(Cm[:], 0.0), W=["Cm"])
        for j in range(8):
            K.op(dve, lambda e: e.memset(stgA[:], 0.0), W=["stgA"])
            K.op(dve, lambda e: e.memset(stgB[:], 0.0), W=["stgB"])
            for qq in range(4):
                q = 4 * j + qq
                base = 32 * qq
                K.dma(stg2[:, 0:16], s5_b_re[l, d, 2 * q:2 * q + 2].rearrange("g p c -> (g p) c"), W=["stg2"])
                K.dma(stg2[:, 16:32], s5_b_im[l, d, 2 * q:2 * q + 2].rearrange("g p c -> (g p) c"), W=["stg2"])
                br, bi = stg2[:, 0:16], stg2[:, 16:32]
                o1, o2 = stg2[:, 32:48], stg2[:, 48:64]
                grq, giq = s5p[:, q, 29:30], s5p[:, q, 30:31]
                Rq, Wq = ["stg2", "s5p"], ["stg2"]
                TS(dve, o1, bi, giq, -1.0, ALU.mult, ALU.mult, Rq, Wq)
                STT(dve, o1, br, grq, o1, ALU.mult, ALU.add, Rq, Wq)
                TS(dve, o2, br, giq, None, ALU.mult, None, Rq, Wq)
                STT(dve, o2, bi, grq, o2, ALU.mult, ALU.add, Rq, Wq)
                for o, st, sn_ in ((o1, stgA, "stgA"), (o2, stgB, "stgB")):
                    K.op(dve, lambda e: e.tensor_copy(st[0:64, base:base + 16], o[0:64, :]), R=["stg2"], W=[sn_])
                    K.op(dve, lambda e: e.tensor_copy(st[64:128, base + 16:base + 32], o[64:128, :]), R=["stg2"], W=[sn_])
            for ri, (st, sn_) in enumerate(((stgA, "stgA"), (stgB, "stgB"))):
                TR(psx[:, 0:128], st[:, :], ident, R=[sn_, "cs"], W=["psx"])
                K.op(dve, lambda e: e.tensor_copy(Bm[:, ri, j, :], psx[:, 0:128]), R=["psx"], W=["Bm"])
                K.op(dve, lambda e: e.tensor_copy(Bm[:, ri, 8 + j, :], psx[:, 0:128]), R=["psx"], W=["Bm"])
                K.op(dve, lambda e: e.memset(Bm[64:96, ri, 8 + j, :], 0.0), W=["Bm"])
            for ri, csrc in enumerate((s5_c_re, s5_c_im)):
                K.op(dve, lambda e: e.memset(stgA[:], 0.0), W=["stgA"])
                for qq in range(4):
                    q = 4 * j + qq
                    K.dma(stgA[32 * qq:32 * qq + 16, 0:64], csrc[l, d, 2 * q], W=["stgA"])
                    K.dma(stgA[32 * qq + 16:32 * qq + 32, 64:128], csrc[l, d, 2 * q + 1], W=["stgA"])
                TR(psx[:, 0:128], stgA[:, :], ident, R=["stgA", "cs"], W=["psx"])
                for qq in range(4):
                    co = 32 if qq == 3 else 0
                    cdst = Cm[:, ri, 4 * j + qq, co:co + 32]
                    TS(dve, cdst, psx[:, 32 * qq:32 * qq + 32], 1.0 if ri == 0 else -1.0, None, ALU.mult, None, R=["psx"], W=["Cm"])

    def load_norm(src, seg, r0, l, ext, hm=0):
        K.dma(xt, src[HALO + r0:HALO + r0 + T, :].rearrange("(s p) d -> p s d", p=128), W=["xt"])
        nsub = 4
        if ext:
            K.dma(xh[0:HALO, :], src[r0:r0 + HALO, :], W=["xh"])
            K.dma(xh[HALO:2 * HALO, :], src[HALO + r0 + T:HALO + r0 + T + HALO, :], W=["xh"])
            nsub = 5
        for s in range(nsub):
            np_ = 128 if s < 4 else 32
            xin_ = xt[:, s, :] if s < 4 else xh[:, :]
            rs = ["xt"] if s < 4 else ["xh"]
            A(act, junk[0:np_, :], xin_, AF.Square, R=rs, W=["junk", "ss"], accum=ss[0:np_, s:s + 1])
        TS(dve, rstd[:, 0:nsub], ss[:, 0:nsub], 1.0 / D, EPS, ALU.mult, ALU.add, R=["ss"], W=["rstd"])
        A(act, rstd[:, 0:nsub], rstd[:, 0:nsub], AF.Ln, R=["rstd"], W=["rstd"])
        A(act, rstd[:, 0:nsub], rstd[:, 0:nsub], AF.Exp, R=["rstd"], W=["rstd"], scale=-0.5)
        if ext and hm != 0:
            TT(dve, rstd[0:32, 4:5], rstd[0:32, 4:5], hmk[0:32, hm:hm + 1], ALU.mult, R=["rstd", "hmk"], W=["rstd"])
        for s in range(nsub):
            np_ = 128 if s < 4 else 32
            xin_ = xt[:, s, :] if s < 4 else xh[:, :]
            rs = ["xt"] if s < 4 else ["xh"]
            A(act, hb[0:np_, s, :], xin_, AF.Copy, R=rs + ["rstd"], W=["hb"], scale=rstd[0:np_, s:s + 1])
        for dc in range(16):
            half = dc % 2
            pb = psb[:, 0:T] if half == 0 else ps[5][:, 0:256].bitcast(BF16)
            pbn = "psb" if half == 0 else ("ps", 5)
            for s in range(4):
                TR(pb[:, s * 128:(s + 1) * 128], hb[:, s, dc * 128:(dc + 1) * 128], identb[:], R=["hb", "identb"], W=[pbn])
            A(act, hT[:, dc, HALO:HALO + T], pb, AF.Copy, R=[pbn, "vecT"], W=["hT"], scale=vecT[:, V_NG + dc:V_NG + dc + 1])
            if ext:
                TR(psx[:, 0:32].bitcast(BF16)[:, 0:32], hb[0:32, 4, dc * 128:(dc + 1) * 128], identb[0:32, 0:32], R=["hb", "identb"], W=["psx"])
                pxb = psx[:, 0:32].bitcast(BF16)
                A(act, hT[:, dc, 0:HALO], pxb[:, 0:HALO], AF.Copy, R=["psx", "vecT"], W=["hT"], scale=vecT[:, V_NG + dc:V_NG + dc + 1])
                A(act, hT[:, dc, HALO + T:TE], pxb[:, HALO:2 * HALO], AF.Copy, R=["psx", "vecT"], W=["hT"], scale=vecT[:, V_NG + dc:V_NG + dc + 1])

    def hT_k(kc):
        return hT[:, kc, :]

    def reset_states():
        K.op(dve, lambda e: e.memset(S32[:], 0.0), W=["S32"])
        K.op(dve, lambda e: e.memset(Sbf[:], 0.0), W=["Sbf"])
        K.op(dve, lambda e: e.memset(hc[:], 0.0), W=["hc"])

    def scale_states():
        TS(dve, S32[:].rearrange("p h k -> p (h k)"), S32[:].rearrange("p h k -> p (h k)"), cfl[:, 0:1], None, ALU.mult, None, R=["S32", "cfl"], W=["S32"])
        A(act, Sbf[:].rearrange("p h k -> p (h k)"), S32[:].rearrange("p h k -> p (h k)"), AF.Copy, R=["S32"], W=["Sbf"])
        TS(dve, hc[:].rearrange("p a q -> p (a q)"), hc[:].rearrange("p a q -> p (a q)"), cfl[:, 0:1], None, ALU.mult, None, R=["hc", "cfl"], W=["hc"])

    def soft_left(ti, nt):
        return segt is not None and ti > 0 and ti % segt == 0

    def soft_right(ti, nt):
        return segt is not None and ti < nt - 1 and ti % segt == segt - 1

    def hg_head(hd, d):
        q_, sg, f_, lf, b_, e1, e2, k_ = (sc[i][:, 0:T] for i in range(8))
        lcol = hd if d == 0 else 8 + hd
        A(act, f_, sg, AF.Identity, R=["sc1", "omlb", "lbv"], W=["sc2"], scale=omlb[:, lcol:lcol + 1], bias=lbv[:, lcol:lcol + 1])
        A(act, lf, f_, AF.Ln, R=["sc2"], W=["sc3"])
        A(act, k_, sg, AF.Identity, R=["sc1", "omlb", "nomlb"], W=["sc7"], scale=nomlb[:, lcol:lcol + 1], bias=omlb[:, lcol:lcol + 1])
        yield
        K.op(dve, lambda e: e.tensor_tensor_scan(b_, rmask, lf, 0.0, ALU.mult, ALU.add), R=["cs", "sc3"], W=["sc4"])
        yield
        bend = sc[4][:, 0:T].rearrange("p (c t) -> p c t", t=CH)[:, :, CH - 1]
        A(act, sc[8][:, 0:8], bend, AF.Exp, R=["sc4"], W=["sc8"])
        if d == 0:
            A(act, e1, b_, AF.Exp, R=["sc4"], W=["sc5"])
            A(act, e2, b_, AF.Exp, R=["sc4"], W=["sc6"], scale=-1.0)
        else:
            TT(dve, b_, b_, lf, ALU.subtract, R=["sc4", "sc3"], W=["sc4"])
            A(act, e1, b_, AF.Exp, R=["sc4"], W=["sc5"], scale=-1.0)
            A(act, e2, b_, AF.Exp, R=["sc4"], W=["sc6"])
        yield
        TT(dve, bq[:], q_, e1, ALU.mult, R=["sc0", "sc5"], W=["bq"])
        TT(dve, bk[:], k_, e2, ALU.mult, R=["sc7", "sc6"], W=["bk"])
        for s in range(4):
            TR(ps5b[:, s * 128:(s + 1) * 128], bk[:, s * 128:(s + 1) * 128], identb[:], R=["bk", "identb"], W=[("ps", 5)])
        A(act, kT[:], ps5b.rearrange("p (s k) -> p s k", k=128), AF.Copy, R=[("ps", 5)], W=["kT"])
        yield
        for c in range(8):
            h64 = (c % 2) * 64
            MM(ps[5][h64:h64 + 64, (c // 2) * CH:(c // 2 + 1) * CH], bk[:, c * CH:(c + 1) * CH], bq[:, c * CH:(c + 1) * CH],
               True, True, R=["bk", "bq"], W=[("ps", 5)])
        yield
        msk = maskF if d == 0 else maskB
        TT(dve, At[:], ps[5][:, 0:256].rearrange("p (a t) -> p a t", t=CH), msk.rearrange("p (a t) -> p a t", t=CH),
           ALU.mult, R=[("ps", 5), "cs"], W=["At"])
        order = range(8) if d == 0 else range(7, -1, -1)
        for c in order:
            h64 = (c % 2) * 64
            et = sc[8][:, c:c + 1]
            if d == 1:
                TS(dve, S32[:, hd, :], S32[:, hd, :], et, None, ALU.mult, None, R=["S32", "sc8"], W=["S32"])
                A(act, Sbf[:, hd, :], S32[:, hd, :], AF.Copy, R=["S32"], W=["Sbf"])
            oc = ps[4][:, c * CH:(c + 1) * CH]
            MM(oc, vT[h64:h64 + 64, hd, c // 2, :], At[h64:h64 + 64, c // 2, :], True, False, R=["vT", "At"], W=[("ps", 4)])
            MM(oc, Sbf[:, hd, :], bq[:, c * CH:(c + 1) * CH], False, True, R=["Sbf", "bq"], W=[("ps", 4)])
            MM(psx[:, 0:128], kT[h64:h64 + 64, c // 2, :], vT[h64:h64 + 64, hd, c // 2, :], True, True, R=["kT", "vT"], W=["psx"])
            yield
            if d == 0:
                TT(dve, Tm[:], psx[:, 0:128], S32[:, hd, :], ALU.add, R=["psx", "S32"], W=["Tm"])
                TS(dve, S32[:, hd, :], Tm[:], et, None, ALU.mult, None, R=["Tm", "sc8"], W=["S32"])
                A(act, Sbf[:, hd, :], S32[:, hd, :], AF.Copy, R=["S32"], W=["Sbf"])
            else:
                TT(dve, S32[:, hd, :], psx[:, 0:128], S32[:, hd, :], ALU.add, R=["psx", "S32"], W=["S32"])
            yield

    def run_streams(gens, weights):
        alive = [True] * len(gens)
        while any(alive):
            for gi, g in enumerate(gens):
                if not alive[gi]:
                    continue
                for _ in range(weights[gi]):
                    try:
                        next(g)
                    except StopIteration:
                        alive[gi] = False
                        break

    def s5_dir(l, d):
        ETc, ETs = s5p[:, :, 26], s5p[:, :, 27]
        R_, W_ = ["s5p", "hc", "cin", "tmp32"], ["cin", "tmp32"]
        TT(dve, cin[:, 0, :], ETc, hc[:, 0, :], ALU.mult, R_, W_)
        TT(dve, tmp32[:, 0, :], ETs, hc[:, 1, :], ALU.mult, R_, W_)
        TT(dve, cin[:, 0, :], cin[:, 0, :], tmp32[:, 0, :], ALU.subtract, R_, W_)
        TT(dve, cin[:, 1, :], ETc, hc[:, 1, :], ALU.mult, R_, W_)
        TT(dve, tmp32[:, 0, :], ETs, hc[:, 0, :], ALU.mult, R_, W_)
        TT(dve, cin[:, 1, :], cin[:, 1, :], tmp32[:, 0, :], ALU.add, R_, W_)
        bA, bB, bC, bD = s5buf
        psI = psb[:, :].bitcast(F32)
        K.dma(tbuf[0], tabd[l][d][0], R=[("tabd",)], W=["arB", ("tb", 0)])
        pending = []
        for q in range(32):
            j, base = q // 4, 32 * (q % 4)
            tb = tbuf[q % 2]
            tn = ("tb", q % 2)
            if q + 1 < 32:
                K.dma(tbuf[(q + 1) % 2], tabd[l][d][q + 1], R=[("tabd",)], W=[("tb", (q + 1) % 2)])
            hsel = q % 2
            hbq = Hb[hsel]
            hbn = ("Hb", hsel)
            for ri, (pt, pn) in enumerate(((ps[2][:, :], ("ps", 2)), (psI, "psb"))):
                if base == 96:
                    MM(pt, Bm[64:128, ri, 8 + j, :], ubf[64:128, j, :], True, True, R=["Bm", "ubf"], W=[pn])
                else:
                    MM(pt, Bm[base:base + 32, ri, j, :], ubf[base:base + 32, j, :], True, True, R=["Bm", "ubf"], W=[pn])
            if d == 0:
                vr, vi = ps[2][:, :], psI
                hro, hio = hbq[0][:], hbq[1][:]
            else:
                vr, vi = ps[2][:, ::-1], psI[:, ::-1]
                hro, hio = hbq[0][:, ::-1], hbq[1][:, ::-1]
            cc, ssn = tb[:, 0, :], tb[:, 1, :]
            TT(dve, bA, cc, vr, ALU.mult, R=[tn, ("ps", 2)], W=["s5A"])
            TT(dve, bB, ssn, vi, ALU.mult, R=[tn, "psb"], W=["s5B"])
            TT(dve, bC, cc, vi, ALU.mult, R=[tn, "psb"], W=["s5C"])
            TT(dve, bD, ssn, vr, ALU.mult, R=[tn, ("ps", 2)], W=["s5D"])
            TT(dve, bA, bA, bB, ALU.add, R=["s5A", "s5B"], W=["s5A"])
            TT(dve, bC, bC, bD, ALU.subtract, R=["s5C", "s5D"], W=["s5C"])
            yield
            rho_bc = s5p[:, q, 0:1].to_broadcast([128, T])
            K.op(dve, lambda e: e.tensor_tensor_scan(bB, rho_bc, bA, cin[:, 0, q:q + 1], ALU.mult, ALU.add), R=["s5A", "s5p", "cin"], W=["s5B"])
            K.op(dve, lambda e: e.tensor_tensor_scan(bD, rho_bc, bC, cin[:, 1, q:q + 1], ALU.mult, ALU.add), R=["s5C", "s5p", "cin"], W=["s5D"])
            yield
            TT(dve, bA, cc, bB, ALU.mult, R=[tn, "s5B"], W=["s5A"])
            TT(dve, bC, ssn, bD, ALU.mult, R=[tn, "s5D"], W=["s5C"])
            TT(dve, hro, bA, bC, ALU.subtract, R=["s5A", "s5C", "arB"], W=[hbn])
            TT(dve, bA, cc, bD, ALU.mult, R=[tn, "s5D"], W=["s5A"])
            TT(dve, bC, ssn, bB, ALU.mult, R=[tn, "s5B"], W=["s5C"])
            TT(dve, hio, bA, bC, ALU.add, R=["s5A", "s5C", "arB"], W=[hbn])
            A(act, hc[:, 0, q:q + 1], bB[:, T - 1:T], AF.Copy, R=["s5B", "cin"], W=["hc"])
            A(act, hc[:, 1, q:q + 1], bD[:, T - 1:T], AF.Copy, R=["s5D", "cin"], W=["hc"])

            def cmm(q=q, j=j, base=base, hbq=hbq, hbn=hbn):
                if base < 64:
                    MM(ps[3][base:base + 32, :], Cm[:, 0, q, 0:32], hbq[0][:], True, False, R=["Cm", hbn], W=[("psY",)])
                    MM(ps[3][base:base + 32, :], Cm[:, 1, q, 0:32], hbq[1][:], False, True, R=["Cm", hbn], W=[("psY",)])
                else:
                    MM(ps[3][64:128, :], Cm[:, 0, q, :], hbq[0][:], base == 64, False, R=["Cm", hbn], W=[("psY",)])
                    MM(ps[3][64:128, :], Cm[:, 1, q, :], hbq[1][:], False, base == 96, R=["Cm", hbn], W=[("psY",)])
                if q % 4 == 3:
                    TT(dve, ysf[:, j, :], ysf[:, j, :], ps[3][:, :], ALU.add, R=[("psY",), "ysf"], W=["ysf"])
            pending.append(cmm)
            if len(pending) > 1:
                pending.pop(0)()
            yield
        while pending:
            pending.pop(0)()
        yield

    def spill(dst, src_ap, R):
        K.dma(dst, src_ap, R=R, W=[("spill",)])

    def passA(l, seg, ti):
        r0 = ti * T
        src = xin[seg] if l == 0 else x1[seg]
        load_norm(src, seg, r0, l, False)
        if ti == 0:
            reset_states()
        elif soft_left(ti, segs[seg] // T):
            scale_states()
        def streamH():
            for hd in range(8):
                def ev(i, m, gi, p, pslot, hd=hd):
                    if i == 0:
                        A(act, sc[0][:, 0:T], p, AF.Copy, R=[pslot], W=["sc0"])
                        spill(sp_q[seg][hd, :, r0:r0 + T], sc[0][:, 0:T], R=["sc0"])
                    elif i == 1:
                        A(act, sc[1][:, 0:T], p, AF.Sigmoid, R=[pslot], W=["sc1"])
                    else:
                        A(act, bvf[:], p, AF.Copy, R=[pslot], W=["bvf"])
                        for s in range(4):
                            TR(ps5b[:, s * 128:(s + 1) * 128], bvf[:, s * 128:(s + 1) * 128], identb[:], R=["bvf", "identb"], W=[("ps", 5)])
                        A(act, vT[:, hd, :, :], ps5b.rearrange("p (s k) -> p s k", k=128), AF.Copy, R=[("ps", 5)], W=["vT"])
                        spill(sp_v[seg][r0:r0 + T, hd * 128:(hd + 1) * 128].rearrange("(s p) v -> p s v", p=128), vT[:, hd, :, :], R=["vT"])
                linear(wb_in[l], 16, [C_Q + hd, C_FF + hd, C_I + hd], hT_k, [(HALO, T)], ev, ["hT"])
                yield
                yield from hg_head(hd, 0)
                A(act, sc[9][:, 0:T], ps[4][:, :], AF.Copy, R=[("ps", 4)], W=["sc9"])
                spill(sp_o[seg][hd, :, r0:r0 + T], sc[9][:, 0:T], R=["sc9"])
                yield

        def streamS():
            def evu(i, m, gi, p, pslot):
                A(act, ubf[:, i, :], p, AF.Copy, R=[pslot], W=["ubf"])
                A(act, ysf[:, i, :], p, AF.Copy, R=[pslot, "vecT"], W=["ysf"], scale=vecT[:, V_SD + i:V_SD + i + 1])
            linear(wb_in[l], 16, [C_DIN + j for j in range(8)], hT_k, [(HALO, T)], evu, ["hT"])
            spill(sp_u[seg][:, :, r0:r0 + T].rearrange("j p t -> p j t"), ubf, R=["ubf"])
            yield
            yield from s5_dir(l, 0)
            spill(sp_y[seg][:, :, r0:r0 + T].rearrange("j p t -> p j t"), ysf, R=["ysf"])
        run_streams([streamS(), streamH()], [SW_S, SW_H])

    def rstd_from(ps_ms, out_sc, R, W):
        TS(dve, out_sc, ps_ms, EPS, None, ALU.add, None, R=R, W=W)
        A(act, out_sc, out_sc, AF.Ln, R=W, W=W)
        A(act, out_sc, out_sc, AF.Exp, R=W, W=W, scale=-0.5)

    def passB(l, seg, ti, ntiles, last_layer):
        r0 = ti * T
        src = xin[seg] if l == 0 else x1[seg]
        load_norm(src, seg, r0, l, True, hm=(1 if soft_left(ti, ntiles) else (2 if soft_right(ti, ntiles) else 0)))
        if ti == ntiles - 1:
            reset_states()
        elif soft_right(ti, ntiles):
            scale_states()
        def streamD():
            K.dma(ubf, sp_u[seg][:, :, r0:r0 + T].rearrange("j p t -> p j t"), R=[("spill",)], W=["ubf"])
            K.dma(ysf, sp_y[seg][:, :, r0:r0 + T].rearrange("j p t -> p j t"), R=[("spill",)], W=["ysf"])
            yield
            yield from s5_dir(l, 1)
            for j in range(8):
                A(act, ysf[:, j, :], ysf[:, j, :], AF.Gelu, R=["ysf"], W=["ysf"])
                A(act, zbf[:, j, :], ysf[:, j, :], AF.Copy, R=["ysf"], W=["zbf"])

            def evglu(i, m, gi, p, pslot):
                A(act, s5buf[0], p, AF.Sigmoid, R=[pslot, "vecT"], W=["arB"], bias=vecT[:, V_GB + i:V_GB + i + 1])
                TT(dve, ysf[:, i, :], ysf[:, i, :], s5buf[0], ALU.mult, R=["ysf", "arB"], W=["ysf"])
            linear(wb_glu[l], 8, list(range(8)), lambda kc: zbf[:, kc, :], [(0, T)], evglu, ["zbf"])

            def evdg(i, m, gi, p, pslot):
                A(act, s5buf[0], p, AF.Silu, R=[pslot], W=["arB"])
                TT(dve, yd[:, i, :], ysf[:, i, :], s5buf[0], ALU.mult, R=["ysf", "arB"], W=["yd"])
            linear(wb_in[l], 16, [C_DGATE + j for j in range(8)], hT_k, [(HALO, T)], evdg, ["hT"])
            yield

        def streamC():
            for hd in range(8):
                K.dma(sc[0][:, 0:T], sp_q[seg][hd, :, r0:r0 + T], R=[("spill",)], W=["sc0"])
                K.dma(vT[:, hd, :, :], sp_v[seg][r0:r0 + T, hd * 128:(hd + 1) * 128].rearrange("(s p) v -> p s v", p=128), R=[("spill",)], W=["vT"])
                K.dma(sc[9][:, 0:T], sp_o[seg][hd, :, r0:r0 + T], R=[("spill",)], W=["sc9"])

                def ev(i, m, gi, p, pslot):
                    A(act, sc[1][:, 0:T], p, AF.Sigmoid, R=[pslot], W=["sc1"])
                linear(wb_in[l], 16, [C_FB + hd], hT_k, [(HALO, T)], ev, ["hT"])
                yield
                yield from hg_head(hd, 1)
                TT(dve, sc[9][:, 0:T], ps[4][:, :], sc[9][:, 0:T], ALU.add, R=[("ps", 4), "sc9"], W=["sc9"])
                A(act, bq[:], sc[9][:, 0:T], AF.Square, R=["sc9"], W=["bq"])
                MM(ps[5][:, :], onesH[:], bq[:], True, True, R=["onesH", "bq"], W=[("ps", 5)])
                rstd_from(ps[5][:, :], sc[2][:, 0:T], R=[("ps", 5)], W=["sc2"])
                STT(dve, sc[3][:, 0:T], sc[9][:, 0:T], vecT[:, V_HG + hd:V_HG + hd + 1], sc[2][:, 0:T], ALU.mult, ALU.mult,
                    R=["sc9", "vecT", "sc2"], W=["sc3"])

                def evg(i, m, gi, p, pslot, hd=hd):
                    A(act, sc[5][:, 0:T], p, AF.Silu, R=[pslot], W=["sc5"])
                    TT(dve, yc[:, hd, :], sc[3][:, 0:T], sc[5][:, 0:T], ALU.mult, R=["sc3", "sc5"], W=["yc"])
                linear(wb_in[l], 16, [C_CGATE + hd], hT_k, [(HALO, T)], evg, ["hT"])
                yield
            yield

        run_streams([streamD(), streamC()], [SW_S, SW_H])
        CG = [(0, T), (T, 2 * HALO)]
        for c in range(8):
            def eva(i, m, gi, p, pslot, c=c):
                c0, ncol = CG[gi]
                if i == 0:
                    A(act, sc[0][:, c0:c0 + ncol], p, AF.Copy, R=[pslot], W=["sc0"])
                else:
                    A(act, sc[1][:, c0:c0 + ncol], p, AF.Sigmoid, R=[pslot], W=["sc1"])
                    TT(dve, ue[:, c, c0:c0 + ncol], sc[0][:, c0:c0 + ncol], sc[1][:, c0:c0 + ncol], ALU.mult, R=["sc0", "sc1"], W=["ue"])
            linear(wb_in[l], 16, [C_AVAL + c, C_AGLU + c], hT_k, CG, eva, ["hT"])
        for c in range(8):
            b = c % 2
            K.dma(cbuf[b][:], cdiag[l][c], R=[("cdiag",)], W=[("cbuf", b)])
            p, pslot = nps()
            for j in range(31):
                MM(p[:, :], cbuf[b][:, j, :], ue[:, c, j + 1:j + 1 + T], j == 0, j == 30, R=[("cbuf", b), "ue"], W=[pslot])
            A(act, ysf[:, c, :], p[:, :], AF.Identity, R=[pslot, "vecT"], W=["ysf"], bias=vecT[:, V_CB + c:V_CB + c + 1])
            A(act, zbf[:, c, :], ysf[:, c, :], AF.Copy, R=["ysf"], W=["zbf"])
        for c in range(8):
            MM(ps[4][:, :], onesW[:], zbf[:, c, :], c == 0, c == 7, R=["onesW", "zbf"], W=[("ps", 4)])
        for c in range(8):
            A(act, bq[:], ysf[:, c, :], AF.Square, R=["ysf"], W=["bq"])
            MM(ps[5][:, :], onesW[:], bq[:], c == 0, c == 7, R=["onesW", "bq"], W=[("ps", 5)])
        A(act, sc[0][:, 0:T], ps[4][:, :], AF.Copy, R=[("ps", 4)], W=["sc0"])
        TT(dve, sc[1][:, 0:T], sc[0][:, 0:T], sc[0][:, 0:T], ALU.mult, R=["sc0"], W=["sc1"])
        TT(dve, sc[1][:, 0:T], ps[5][:, :], sc[1][:, 0:T], ALU.subtract, R=[("ps", 5), "sc1"], W=["sc1"])
        rstd_from(sc[1][:, 0:T], sc[2][:, 0:T], R=["sc1"], W=["sc2"])
        for c in range(8):
            TT(dve, sc[3][:, 0:T], ysf[:, c, :], sc[0][:, 0:T], ALU.subtract, R=["ysf", "sc0"], W=["sc3"])
            STT(dve, sc[3][:, 0:T], sc[3][:, 0:T], vecT[:, V_LNG + c:V_LNG + c + 1], sc[2][:, 0:T], ALU.mult, ALU.mult,
                R=["sc3", "vecT", "sc2"], W=["sc3"])
            A(act, sc[4][:, 0:T], sc[3][:, 0:T], AF.Silu, R=["sc3", "vecT"], W=["sc4"], bias=vecT[:, V_LNB + c:V_LNB + c + 1])

            def evag(i, m, gi, p, pslot, c=c):
                A(act, sc[5][:, 0:T], p, AF.Silu, R=[pslot], W=["sc5"])
                TT(dve, ya[:, c, :], sc[4][:, 0:T], sc[5][:, 0:T], ALU.mult, R=["sc4", "sc5"], W=["ya"])
            linear(wb_in[l], 16, [C_AGATE + c], hT_k, [(HALO, T)], evag, ["hT"])
        def evb(i, m, gi, p, pslot):
            c0, ncol = CG[gi]
            A(act, ue[:, i, c0:c0 + ncol], p, AF.Copy, R=[pslot], W=["ue"])
        linear(wb_in[l], 16, [C_BIN + c for c in range(8)], hT_k, CG, evb, ["hT"])
        for g in range(4):
            K.dma(rcb, rcin[seg][g, r0:r0 + T].partition_broadcast(128), W=["sc8"])
            for mm in range(2):
                c = 2 * g + mm

                def evz(i, m, gi, p, pslot):
                    c0, ncol = CG[gi]
                    A(act, sc[0][:, c0:c0 + ncol], p, AF.Copy, R=[pslot], W=["sc0"])
                linear(wb_pool[l][g], 2, [mm], lambda kc, g=g: ue[:, 2 * g + kc, :], CG, evz, ["ue"])
                z = sc[0]
                a, bb = sc[2], sc[3]
                TT(dve, a[:, 1:TE], z[:, 0:TE - 1], z[:, 1:TE], ALU.add, R=["sc0"], W=["sc2"])
                cur, lo, hi = a, 1, TE
                if g >= 1:
                    TT(dve, bb[:, lo + 1:hi - 1], cur[:, lo:hi - 2], cur[:, lo + 2:hi], ALU.add, R=["sc2"], W=["sc3"])
                    cur, lo, hi = bb, lo + 1, hi - 1
                if g >= 2:
                    TT(dve, a[:, lo + 2:hi - 2], cur[:, lo:hi - 4], cur[:, lo + 4:hi], ALU.add, R=["sc3"], W=["sc2"])
                    cur, lo, hi = a, lo + 2, hi - 2
                if g >= 3:
                    TT(dve, bb[:, lo + 4:hi - 4], cur[:, lo:hi - 8], cur[:, lo + 8:hi], ALU.add, R=["sc2"], W=["sc3"])
                    cur, lo, hi = bb, lo + 4, hi - 4
                cn_ = "sc2" if cur is a else "sc3"
                TT(dve, sc[4][:, 0:T], cur[:, HALO:HALO + T], rcb, ALU.mult, R=[cn_, "sc8"], W=["sc4"])
                TT(dve, sc[4][:, 0:T], sc[4][:, 0:T], z[:, HALO:HALO + T], ALU.subtract, R=["sc4", "sc0"], W=["sc4"])

                def evbg(i, m, gi, p, pslot, c=c):
                    A(act, sc[5][:, 0:T], p, AF.Silu, R=[pslot], W=["sc5"])
                    STT(dve, yb[:, c, :], sc[4][:, 0:T], vecT[:, V_PS + c:V_PS + c + 1], sc[5][:, 0:T], ALU.mult, ALU.mult,
                        R=["sc4", "vecT", "sc5"], W=["yb"])
                linear(wb_in[l], 16, [C_BGATE + c], hT_k, [(HALO, T)], evbg, ["hT"])
        ys = [ya, yb, yc, yd]
        yn = ["ya", "yb", "yc", "yd"]
        for dch in range(16):
            for i in range(4):
                def evr(ii, m, gi, p, pslot, i=i):
                    A(act, sc[5][:, 0:T], p, AF.Sigmoid, R=[pslot], W=["sc5"])
                linear(wb_in[l], 16, [C_R + i * 16 + dch], hT_k, [(HALO, T)], evr, ["hT"])

                def evo(ii, m, gi, p, pslot, i=i):
                    if i == 0:
                        TT(dve, sc[6][:, 0:T], p, sc[5][:, 0:T], ALU.mult, R=[pslot, "sc5"], W=["sc6"])
                    else:
                        TT(dve, sc[7][:, 0:T], p, sc[5][:, 0:T], ALU.mult, R=[pslot, "sc5"], W=["sc7"])
                        TT(dve, sc[6][:, 0:T], sc[6][:, 0:T], sc[7][:, 0:T], ALU.add, R=["sc6", "sc7"], W=["sc6"])
                linear(wb_ao[l][i], 8, [dch], lambda kc, i=i: ys[i][:, kc, :], [(0, T)], evo, [yn[i]])
            A(act, mT[:, dch, :], sc[6][:, 0:T], AF.Copy, R=["sc6"], W=["mT"])
        def evout(i, m, gi, p, pslot):
            A(act, sc[0][:, 0:T], p, AF.Copy, R=[pslot], W=["sc0"])
            for s in range(4):
                TR(psx[:, 0:128], sc[0][:, s * 128:(s + 1) * 128], ident, R=["sc0", "cs"], W=["psx"])
                TT(dve, xt[:, s, i * 128:(i + 1) * 128], xt[:, s, i * 128:(i + 1) * 128], psx[:, 0:128], ALU.add, R=["psx", "xt"], W=["xt"])
        K.dma(xt, src[HALO + r0:HALO + r0 + T, :].rearrange("(s p) d -> p s d", p=128), W=["xt"])
        linear(wb_out[l], 16, list(range(16)), lambda kc: mT[:, kc, :], [(0, T)], evout, ["mT"])
        if not last_layer:
            K.dma(x1[seg][HALO + r0:HALO + r0 + T, :].rearrange("(s p) d -> p s d", p=128), xt, R=["xt"], W=[("x1", seg)])
        else:
            for s in range(4):
                A(act, junk[:, :], xt[:, s, :], AF.Square, R=["xt"], W=["junk", "ss"], accum=ss[:, s:s + 1])
            TS(dve, rstd[:, 0:4], ss[:, 0:4], 1.0 / D, EPS, ALU.mult, ALU.add, R=["ss"], W=["rstd"])
            A(act, rstd[:, 0:4], rstd[:, 0:4], AF.Ln, R=["rstd"], W=["rstd"])
            A(act, rstd[:, 0:4], rstd[:, 0:4], AF.Exp, R=["rstd"], W=["rstd"], scale=-0.5)
            K.dma(fgb, final_g.partition_broadcast(128), W=["fgb"])
            for s in range(4):
                STT(dve, xt[:, s, :], xt[:, s, :], rstd[:, s:s + 1], fgb, ALU.mult, ALU.mult, R=["xt", "rstd", "fgb"], W=["xt"])
            K.dma(yout[seg][r0:r0 + T, :].rearrange("(s p) d -> p s d", p=128), xt, R=["xt"], W=[("yout", seg)])

    for l in range(depth):
        prep_layer(l)
        prep_s5(l, 0)
        for seg in range(nseg):
            nt = segs[seg] // T
            for ti in range(nt):
                passA(l, seg, ti)
        prep_s5(l, 1)
        for seg in range(nseg):
            nt = segs[seg] // T
            for ti in range(nt - 1, -1, -1):
                passB(l, seg, ti, nt, l == depth - 1)
    K.finish()
    return nc


def _consts():
    c = np.zeros((128, 128 + 256 + 256 + 512), np.float32)
    c[:, 0:128] = np.eye(128, dtype=np.float32)
    s = np.arange(128) % 64
    t = np.arange(256) % 64
    c[:, 128:384] = (s[:, None] <= t[None, :]).astype(np.float32)
    c[:, 384:640] = (s[:, None] >= t[None, :]).astype(np.float32)
    rm = np.ones(512, np.float32)
    rm[::64] = 0.0
    c[:, 640:1152] = rm[None, :]
    return c


def _rc(L):
    t = np.arange(L)
    out = np.zeros((4, L), np.float32)
    for g, win in enumerate((2, 4, 8, 16)):
        left = win // 2
        right = win - 1 - left
        lo = np.maximum(t - left, 0)
        hi = np.minimum(t + right, L - 1) + 1
        out[g] = 1.0 / (hi - lo).astype(np.float32)
    return out


_WNAMES = ["norm_g", "w_in", "conv_w", "conv_b", "conv_ln_g", "conv_ln_b", "w_a_out", "pool_w", "pool_scale", "w_b_out",
           "hg_lb", "hg_norm_g", "w_c_out", "s5_a_re", "s5_a_im", "s5_log_dt", "s5_b_re", "s5_b_im", "s5_c_re", "s5_c_im",
           "s5_d", "s5_glu_w", "s5_glu_b", "w_d_out", "w_out", "final_g"]


def run(x_prompt, x_sample, weights, ncores=8):
    Lp, Ls = x_prompt.shape[1], x_sample.shape[1]
    depth = weights["w_in"].shape[0]
    nb_p, nb_s = x_prompt.shape[0], x_sample.shape[0]
    slots = Lp // Ls
    segt = Ls // T
    nc = build([Lp], depth, segt=segt)
    base = {k: np.ascontiguousarray(np.asarray(weights[k], dtype=np.float32)) for k in _WNAMES}
    base["cst"] = _consts()
    rc_p = _rc(Lp)
    rc_s = np.ascontiguousarray(np.tile(_rc(Ls), (1, slots)))
    n_score = ncores - nb_p
    per = -(-nb_s // n_score)
    assert per <= slots
    in_maps, assign = [], []
    for c in range(ncores):
        m = dict(base)
        xp = np.zeros((Lp + 2 * HALO, D), np.float32)
        hm = np.ones((32, 3), np.float32)
        if c < nb_p:
            xp[HALO:HALO + Lp] = x_prompt[c]
            m["rc0"] = rc_p
            cf = 1.0
            assign.append(("p", c))
        else:
            ids = [i for i in range((c - nb_p) * per, min((c - nb_p + 1) * per, nb_s))]
            for k, i in enumerate(ids):
                xp[HALO + k * Ls:HALO + (k + 1) * Ls] = x_sample[i]
            m["rc0"] = rc_s
            cf = 0.0
            assign.append(("s", ids))
        hm[0:16, 1] = cf
        hm[16:32, 2] = cf
        m["x0"] = xp
        m["cfl"] = np.full((128, 1), cf, np.float32)
        m["hmk"] = hm
        in_maps.append(m)
    res = run_bass_kernel_spmd(nc, in_maps, core_ids=list(range(ncores)))
    yp = np.zeros(x_prompt.shape, np.float32)
    ysm = np.zeros(x_sample.shape, np.float32)
    for c in range(ncores):
        kind, ids = assign[c]
        y = res.results[c]["y0"]
        if kind == "p":
            yp[ids] = y
        else:
            for k, i in enumerate(ids):
                ysm[i] = y[k * Ls:(k + 1) * Ls]
    return yp, ysm


def kernel(x_prompt, x_sample, **weights):
    x_prompt = np.asarray(x_prompt, dtype=np.float32)
    x_sample = np.asarray(x_sample, dtype=np.float32)
    return run(x_prompt, x_sample, weights)
```

```python
import numpy as np
import concourse.bass as bass
import concourse.mybir as mybir
from concourse.bass_utils import run_bass_kernel_spmd

F32 = mybir.dt.float32
BF16 = mybir.dt.bfloat16
AF = mybir.ActivationFunctionType
ALU = mybir.AluOpType

D = 2048
WB = 1024
NIN = 20480
T = 512
HALO = 16
TE = T + 2 * HALO
EPS = 1e-6
CH = 64
PAD = 256
POOL_MOD = 1000
SW_S = 1
SW_H = 1
C_AVAL, C_AGLU, C_AGATE, C_BIN, C_BGATE, C_Q, C_FF, C_FB, C_I, C_CGATE, C_DIN, C_DGATE, C_R = (
    0, 8, 16, 24, 32, 40, 48, 56, 64, 72, 80, 88, 96)
V_NG, V_CB, V_LNG, V_LNB, V_PS, V_HG, V_SD, V_GB, V_LB0, V_LB1 = 0, 16, 24, 32, 40, 48, 56, 64, 72, 88
NVEC = 104


class Slot:
    __slots__ = ("w", "r")

    def __init__(self):
        self.w = {}
        self.r = {}


class EngW:
    def __init__(self, nc, eng, name, selfwait):
        self.eng = eng
        self.sem = nc.alloc_semaphore(name)
        self.n = 0
        self.seen = {}
        self.selfwait = selfwait


class Ker:
    def __init__(self, nc):
        self.nc = nc
        self.pe = EngW(nc, nc.tensor, "s_pe", False)
        self.act = EngW(nc, nc.scalar, "s_act", True)
        self.dve = EngW(nc, nc.vector, "s_dve", True)
        self.pool = EngW(nc, nc.gpsimd, "s_pool", True)
        self.sp = EngW(nc, nc.sync, "s_sp", False)
        self.ndsem = 24
        self.dsem = [nc.alloc_semaphore(f"s_dma{i}") for i in range(self.ndsem)]
        self.dval = [0] * self.ndsem
        self.dnext = 0
        self.slots = {}
        self.canon = {"xt": "arA", "ysf": "arA", "ubf": "arA", "zbf": "arA", ("cvt", 0): "arA", ("cvt", 1): "arA",
                      ("cvb", 0): "arA", ("cvb", 1): "arA",
                      "arB": "arB", ("ks", 1): "arB", "ue": "arB", ("cbuf", 0): "arB", ("cbuf", 1): "arB",
                      "hb": "arC", "mT": "arC", "hT": "arD", "fgb": "arD", "junk": "arD", "xh": "arB", ("psb", 0): "psb", ("psb", 1): "psb", "ya": "arY", "yb": "arY", ("kp", 0): "arY", ("kp", 1): "arY"}

    def slot(self, key):
        key = self.canon.get(key, key)
        s = self.slots.get(key)
        if s is None:
            s = Slot()
            self.slots[key] = s
        return s

    def _wait(self, E, R, W):
        deps = {}
        for k in R:
            for i, ev in self.slot(k).w.items():
                if deps.get(i, (None, 0))[1] < ev[1]:
                    deps[i] = ev
        for k in W:
            s = self.slot(k)
            for dd in (s.w, s.r):
                for i, ev in dd.items():
                    if deps.get(i, (None, 0))[1] < ev[1]:
                        deps[i] = ev
        for i, (sem, val) in deps.items():
            if sem is E.sem and not E.selfwait:
                continue
            if E.seen.get(i, 0) >= val:
                continue
            E.eng.wait_ge(sem, val)
            E.seen[i] = val

    def _update(self, ev, R, W):
        i = id(ev[0])
        for k in R:
            self.slot(k).r[i] = ev
        for k in W:
            s = self.slot(k)
            if s.r:
                s.r = {}
                s.w = {}
            s.w[i] = ev

    def op(self, E, fn, R=(), W=()):
        self._wait(E, R, W)
        ins = fn(E.eng)
        E.n += 1
        ins.then_inc(E.sem, 1)
        self._update((E.sem, E.n), R, W)

    def dma(self, out, in_, R=(), W=()):
        E = self.sp
        self._wait(E, R, W)
        k = self.dnext
        self.dnext = (k + 1) % self.ndsem
        sem = self.dsem[k]
        if self.dval[k] > 0 and E.seen.get(id(sem), 0) < self.dval[k]:
            E.eng.wait_ge(sem, self.dval[k])
            E.seen[id(sem)] = self.dval[k]
        E.eng.dma_start(out=out, in_=in_).then_inc(sem, 16)
        self.dval[k] += 16
        self._update((sem, self.dval[k]), R, W)

    def finish(self):
        E = self.sp
        for k in range(self.ndsem):
            if self.dval[k] > 0:
                E.eng.wait_ge(self.dsem[k], self.dval[k])
        for X in (self.pe, self.act, self.dve, self.pool):
            if X.n > 0:
                E.eng.wait_ge(X.sem, X.n)


def build(segs, depth=2, segt=None):
    nc = bass.Bass("TRN2", target_bir_lowering=False)
    K = Ker(nc)
    pe, act, dve, pool = K.pe, K.act, K.dve, K.pool
    nseg = len(segs)

    def din(name, shape, dt=F32):
        return nc.dram_tensor(name, list(shape), dt, kind="ExternalInput").ap()

    def dscr(name, shape, dt=F32):
        return nc.dram_tensor(name, list(shape), dt).ap()

    xin = [din(f"x{s}", [segs[s] + 2 * HALO, D]) for s in range(nseg)]
    yout = [nc.dram_tensor(f"y{s}", [segs[s], D], F32, kind="ExternalOutput").ap() for s in range(nseg)]
    x1 = [dscr(f"x1_{s}", [segs[s] + 2 * HALO, D]) for s in range(nseg)]
    rcin = [din(f"rc{s}", [4, segs[s]]) for s in range(nseg)]
    cst = din("cst", [128, 128 + 256 + 256 + 512])
    cfl_d = din("cfl", [128, 1])
    hmk_d = din("hmk", [32, 3])
    norm_g = din("norm_g", [depth, D]); w_in = din("w_in", [depth, D, NIN])
    conv_w = din("conv_w", [depth, 31, WB]); conv_b = din("conv_b", [depth, WB])
    conv_ln_g = din("conv_ln_g", [depth, WB]); conv_ln_b = din("conv_ln_b", [depth, WB])
    w_a_out = din("w_a_out", [depth, WB, D]); pool_w = din("pool_w", [depth, 4, 256, 256])
    pool_scale = din("pool_scale", [depth, WB]); w_b_out = din("w_b_out", [depth, WB, D])
    hg_lb = din("hg_lb", [depth, 2, WB]); hg_norm_g = din("hg_norm_g", [depth, WB])
    w_c_out = din("w_c_out", [depth, WB, D])
    s5_a_re = din("s5_a_re", [depth, 2, 64, 64]); s5_a_im = din("s5_a_im", [depth, 2, 64, 64])
    s5_log_dt = din("s5_log_dt", [depth, 2, 64])
    s5_b_re = din("s5_b_re", [depth, 2, 64, 64, 16]); s5_b_im = din("s5_b_im", [depth, 2, 64, 64, 16])
    s5_c_re = din("s5_c_re", [depth, 2, 64, 16, 64]); s5_c_im = din("s5_c_im", [depth, 2, 64, 16, 64])
    s5_d = din("s5_d", [depth, WB]); s5_glu_w = din("s5_glu_w", [depth, WB, WB])
    s5_glu_b = din("s5_glu_b", [depth, WB]); w_d_out = din("w_d_out", [depth, WB, D])
    w_out = din("w_out", [depth, D, D]); final_g = din("final_g", [D])

    def wscr(name, Kdim, N):
        return dscr(name, [N // 128, 128, Kdim // 128, 128], BF16)
    wb_in = [wscr(f"wb_in{l}", D, NIN) for l in range(depth)]
    wb_ao = [[wscr(f"wb_o{i}_{l}", WB, D) for i in range(4)] for l in range(depth)]
    wb_out = [wscr(f"wb_out{l}", D, D) for l in range(depth)]
    wb_glu = [wscr(f"wb_glu{l}", WB, WB) for l in range(depth)]
    wb_pool = [[wscr(f"wb_pool{l}_{g}", 256, 256) for g in range(4)] for l in range(depth)]
    cdiag = [dscr(f"cdiag{l}", [8, 128, 31, 128], BF16) for l in range(depth)]
    tabd = [[dscr(f"tabd{l}_{d}", [32, 128, 2, T]) for d in range(2)] for l in range(depth)]
    sp_q = [dscr(f"sp_q{s}", [8, 128, segs[s]]) for s in range(nseg)]
    sp_o = [dscr(f"sp_o{s}", [8, 128, segs[s]]) for s in range(nseg)]
    sp_y = [dscr(f"sp_y{s}", [8, 128, segs[s]]) for s in range(nseg)]
    sp_u = [dscr(f"sp_u{s}", [8, 128, segs[s]], BF16) for s in range(nseg)]
    sp_v = [dscr(f"sp_v{s}", [segs[s], WB], BF16) for s in range(nseg)]

    def sb(name, shape, dt=F32):
        return nc.alloc_sbuf_tensor(name, list(shape), dt)
    cs = sb("cs", [128, 128 + 256 + 256 + 512])
    ident = cs[:, 0:128]
    maskF = cs[:, 128:384]
    maskB = cs[:, 384:640]
    rmask = cs[:, 640:1152]
    identb = sb("identb", [128, 128], BF16)
    onesW = sb("onesW", [128, 128], BF16)
    onesH = sb("onesH", [128, 128], BF16)
    vecT = sb("vecT", [128, NVEC])
    lbv = sb("lbv", [128, 16]); omlb = sb("omlb", [128, 16]); nomlb = sb("nomlb", [128, 16])
    arA = sb("arA", [128, 8192])
    arB = sb("arB", [128, 4352])
    arC = sb("arC", [128, 5120])
    arD = sb("arD", [128, 4352])
    xt = arA[:, :].rearrange("p (s d) -> p s d", d=D)
    ysf = arA[:, 0:4096].rearrange("p (j t) -> p j t", t=T)
    ubf = arA[:, 4096:6144].bitcast(BF16).rearrange("p (j t) -> p j t", t=T)
    zbf = arA[:, 6144:8192].bitcast(BF16).rearrange("p (j t) -> p j t", t=T)
    cvt = [arA[:, i * 2048:(i + 1) * 2048].rearrange("p (k n) -> p k n", n=128) for i in range(2)]
    cvb = [arA[:, 4096 + i * 1024:4096 + (i + 1) * 1024].bitcast(BF16).rearrange("p (k n) -> p k n", n=128) for i in range(2)]
    s5buf = [arB[:, i * 512:(i + 1) * 512] for i in range(4)]
    tbuf = [arB[:, 2048 + i * 1024:2048 + (i + 1) * 1024].rearrange("p (a t) -> p a t", t=T) for i in range(2)]
    ks = [[s5buf[0]]]
    ue = arB[:, 0:2176].bitcast(BF16).rearrange("p (c t) -> p c t", t=TE)
    cb0 = arB[:, 2176:2176 + 1984].bitcast(BF16).rearrange("p (j n) -> p j n", n=128)
    cbuf = [cb0, cb0]
    hb = arC[:, :].bitcast(BF16).rearrange("p (s d) -> p s d", d=D)
    mT = arC[:, 0:4096].bitcast(BF16).rearrange("p (c t) -> p c t", t=T)
    hT = arD[:, :].bitcast(BF16).rearrange("p (c t) -> p c t", t=TE)
    fgb = arD[:, 0:D]
    xh = arB[0:32, 0:D]
    NWB = 3
    wbuf = [sb(f"wbuf{i}", [128, 16, 128], BF16) for i in range(NWB)]
    ss = sb("ss", [128, 8]); rstd = sb("rstd", [128, 8])
    junk = arD[:, 2048:3072].bitcast(BF16)
    NSC = 10
    sc = [sb(f"sc{i}", [128, TE]) for i in range(NSC)]
    bq = sb("bq", [128, T], BF16); bk = sb("bk", [128, T], BF16); bvf = sb("bvf", [128, T], BF16)
    vT = sb("vT", [128, 8, 4, 128], BF16)
    kT = sb("kT", [128, 4, 128], BF16)
    At = sb("At", [128, 4, CH], BF16)
    S32 = sb("S32", [128, 8, 128]); Sbf = sb("Sbf", [128, 8, 128], BF16); Tm = sb("Tm", [128, 128])
    Hb = [[sb(f"Hb{a}{b}", [128, T], BF16) for b in range(2)] for a in range(2)]
    s5p = sb("s5p", [128, 32, 32])
    hc = sb("hc", [128, 2, 32]); cin = sb("cin", [128, 2, 32]); tmp32 = sb("tmp32", [128, 4, 32])
    Bm = sb("Bm", [128, 2, 16, 128], BF16)
    Cm = sb("Cm", [128, 2, 32, 64], BF16)
    stg = sb("stg", [128, 128])
    stgA = sb("stgA", [128, 128]); stgB = sb("stgB", [128, 128])
    stg2 = sb("stg2", [128, 256])
    cfl = sb("cflag", [128, 1]); hmk = sb("hmask", [32, 3])
    tmpi = sb("tmpi", [128, 32], mybir.dt.int32)
    arY = sb("arY", [128, 4096])
    ya = arY[:, 0:2048].bitcast(BF16).rearrange("p (c t) -> p c t", t=T)
    yb = arY[:, 2048:4096].bitcast(BF16).rearrange("p (c t) -> p c t", t=T)
    ks2 = [[arY[:, (2 * a + b) * 1024:(2 * a + b + 1) * 1024] for b in range(2)] for a in range(2)]
    yc = sb("yc", [128, 8, T], BF16); yd = sb("yd", [128, 8, T], BF16)
    rcb = sc[8][:, 0:T]
    print("sbuf remaining", nc.sbuf_bytes_remaining)
    ps = [nc.alloc_psum_tensor(f"ps{i}", [128, T], F32) for i in range(6)]
    psb = nc.alloc_psum_tensor("psb", [128, 2 * T], BF16)
    psx = nc.alloc_psum_tensor("psx", [128, T], F32)
    ps5b = ps[5][:, 256:512].bitcast(BF16)
    pcnt = [0]

    def nps():
        i = pcnt[0] % 2
        pcnt[0] += 1
        return ps[i], ("ps", i)

    def A(E, out, in_, func, R, W, bias=None, scale=None, accum=None):
        kw = {}
        if bias is not None:
            kw["bias"] = bias
        if scale is not None:
            kw["scale"] = scale
        if accum is not None:
            kw["accum_out"] = accum
        K.op(act, lambda e: e.activation(out=out, in_=in_, func=func, **kw), R, W)

    def TS(E, out, in0, s1, s2, op0, op1, R, W):
        if s2 is None:
            K.op(E, lambda e: e.tensor_scalar(out, in0, s1, None, op0), R, W)
        else:
            K.op(E, lambda e: e.tensor_scalar(out, in0, s1, s2, op0, op1), R, W)

    def TT(E, out, in0, in1, op, R, W):
        K.op(E, lambda e: e.tensor_tensor(out, in0, in1, op), R, W)

    def STT(E, out, in0, scalar, in1, op0, op1, R, W):
        K.op(E, lambda e: e.scalar_tensor_tensor(out, in0, scalar, in1, op0, op1), R, W)

    def MM(out, lhsT, rhs, start, stop, R, W):
        K.op(pe, lambda e: e.matmul(out, lhsT, rhs, start=start, stop=stop), R, W)

    def TR(out, in_, idn, R, W):
        K.op(pe, lambda e: e.transpose(out, in_, idn), R, W)

    wcnt = [0]

    def linear(wb, nk, mlist, rhs_fn, ncols_list, evac, Rin):
        n = len(mlist)
        loaded = {}

        def load(i):
            b = wcnt[0] % NWB
            wcnt[0] += 1
            K.dma(wbuf[b][:, 0:nk, :], wb[mlist[i]], R=[], W=[("wbuf", b)])
            loaded[i] = b
        for i in range(min(NWB - 1, n)):
            load(i)
        for i in range(n):
            if i + NWB - 1 < n:
                load(i + NWB - 1)
            b = loaded.pop(i)
            for gi, (c0, ncol) in enumerate(ncols_list):
                p, pslot = nps()
                for kc in range(nk):
                    MM(p[:, 0:ncol], wbuf[b][:, kc, :], rhs_fn(kc)[:, c0:c0 + ncol], kc == 0, kc == nk - 1,
                       R=[("wbuf", b)] + Rin, W=[pslot])
                evac(i, mlist[i], gi, p[:, 0:ncol], pslot)

    ccnt = [0]

    def convert(src, Kdim, N, dst):
        nk = Kdim // 128
        srcv = src.rearrange("(kc kp) n -> kp kc n", kp=128)
        dstv = dst.rearrange("m kp kc j -> kp m kc j")
        for n0 in range(0, N, 128):
            i = ccnt[0] % 2
            ccnt[0] += 1
            K.dma(cvt[i][:, 0:nk, :], srcv[:, :, n0:n0 + 128], R=[], W=[("cvt", i)])
            eng = act if (ccnt[0] % 2 == 0) else dve
            if eng is act:
                A(act, cvb[i][:, 0:nk, :], cvt[i][:, 0:nk, :], AF.Copy, R=[("cvt", i)], W=[("cvb", i)])
            else:
                K.op(dve, lambda e: e.tensor_copy(cvb[i][:, 0:nk, :], cvt[i][:, 0:nk, :]), R=[("cvt", i)], W=[("cvb", i)])
            K.dma(dstv[:, n0 // 128, :, :], cvb[i][:, 0:nk, :], R=[("cvb", i)], W=[("wdram",)])

    K.dma(cs[:], cst[:, :], W=["cs"])
    K.dma(cfl[:], cfl_d[:, :], W=["cfl"])
    K.dma(hmk[:], hmk_d[:, :], W=["hmk"])
    K.op(dve, lambda e: e.tensor_copy(identb[:], ident), R=["cs"], W=["identb"])
    K.op(dve, lambda e: e.memset(ss[:], 1.0), W=["ss"])
    K.op(dve, lambda e: e.memset(onesW[:], 1.0 / 1024.0), W=["onesW"])
    K.op(dve, lambda e: e.memset(onesH[:], 1.0 / 128.0), W=["onesH"])
    for s in range(nseg):
        for off in (0, HALO + segs[s]):
            K.op(dve, lambda e: e.memset(xh[0:HALO, :], 0.0), W=["xh"])
            K.dma(x1[s][off:off + HALO, :], xh[0:HALO, :], R=["xh"], W=[("x1", s)])

    for l in range(depth):
        convert(w_in[l], D, NIN, wb_in[l])
        for i, wsrc in enumerate((w_a_out, w_b_out, w_c_out, w_d_out)):
            convert(wsrc[l], WB, D, wb_ao[l][i])
        convert(w_out[l], D, D, wb_out[l])
        convert(s5_glu_w[l], WB, WB, wb_glu[l])
        for g in range(4):
            convert(pool_w[l, g], 256, 256, wb_pool[l][g])

    def prep_layer(l):
        rows = [(norm_g[l], 16), (conv_b[l], 8), (conv_ln_g[l], 8), (conv_ln_b[l], 8), (pool_scale[l], 8),
                (hg_norm_g[l], 8), (s5_d[l], 8), (s5_glu_b[l], 8), (hg_lb[0, 0], 8), (hg_lb[0, 1], 8),
                (hg_lb[1 if depth > 1 else 0, 0], 8), (hg_lb[1 if depth > 1 else 0, 1], 8)]
        r = 0
        K.op(dve, lambda e: e.memset(stg[:], 0.0), W=["stg"])
        for v, n in rows:
            K.dma(stg[r:r + n, :], v.rearrange("(c p) -> c p", p=128), W=["stg"])
            r += n
        TR(psx[:, 0:128], stg[:, :], ident, R=["stg", "cs"], W=["psx"])
        K.op(dve, lambda e: e.tensor_copy(vecT[:], psx[:, 0:NVEC]), R=["psx"], W=["vecT"])
        if l == 0:
            K.op(dve, lambda e: e.memset(lbv[:], 0.0), W=["lbv"])
        else:
            TT(dve, lbv[:], vecT[:, V_LB1:V_LB1 + 16], vecT[:, V_LB0:V_LB0 + 16], ALU.subtract, R=["vecT"], W=["lbv"])
            A(act, lbv[:], lbv[:], AF.Sigmoid, R=["lbv"], W=["lbv"])
        TS(dve, omlb[:], lbv[:], -1.0, 1.0, ALU.mult, ALU.add, R=["lbv"], W=["omlb"])
        TS(dve, nomlb[:], omlb[:], -1.0, None, ALU.mult, None, R=["omlb"], W=["omlb"])
        for half in range(2):
            nr = 128 if half == 0 else 31 * 8 - 128
            K.op(dve, lambda e: e.memset(stg[:], 0.0), W=["stg"])
            src = conv_w[l].rearrange("j (c p) -> (j c) p", p=128)
            K.dma(stg[0:nr, :], src[half * 128:half * 128 + nr, :], W=["stg"])
            TR(psx[:, 0:128], stg[:, :], ident, R=["stg", "cs"], W=["psx"])
            K.op(dve, lambda e: e.tensor_copy(stg2[:, half * 128:half * 128 + 128], psx[:, 0:128]), R=["psx"], W=["stg2"])
        for c in range(8):
            b = c % 2
            for j in range(31):
                col = j * 8 + c
                TS(dve, cbuf[b][:, j, :], ident, stg2[:, col:col + 1], None, ALU.mult, None, R=["stg2", "cs"], W=[("cbuf", b)])
            K.dma(cdiag[l][c], cbuf[b][:], R=[("cbuf", b)], W=[("cdiag",)])

    def prep_s5(l, d):
        def ldT(src, dstcol):
            K.op(dve, lambda e: e.memset(stg[:], 0.0), W=["stg"])
            K.dma(stg[0:32, :], src.rearrange("(q two) p -> q (two p)", two=2), W=["stg"])
            TR(psx[:, 0:128], stg[:, :], ident, R=["stg", "cs"], W=["psx"])
            K.op(dve, lambda e: e.tensor_copy(tmp32[:, dstcol, :], psx[:, 0:32]), R=["psx"], W=["tmp32"])
        ldT(s5_a_re[l, d], 0)
        ldT(s5_a_im[l, d], 1)
        K.op(dve, lambda e: e.memset(stg[:], 0.0), W=["stg"])
        K.dma(stg2[0:32, 0:2], s5_log_dt[l, d].rearrange("(q two) -> q two", two=2), W=["stg2"])
        for two in range(2):
            TS(dve, stg[0:32, two * 64:(two + 1) * 64], stg[0:32, two * 64:(two + 1) * 64], stg2[0:32, two:two + 1], None,
               ALU.add, None, R=["stg", "stg2"], W=["stg"])
        TR(psx[:, 0:128], stg[:, :], ident, R=["stg", "cs"], W=["psx"])
        A(act, tmp32[:, 2, :], psx[:, 0:32], AF.Exp, R=["psx"], W=["tmp32"])
        are, aim, dt = tmp32[:, 0, :], tmp32[:, 1, :], tmp32[:, 2, :]
        t3 = tmp32[:, 3, :]
        R_, W_ = ["tmp32", "s5p"], ["tmp32", "s5p"]
        mag, ang, sn, cn = s5p[:, :, 27], s5p[:, :, 28], s5p[:, :, 29], s5p[:, :, 30]
        TT(dve, mag, dt, are, ALU.mult, R_, W_)
        A(act, mag, mag, AF.Exp, R_, W_)
        TT(dve, ang, dt, aim, ALU.mult, R_, W_)
        TWO_PI = 2.0 * np.pi

        def sin_of(dst, phase):
            TS(dve, dst, ang, 1.0 / TWO_PI, phase / TWO_PI, ALU.mult, ALU.add, R_, W_)
            K.op(dve, lambda e: e.tensor_copy(tmpi[:], dst), R_ + ["tmpi"], W_ + ["tmpi"])
            K.op(dve, lambda e: e.tensor_copy(t3, tmpi[:]), R_ + ["tmpi"], W_)
            TT(dve, dst, dst, t3, ALU.subtract, R_, W_)
            TS(dve, t3, dst, 0.5, None, ALU.is_gt, None, R_, W_)
            TT(dve, dst, dst, t3, ALU.subtract, R_, W_)
            TS(dve, t3, dst, -0.5, None, ALU.is_lt, None, R_, W_)
            TT(dve, dst, dst, t3, ALU.add, R_, W_)
            A(act, dst, dst, AF.Sin, R_, W_, scale=TWO_PI)
        sin_of(sn, 0.0)
        sin_of(cn, np.pi / 2)
        lre, lim = s5p[:, :, 1], s5p[:, :, 2]
        TT(dve, lre, mag, cn, ALU.mult, R_, W_)
        TT(dve, lim, mag, sn, ALU.mult, R_, W_)
        K.op(dve, lambda e: e.tensor_copy(s5p[:, :, 0], mag), R_, W_)
        K.op(dve, lambda e: e.tensor_copy(s5p[:, :, 8], cn), R_, W_)
        K.op(dve, lambda e: e.tensor_copy(s5p[:, :, 9], sn), R_, W_)
        den, xm, gre, gim = s5p[:, :, 27], s5p[:, :, 28], s5p[:, :, 29], s5p[:, :, 30]
        TT(dve, den, are, are, ALU.mult, R_, W_)
        TT(dve, t3, aim, aim, ALU.mult, R_, W_)
        TT(dve, den, den, t3, ALU.add, R_, W_)
        K.op(dve, lambda e: e.reciprocal(den, den), R_, W_)
        TS(dve, xm, lre, -1.0, None, ALU.add, None, R_, W_)
        TT(dve, gre, xm, are, ALU.mult, R_, W_)
        TT(dve, t3, lim, aim, ALU.mult, R_, W_)
        TT(dve, gre, gre, t3, ALU.add, R_, W_)
        TT(dve, gre, gre, den, ALU.mult, R_, W_)
        TT(dve, gim, lim, are, ALU.mult, R_, W_)
        TT(dve, t3, xm, aim, ALU.mult, R_, W_)
        TT(dve, gim, gim, t3, ALU.subtract, R_, W_)
        TT(dve, gim, gim, den, ALU.mult, R_, W_)
        for k in range(1, 10):
            pr, pi_, qr, qi = s5p[:, :, 6 + 2 * k], s5p[:, :, 7 + 2 * k], s5p[:, :, 8 + 2 * k], s5p[:, :, 9 + 2 * k]
            TT(dve, qr, pr, pr, ALU.mult, R_, W_)
            TT(dve, t3, pi_, pi_, ALU.mult, R_, W_)
            TT(dve, qr, qr, t3, ALU.subtract, R_, W_)
            TT(dve, qi, pr, pi_, ALU.mult, R_, W_)
            TS(dve, qi, qi, 2.0, None, ALU.mult, None, R_, W_)
        TS(dve, s5p[:, :, 3], s5p[:, :, 9], -1.0, None, ALU.mult, None, R_, W_)
        Cc = arA[:, 0:4096].rearrange("p (q t) -> p q t", t=T)
        Ss = arA[:, 4096:8192].rearrange("p (q t) -> p q t", t=T)
        tg = arB[:, 0:2048].rearrange("p (q t) -> p q t", t=256)
        RW = ["arA", "arB", "s5p"]
        for qg in range(4):
            q0 = 8 * qg
            K.op(dve, lambda e: e.memset(Cc[:, :, 0:1], 1.0), R=RW, W=RW[:2])
            K.op(dve, lambda e: e.memset(Ss[:, :, 0:1], 0.0), R=RW, W=RW[:2])
            for k in range(9):
                n = 1 << k
                ur = s5p[:, q0:q0 + 8, 8 + 2 * k:9 + 2 * k].to_broadcast([128, 8, n])
                ui = s5p[:, q0:q0 + 8, 9 + 2 * k:10 + 2 * k].to_broadcast([128, 8, n])
                TT(dve, Cc[:, :, n:2 * n], Cc[:, :, 0:n], ur, ALU.mult, RW, RW[:2])
                TT(dve, tg[:, :, 0:n], Ss[:, :, 0:n], ui, ALU.mult, RW, RW[:2])
                TT(dve, Cc[:, :, n:2 * n], Cc[:, :, n:2 * n], tg[:, :, 0:n], ALU.subtract, RW, RW[:2])
                TT(dve, Ss[:, :, n:2 * n], Cc[:, :, 0:n], ui, ALU.mult, RW, RW[:2])
                TT(dve, tg[:, :, 0:n], Ss[:, :, 0:n], ur, ALU.mult, RW, RW[:2])
                TT(dve, Ss[:, :, n:2 * n], Ss[:, :, n:2 * n], tg[:, :, 0:n], ALU.add, RW, RW[:2])
            dstv = tabd[l][d][q0:q0 + 8].rearrange("q p a t -> p q a t")
            K.dma(dstv[:, :, 0, :], Cc, R=RW[:2], W=[("tabd",)])
            K.dma(dstv[:, :, 1, :], Ss, R=RW[:2], W=[("tabd",)])
        K.op(dve, lambda e: e.memset(Bm[:], 0.0), W=["Bm"])
        K.op(dve, lambda e: e.memset(Cm[:], 0.0), W=["Cm"])
        for j in range(8):
            K.op(dve, lambda e: e.memset(stgA[:], 0.0), W=["stgA"])
            K.op(dve, lambda e: e.memset(stgB[:], 0.0), W=["stgB"])
            for qq in range(4):
                q = 4 * j + qq
                base = 32 * qq
                K.dma(stg2[:, 0:16], s5_b_re[l, d, 2 * q:2 * q + 2].rearrange("g p c -> (g p) c"), W=["stg2"])
                K.dma(stg2[:, 16:32], s5_b_im[l, d, 2 * q:2 * q + 2].rearrange("g p c -> (g p) c"), W=["stg2"])
                br, bi = stg2[:, 0:16], stg2[:, 16:32]
                o1, o2 = stg2[:, 32:48], stg2[:, 48:64]
                grq, giq = s5p[:, q, 29:30], s5p[:, q, 30:31]
                Rq, Wq = ["stg2", "s5p"], ["stg2"]
                TS(dve, o1, bi, giq, -1.0, ALU.mult, ALU.mult, Rq, Wq)
                STT(dve, o1, br, grq, o1, ALU.mult, ALU.add, Rq, Wq)
                TS(dve, o2, br, giq, None, ALU.mult, None, Rq, Wq)
                STT(dve, o2, bi, grq, o2, ALU.mult, ALU.add, Rq, Wq)
                for o, st, sn_ in ((o1, stgA, "stgA"), (o2, stgB, "stgB")):
                    K.op(dve, lambda e: e.tensor_copy(st[0:64, base:base + 16], o[0:64, :]), R=["stg2"], W=[sn_])
                    K.op(dve, lambda e: e.tensor_copy(st[64:128, base + 16:base + 32], o[64:128, :]), R=["stg2"], W=[sn_])
            for ri, (st, sn_) in enumerate(((stgA, "stgA"), (stgB, "stgB"))):
                TR(psx[:, 0:128], st[:, :], ident, R=[sn_, "cs"], W=["psx"])
                K.op(dve, lambda e: e.tensor_copy(Bm[:, ri, j, :], psx[:, 0:128]), R=["psx"], W=["Bm"])
                K.op(dve, lambda e: e.tensor_copy(Bm[:, ri, 8 + j, :], psx[:, 0:128]), R=["psx"], W=["Bm"])
                K.op(dve, lambda e: e.memset(Bm[64:96, ri, 8 + j, :], 0.0), W=["Bm"])
            for ri, csrc in enumerate((s5_c_re, s5_c_im)):
                K.op(dve, lambda e: e.memset(stgA[:], 0.0), W=["stgA"])
                for qq in range(4):
                    q = 4 * j + qq
                    K.dma(stgA[32 * qq:32 * qq + 16, 0:64], csrc[l, d, 2 * q], W=["stgA"])
                    K.dma(stgA[32 * qq + 16:32 * qq + 32, 64:128], csrc[l, d, 2 * q + 1], W=["stgA"])
                TR(psx[:, 0:128], stgA[:, :], ident, R=["stgA", "cs"], W=["psx"])
                for qq in range(4):
                    co = 32 if qq == 3 else 0
                    cdst = Cm[:, ri, 4 * j + qq, co:co + 32]
                    TS(dve, cdst, psx[:, 32 * qq:32 * qq + 32], 1.0 if ri == 0 else -1.0, None, ALU.mult, None, R=["psx"], W=["Cm"])

    def load_norm(src, seg, r0, l, ext, hm=0):
        K.dma(xt, src[HALO + r0:HALO + r0 + T, :].rearrange("(s p) d -> p s d", p=128), W=["xt"])
        nsub = 4
        if ext:
            K.dma(xh[0:HALO, :], src[r0:r0 + HALO, :], W=["xh"])
            K.dma(xh[HALO:2 * HALO, :], src[HALO + r0 + T:HALO + r0 + T + HALO, :], W=["xh"])
            nsub = 5
        for s in range(nsub):
            np_ = 128 if s < 4 else 32
            xin_ = xt[:, s, :] if s < 4 else xh[:, :]
            rs = ["xt"] if s < 4 else ["xh"]
            A(act, junk[0:np_, :], xin_, AF.Square, R=rs, W=["junk", "ss"], accum=ss[0:np_, s:s + 1])
        TS(dve, rstd[:, 0:nsub], ss[:, 0:nsub], 1.0 / D, EPS, ALU.mult, ALU.add, R=["ss"], W=["rstd"])
        A(act, rstd[:, 0:nsub], rstd[:, 0:nsub], AF.Ln, R=["rstd"], W=["rstd"])
        A(act, rstd[:, 0:nsub], rstd[:, 0:nsub], AF.Exp, R=["rstd"], W=["rstd"], scale=-0.5)
        if ext and hm != 0:
            TT(dve, rstd[0:32, 4:5], rstd[0:32, 4:5], hmk[0:32, hm:hm + 1], ALU.mult, R=["rstd", "hmk"], W=["rstd"])
        for s in range(nsub):
            np_ = 128 if s < 4 else 32
            xin_ = xt[:, s, :] if s < 4 else xh[:, :]
            rs = ["xt"] if s < 4 else ["xh"]
            A(act, hb[0:np_, s, :], xin_, AF.Copy, R=rs + ["rstd"], W=["hb"], scale=rstd[0:np_, s:s + 1])
        for dc in range(16):
            half = dc % 2
            pb = psb[:, 0:T] if half == 0 else ps[5][:, 0:256].bitcast(BF16)
            pbn = "psb" if half == 0 else ("ps", 5)
            for s in range(4):
                TR(pb[:, s * 128:(s + 1) * 128], hb[:, s, dc * 128:(dc + 1) * 128], identb[:], R=["hb", "identb"], W=[pbn])
            A(act, hT[:, dc, HALO:HALO + T], pb, AF.Copy, R=[pbn, "vecT"], W=["hT"], scale=vecT[:, V_NG + dc:V_NG + dc + 1])
            if ext:
                TR(psx[:, 0:32].bitcast(BF16)[:, 0:32], hb[0:32, 4, dc * 128:(dc + 1) * 128], identb[0:32, 0:32], R=["hb", "identb"], W=["psx"])
                pxb = psx[:, 0:32].bitcast(BF16)
                A(act, hT[:, dc, 0:HALO], pxb[:, 0:HALO], AF.Copy, R=["psx", "vecT"], W=["hT"], scale=vecT[:, V_NG + dc:V_NG + dc + 1])
                A(act, hT[:, dc, HALO + T:TE], pxb[:, HALO:2 * HALO], AF.Copy, R=["psx", "vecT"], W=["hT"], scale=vecT[:, V_NG + dc:V_NG + dc + 1])

    def hT_k(kc):
        return hT[:, kc, :]

    def reset_states():
        K.op(dve, lambda e: e.memset(S32[:], 0.0), W=["S32"])
        K.op(dve, lambda e: e.memset(Sbf[:], 0.0), W=["Sbf"])
        K.op(dve, lambda e: e.memset(hc[:], 0.0), W=["hc"])

    def scale_states():
        TS(dve, S32[:].rearrange("p h k -> p (h k)"), S32[:].rearrange("p h k -> p (h k)"), cfl[:, 0:1], None, ALU.mult, None, R=["S32", "cfl"], W=["S32"])
        A(act, Sbf[:].rearrange("p h k -> p (h k)"), S32[:].rearrange("p h k -> p (h k)"), AF.Copy, R=["S32"], W=["Sbf"])
        TS(dve, hc[:].rearrange("p a q -> p (a q)"), hc[:].rearrange("p a q -> p (a q)"), cfl[:, 0:1], None, ALU.mult, None, R=["hc", "cfl"], W=["hc"])

    def soft_left(ti, nt):
        return segt is not None and ti > 0 and ti % segt == 0

    def soft_right(ti, nt):
        return segt is not None and ti < nt - 1 and ti % segt == segt - 1

    def hg_head(hd, d):
        q_, sg, f_, lf, b_, e1, e2, k_ = (sc[i][:, 0:T] for i in range(8))
        lcol = hd if d == 0 else 8 + hd
        A(act, f_, sg, AF.Identity, R=["sc1", "omlb", "lbv"], W=["sc2"], scale=omlb[:, lcol:lcol + 1], bias=lbv[:, lcol:lcol + 1])
        A(act, lf, f_, AF.Ln, R=["sc2"], W=["sc3"])
        A(act, k_, sg, AF.Identity, R=["sc1", "omlb"], W=["sc7"], scale=nomlb[:, lcol:lcol + 1], bias=omlb[:, lcol:lcol + 1])
        yield
        K.op(dve, lambda e: e.tensor_tensor_scan(b_, rmask, lf, 0.0, ALU.mult, ALU.add), R=["cs", "sc3"], W=["sc4"])
        yield
        bend = sc[4][:, 0:T].rearrange("p (c t) -> p c t", t=CH)[:, :, CH - 1]
        A(act, sc[8][:, 0:8], bend, AF.Exp, R=["sc4"], W=["sc8"])
        if d == 0:
            A(act, e1, b_, AF.Exp, R=["sc4"], W=["sc5"])
            A(act, e2, b_, AF.Exp, R=["sc4"], W=["sc6"], scale=-1.0)
        else:
            TT(dve, b_, b_, lf, ALU.subtract, R=["sc4", "sc3"], W=["sc4"])
            A(act, e1, b_, AF.Exp, R=["sc4"], W=["sc5"], scale=-1.0)
            A(act, e2, b_, AF.Exp, R=["sc4"], W=["sc6"])
        yield
        TT(dve, bq[:], q_, e1, ALU.mult, R=["sc0", "sc5"], W=["bq"])
        TT(dve, bk[:], k_, e2, ALU.mult, R=["sc7", "sc6"], W=["bk"])
        for s in range(4):
            TR(ps5b[:, s * 128:(s + 1) * 128], bk[:, s * 128:(s + 1) * 128], identb[:], R=["bk", "identb"], W=[("ps", 5)])
        A(act, kT[:], ps5b.rearrange("p (s k) -> p s k", k=128), AF.Copy, R=[("ps", 5)], W=["kT"])
        yield
        for c in range(8):
            h64 = (c % 2) * 64
            MM(ps[5][h64:h64 + 64, (c // 2) * CH:(c // 2 + 1) * CH], bk[:, c * CH:(c + 1) * CH], bq[:, c * CH:(c + 1) * CH],
               True, True, R=["bk", "bq"], W=[("ps", 5)])
        yield
        msk = maskF if d == 0 else maskB
        TT(dve, At[:], ps[5][:, 0:256].rearrange("p (a t) -> p a t", t=CH), msk.rearrange("p (a t) -> p a t", t=CH),
           ALU.mult, R=[("ps", 5), "cs"], W=["At"])
        order = range(8) if d == 0 else range(7, -1, -1)
        for c in order:
            h64 = (c % 2) * 64
            et = sc[8][:, c:c + 1]
            if d == 1:
                TS(dve, S32[:, hd, :], S32[:, hd, :], et, None, ALU.mult, None, R=["S32", "sc8"], W=["S32"])
                A(act, Sbf[:, hd, :], S32[:, hd, :], AF.Copy, R=["S32"], W=["Sbf"])
            oc = ps[4][:, c * CH:(c + 1) * CH]
            MM(oc, vT[h64:h64 + 64, hd, c // 2, :], At[h64:h64 + 64, c // 2, :], True, False, R=["vT", "At"], W=[("ps", 4)])
            MM(oc, Sbf[:, hd, :], bq[:, c * CH:(c + 1) * CH], False, True, R=["Sbf", "bq"], W=[("ps", 4)])
            MM(psx[:, 0:128], kT[h64:h64 + 64, c // 2, :], vT[h64:h64 + 64, hd, c // 2, :], True, True, R=["kT", "vT"], W=["psx"])
            yield
            if d == 0:
                TT(dve, Tm[:], psx[:, 0:128], S32[:, hd, :], ALU.add, R=["psx", "S32"], W=["Tm"])
                TS(dve, S32[:, hd, :], Tm[:], et, None, ALU.mult, None, R=["Tm", "sc8"], W=["S32"])
                A(act, Sbf[:, hd, :], S32[:, hd, :], AF.Copy, R=["S32"], W=["Sbf"])
            else:
                TT(dve, S32[:, hd, :], psx[:, 0:128], S32[:, hd, :], ALU.add, R=["psx", "S32"], W=["S32"])
            yield

    def run_streams(gens, weights):
        alive = [True] * len(gens)
        while any(alive):
            for gi, g in enumerate(gens):
                if not alive[gi]:
                    continue
                for _ in range(weights[gi]):
                    try:
                        next(g)
                    except StopIteration:
                        alive[gi] = False
                        break

    def s5_dir(l, d):
        ETc, ETs = s5p[:, :, 26], s5p[:, :, 27]
        R_, W_ = ["s5p", "hc", "cin", "tmp32"], ["cin", "tmp32"]
        TT(dve, cin[:, 0, :], ETc, hc[:, 0, :], ALU.mult, R_, W_)
        TT(dve, tmp32[:, 0, :], ETs, hc[:, 1, :], ALU.mult, R_, W_)
        TT(dve, cin[:, 0, :], cin[:, 0, :], tmp32[:, 0, :], ALU.subtract, R_, W_)
        TT(dve, cin[:, 1, :], ETc, hc[:, 1, :], ALU.mult, R_, W_)
        TT(dve, tmp32[:, 0, :], ETs, hc[:, 0, :], ALU.mult, R_, W_)
        TT(dve, cin[:, 1, :], cin[:, 1, :], tmp32[:, 0, :], ALU.add, R_, W_)
        bA, bB, bC, bD = s5buf
        psI = psb[:, :].bitcast(F32)
        K.dma(tbuf[0], tabd[l][d][0], R=[("tabd",)], W=["arB", ("tb", 0)])
        pending = []
        for q in range(32):
            j, base = q // 4, 32 * (q % 4)
            tb = tbuf[q % 2]
            tn = ("tb", q % 2)
            if q + 1 < 32:
                K.dma(tbuf[(q + 1) % 2], tabd[l][d][q + 1], R=[("tabd",)], W=[("tb", (q + 1) % 2)])
            hsel = q % 2
            hbq = Hb[hsel]
            hbn = ("Hb", hsel)
            for ri, (pt, pn) in enumerate(((ps[2][:, :], ("ps", 2)), (psI, "psb"))):
                if base == 96:
                    MM(pt, Bm[64:128, ri, 8 + j, :], ubf[64:128, j, :], True, True, R=["Bm", "ubf"], W=[pn])
                else:
                    MM(pt, Bm[base:base + 32, ri, j, :], ubf[base:base + 32, j, :], True, True, R=["Bm", "ubf"], W=[pn])
            if d == 0:
                vr, vi = ps[2][:, :], psI
                hro, hio = hbq[0][:], hbq[1][:]
            else:
                vr, vi = ps[2][:, ::-1], psI[:, ::-1]
                hro, hio = hbq[0][:, ::-1], hbq[1][:, ::-1]
            cc, ssn = tb[:, 0, :], tb[:, 1, :]
            TT(dve, bA, cc, vr, ALU.mult, R=[tn, ("ps", 2)], W=["s5A"])
            TT(dve, bB, ssn, vi, ALU.mult, R=[tn, "psb"], W=["s5B"])
            TT(dve, bC, cc, vi, ALU.mult, R=[tn, "psb"], W=["s5C"])
            TT(dve, bD, ssn, vr, ALU.mult, R=[tn, ("ps", 2)], W=["s5D"])
            TT(dve, bA, bA, bB, ALU.add, R=["s5A", "s5B"], W=["s5A"])
            TT(dve, bC, bC, bD, ALU.subtract, R=["s5C", "s5D"], W=["s5C"])
            yield
            rho_bc = s5p[:, q, 0:1].to_broadcast([128, T])
            K.op(dve, lambda e: e.tensor_tensor_scan(bB, rho_bc, bA, cin[:, 0, q:q + 1], ALU.mult, ALU.add), R=["s5A", "s5p", "cin"], W=["s5B"])
            K.op(dve, lambda e: e.tensor_tensor_scan(bD, rho_bc, bC, cin[:, 1, q:q + 1], ALU.mult, ALU.add), R=["s5C", "s5p", "cin"], W=["s5D"])
            yield
            TT(dve, bA, cc, bB, ALU.mult, R=[tn, "s5B"], W=["s5A"])
            TT(dve, bC, ssn, bD, ALU.mult, R=[tn, "s5D"], W=["s5C"])
            TT(dve, hro, bA, bC, ALU.subtract, R=["s5A", "s5C", "arB"], W=[hbn])
            TT(dve, bA, cc, bD, ALU.mult, R=[tn, "s5D"], W=["s5A"])
            TT(dve, bC, ssn, bB, ALU.mult, R=[tn, "s5B"], W=["s5C"])
            TT(dve, hio, bA, bC, ALU.add, R=["s5A", "s5C", "arB"], W=[hbn])
            A(act, hc[:, 0, q:q + 1], bB[:, T - 1:T], AF.Copy, R=["s5B", "cin"], W=["hc"])
            A(act, hc[:, 1, q:q + 1], bD[:, T - 1:T], AF.Copy, R=["s5D", "cin"], W=["hc"])

            def cmm(q=q, j=j, base=base, hbq=hbq, hbn=hbn):
                if base < 64:
                    MM(ps[3][base:base + 32, :], Cm[:, 0, q, 0:32], hbq[0][:], True, False, R=["Cm", hbn], W=[("psY",)])
                    MM(ps[3][base:base + 32, :], Cm[:, 1, q, 0:32], hbq[1][:], False, True, R=["Cm", hbn], W=[("psY",)])
                else:
                    MM(ps[3][64:128, :], Cm[:, 0, q, :], hbq[0][:], base == 64, False, R=["Cm", hbn], W=[("psY",)])
                    MM(ps[3][64:128, :], Cm[:, 1, q, :], hbq[1][:], False, base == 96, R=["Cm", hbn], W=[("psY",)])
                if q % 4 == 3:
                    TT(dve, ysf[:, j, :], ysf[:, j, :], ps[3][:, :], ALU.add, R=[("psY",), "ysf"], W=["ysf"])
            pending.append(cmm)
            if len(pending) > 1:
                pending.pop(0)()
            yield
        while pending:
            pending.pop(0)()
        yield

    def spill(dst, src_ap, R):
        K.dma(dst, src_ap, R=R, W=[("spill",)])

    def passA(l, seg, ti):
        r0 = ti * T
        src = xin[seg] if l == 0 else x1[seg]
        load_norm(src, seg, r0, l, False)
        if ti == 0:
            reset_states()
        elif soft_left(ti, segs[seg] // T):
            scale_states()
        def streamH():
            for hd in range(8):
                def ev(i, m, gi, p, pslot, hd=hd):
                    if i == 0:
                        A(act, sc[0][:, 0:T], p, AF.Copy, R=[pslot], W=["sc0"])
                        spill(sp_q[seg][hd, :, r0:r0 + T], sc[0][:, 0:T], R=["sc0"])
                    elif i == 1:
                        A(act, sc[1][:, 0:T], p, AF.Sigmoid, R=[pslot], W=["sc1"])
                    else:
                        A(act, bvf[:], p, AF.Copy, R=[pslot], W=["bvf"])
                        for s in range(4):
                            TR(ps5b[:, s * 128:(s + 1) * 128], bvf[:, s * 128:(s + 1) * 128], identb[:], R=["bvf", "identb"], W=[("ps", 5)])
                        A(act, vT[:, hd, :, :], ps5b.rearrange("p (s k) -> p s k", k=128), AF.Copy, R=[("ps", 5)], W=["vT"])
                        spill(sp_v[seg][r0:r0 + T, hd * 128:(hd + 1) * 128].rearrange("(s p) v -> p s v", p=128), vT[:, hd, :, :], R=["vT"])
                linear(wb_in[l], 16, [C_Q + hd, C_FF + hd, C_I + hd], hT_k, [(HALO, T)], ev, ["hT"])
                yield
                yield from hg_head(hd, 0)
                A(act, sc[9][:, 0:T], ps[4][:, :], AF.Copy, R=[("ps", 4)], W=["sc9"])
                spill(sp_o[seg][hd, :, r0:r0 + T], sc[9][:, 0:T], R=["sc9"])
                yield

        def streamS():
            def evu(i, m, gi, p, pslot):
                A(act, ubf[:, i, :], p, AF.Copy, R=[pslot], W=["ubf"])
                A(act, ysf[:, i, :], p, AF.Copy, R=[pslot, "vecT"], W=["ysf"], scale=vecT[:, V_SD + i:V_SD + i + 1])
            linear(wb_in[l], 16, [C_DIN + j for j in range(8)], hT_k, [(HALO, T)], evu, ["hT"])
            spill(sp_u[seg][:, :, r0:r0 + T].rearrange("j p t -> p j t"), ubf, R=["ubf"])
            yield
            yield from s5_dir(l, 0)
            spill(sp_y[seg][:, :, r0:r0 + T].rearrange("j p t -> p j t"), ysf, R=["ysf"])
        run_streams([streamS(), streamH()], [SW_S, SW_H])

    def rstd_from(ps_ms, out_sc, R, W):
        TS(dve, out_sc, ps_ms, EPS, None, ALU.add, None, R=R, W=W)
        A(act, out_sc, out_sc, AF.Ln, R=W, W=W)
        A(act, out_sc, out_sc, AF.Exp, R=W, W=W, scale=-0.5)

    def passB(l, seg, ti, ntiles, last_layer):
        r0 = ti * T
        src = xin[seg] if l == 0 else x1[seg]
        load_norm(src, seg, r0, l, True, hm=(1 if soft_left(ti, ntiles) else (2 if soft_right(ti, ntiles) else 0)))
        if ti == ntiles - 1:
            reset_states()
        elif soft_right(ti, ntiles):
            scale_states()
        def streamD():
            K.dma(ubf, sp_u[seg][:, :, r0:r0 + T].rearrange("j p t -> p j t"), R=[("spill",)], W=["ubf"])
            K.dma(ysf, sp_y[seg][:, :, r0:r0 + T].rearrange("j p t -> p j t"), R=[("spill",)], W=["ysf"])
            yield
            yield from s5_dir(l, 1)
            for j in range(8):
                A(act, ysf[:, j, :], ysf[:, j, :], AF.Gelu, R=["ysf"], W=["ysf"])
                A(act, zbf[:, j, :], ysf[:, j, :], AF.Copy, R=["ysf"], W=["zbf"])

            def evglu(i, m, gi, p, pslot):
                A(act, s5buf[0], p, AF.Sigmoid, R=[pslot, "vecT"], W=["arB"], bias=vecT[:, V_GB + i:V_GB + i + 1])
                TT(dve, ysf[:, i, :], ysf[:, i, :], s5buf[0], ALU.mult, R=["ysf", "arB"], W=["ysf"])
            linear(wb_glu[l], 8, list(range(8)), lambda kc: zbf[:, kc, :], [(0, T)], evglu, ["zbf"])

            def evdg(i, m, gi, p, pslot):
                A(act, s5buf[0], p, AF.Silu, R=[pslot], W=["arB"])
                TT(dve, yd[:, i, :], ysf[:, i, :], s5buf[0], ALU.mult, R=["ysf", "arB"], W=["yd"])
            linear(wb_in[l], 16, [C_DGATE + j for j in range(8)], hT_k, [(HALO, T)], evdg, ["hT"])
            yield

        def streamC():
            for hd in range(8):
                K.dma(sc[0][:, 0:T], sp_q[seg][hd, :, r0:r0 + T], R=[("spill",)], W=["sc0"])
                K.dma(vT[:, hd, :, :], sp_v[seg][r0:r0 + T, hd * 128:(hd + 1) * 128].rearrange("(s p) v -> p s v", p=128), R=[("spill",)], W=["vT"])
                K.dma(sc[9][:, 0:T], sp_o[seg][hd, :, r0:r0 + T], R=[("spill",)], W=["sc9"])

                def ev(i, m, gi, p, pslot):
                    A(act, sc[1][:, 0:T], p, AF.Sigmoid, R=[pslot], W=["sc1"])
                linear(wb_in[l], 16, [C_FB + hd], hT_k, [(HALO, T)], ev, ["hT"])
                yield
                yield from hg_head(hd, 1)
                TT(dve, sc[9][:, 0:T], ps[4][:, :], sc[9][:, 0:T], ALU.add, R=[("ps", 4), "sc9"], W=["sc9"])
                A(act, bq[:], sc[9][:, 0:T], AF.Square, R=["sc9"], W=["bq"])
                MM(ps[5][:, :], onesH[:], bq[:], True, True, R=["onesH", "bq"], W=[("ps", 5)])
                rstd_from(ps[5][:, :], sc[2][:, 0:T], R=[("ps", 5)], W=["sc2"])
                STT(dve, sc[3][:, 0:T], sc[9][:, 0:T], vecT[:, V_HG + hd:V_HG + hd + 1], sc[2][:, 0:T], ALU.mult, ALU.mult,
                    R=["sc9", "vecT", "sc2"], W=["sc3"])

                def evg(i, m, gi, p, pslot, hd=hd):
                    A(act, sc[5][:, 0:T], p, AF.Silu, R=[pslot], W=["sc5"])
                    TT(dve, yc[:, hd, :], sc[3][:, 0:T], sc[5][:, 0:T], ALU.mult, R=["sc3", "sc5"], W=["yc"])
                linear(wb_in[l], 16, [C_CGATE + hd], hT_k, [(HALO, T)], evg, ["hT"])
                yield
            yield

        run_streams([streamD(), streamC()], [SW_S, SW_H])
        CG = [(0, T), (T, 2 * HALO)]
        for c in range(8):
            def eva(i, m, gi, p, pslot, c=c):
                c0, ncol = CG[gi]
                if i == 0:
                    A(act, sc[0][:, c0:c0 + ncol], p, AF.Copy, R=[pslot], W=["sc0"])
                else:
                    A(act, sc[1][:, c0:c0 + ncol], p, AF.Sigmoid, R=[pslot], W=["sc1"])
                    TT(dve, ue[:, c, c0:c0 + ncol], sc[0][:, c0:c0 + ncol], sc[1][:, c0:c0 + ncol], ALU.mult, R=["sc0", "sc1"], W=["ue"])
            linear(wb_in[l], 16, [C_AVAL + c, C_AGLU + c], hT_k, CG, eva, ["hT"])
        for c in range(8):
            b = c % 2
            K.dma(cbuf[b][:], cdiag[l][c], R=[("cdiag",)], W=[("cbuf", b)])
            p, pslot = nps()
            for j in range(31):
                MM(p[:, :], cbuf[b][:, j, :], ue[:, c, j + 1:j + 1 + T], j == 0, j == 30, R=[("cbuf", b), "ue"], W=[pslot])
            A(act, ysf[:, c, :], p[:, :], AF.Identity, R=[pslot, "vecT"], W=["ysf"], bias=vecT[:, V_CB + c:V_CB + c + 1])
            A(act, zbf[:, c, :], ysf[:, c, :], AF.Copy, R=["ysf"], W=["zbf"])
        for c in range(8):
            MM(ps[4][:, :], onesW[:], zbf[:, c, :], c == 0, c == 7, R=["onesW", "zbf"], W=[("ps", 4)])
        for c in range(8):
            A(act, bq[:], ysf[:, c, :], AF.Square, R=["ysf"], W=["bq"])
            MM(ps[5][:, :], onesW[:], bq[:], c == 0, c == 7, R=["onesW", "bq"], W=[("ps", 5)])
        A(act, sc[0][:, 0:T], ps[4][:, :], AF.Copy, R=[("ps", 4)], W=["sc0"])
        TT(dve, sc[1][:, 0:T], sc[0][:, 0:T], sc[0][:, 0:T], ALU.mult, R=["sc0"], W=["sc1"])
        TT(dve, sc[1][:, 0:T], ps[5][:, :], sc[1][:, 0:T], ALU.subtract, R=[("ps", 5), "sc1"], W=["sc1"])
        rstd_from(sc[1][:, 0:T], sc[2][:, 0:T], R=["sc1"], W=["sc2"])
        for c in range(8):
            TT(dve, sc[3][:, 0:T], ysf[:, c, :], sc[0][:, 0:T], ALU.subtract, R=["ysf", "sc0"], W=["sc3"])
            STT(dve, sc[3][:, 0:T], sc[3][:, 0:T], vecT[:, V_LNG + c:V_LNG + c + 1], sc[2][:, 0:T], ALU.mult, ALU.mult,
                R=["sc3", "vecT", "sc2"], W=["sc3"])
            A(act, sc[4][:, 0:T], sc[3][:, 0:T], AF.Silu, R=["sc3", "vecT"], W=["sc4"], bias=vecT[:, V_LNB + c:V_LNB + c + 1])

            def evag(i, m, gi, p, pslot, c=c):
                A(act, sc[5][:, 0:T], p, AF.Silu, R=[pslot], W=["sc5"])
                TT(dve, ya[:, c, :], sc[4][:, 0:T], sc[5][:, 0:T], ALU.mult, R=["sc4", "sc5"], W=["ya"])
            linear(wb_in[l], 16, [C_AGATE + c], hT_k, [(HALO, T)], evag, ["hT"])
        def evb(i, m, gi, p, pslot):
            c0, ncol = CG[gi]
            A(act, ue[:, i, c0:c0 + ncol], p, AF.Copy, R=[pslot], W=["ue"])
        linear(wb_in[l], 16, [C_BIN + c for c in range(8)], hT_k, CG, evb, ["hT"])
        for g in range(4):
            K.dma(rcb, rcin[seg][g, r0:r0 + T].partition_broadcast(128), W=["sc8"])
            for mm in range(2):
                c = 2 * g + mm

                def evz(i, m, gi, p, pslot):
                    c0, ncol = CG[gi]
                    A(act, sc[0][:, c0:c0 + ncol], p, AF.Copy, R=[pslot], W=["sc0"])
                linear(wb_pool[l][g], 2, [mm], lambda kc, g=g: ue[:, 2 * g + kc, :], CG, evz, ["ue"])
                z = sc[0]
                a, bb = sc[2], sc[3]
                TT(dve, a[:, 1:TE], z[:, 0:TE - 1], z[:, 1:TE], ALU.add, R=["sc0"], W=["sc2"])
                cur, lo, hi = a, 1, TE
                if g >= 1:
                    TT(dve, bb[:, lo + 1:hi - 1], cur[:, lo:hi - 2], cur[:, lo + 2:hi], ALU.add, R=["sc2"], W=["sc3"])
                    cur, lo, hi = bb, lo + 1, hi - 1
                if g >= 2:
                    TT(dve, a[:, lo + 2:hi - 2], cur[:, lo:hi - 4], cur[:, lo + 4:hi], ALU.add, R=["sc3"], W=["sc2"])
                    cur, lo, hi = a, lo + 2, hi - 2
                if g >= 3:
                    TT(dve, bb[:, lo + 4:hi - 4], cur[:, lo:hi - 8], cur[:, lo + 8:hi], ALU.add, R=["sc2"], W=["sc3"])
                    cur, lo, hi = bb, lo + 4, hi - 4
                cn_ = "sc2" if cur is a else "sc3"
                TT(dve, sc[4][:, 0:T], cur[:, HALO:HALO + T], rcb, ALU.mult, R=[cn_, "sc8"], W=["sc4"])
                TT(dve, sc[4][:, 0:T], sc[4][:, 0:T], z[:, HALO:HALO + T], ALU.subtract, R=["sc4", "sc0"], W=["sc4"])

                def evbg(i, m, gi, p, pslot, c=c):
                    A(act, sc[5][:, 0:T], p, AF.Silu, R=[pslot], W=["sc5"])
                    STT(dve, yb[:, c, :], sc[4][:, 0:T], vecT[:, V_PS + c:V_PS + c + 1], sc[5][:, 0:T], ALU.mult, ALU.mult,
                        R=["sc4", "vecT", "sc5"], W=["yb"])
                linear(wb_in[l], 16, [C_BGATE + c], hT_k, [(HALO, T)], evbg, ["hT"])
        ys = [ya, yb, yc, yd]
        yn = ["ya", "yb", "yc", "yd"]
        for dch in range(16):
            for i in range(4):
                def evr(ii, m, gi, p, pslot, i=i):
                    A(act, sc[5][:, 0:T], p, AF.Sigmoid, R=[pslot], W=["sc5"])
                linear(wb_in[l], 16, [C_R + i * 16 + dch], hT_k, [(HALO, T)], evr, ["hT"])

                def evo(ii, m, gi, p, pslot, i=i):
                    if i == 0:
                        TT(dve, sc[6][:, 0:T], p, sc[5][:, 0:T], ALU.mult, R=[pslot, "sc5"], W=["sc6"])
                    else:
                        TT(dve, sc[7][:, 0:T], p, sc[5][:, 0:T], ALU.mult, R=[pslot, "sc5"], W=["sc7"])
                        TT(dve, sc[6][:, 0:T], sc[6][:, 0:T], sc[7][:, 0:T], ALU.add, R=["sc6", "sc7"], W=["sc6"])
                linear(wb_ao[l][i], 8, [dch], lambda kc, i=i: ys[i][:, kc, :], [(0, T)], evo, [yn[i]])
            A(act, mT[:, dch, :], sc[6][:, 0:T], AF.Copy, R=["sc6"], W=["mT"])
        def evout(i, m, gi, p, pslot):
            A(act, sc[0][:, 0:T], p, AF.Copy, R=[pslot], W=["sc0"])
            for s in range(4):
                TR(psx[:, 0:128], sc[0][:, s * 128:(s + 1) * 128], ident, R=["sc0", "cs"], W=["psx"])
                TT(dve, xt[:, s, i * 128:(i + 1) * 128], xt[:, s, i * 128:(i + 1) * 128], psx[:, 0:128], ALU.add, R=["psx", "xt"], W=["xt"])
        K.dma(xt, src[HALO + r0:HALO + r0 + T, :].rearrange("(s p) d -> p s d", p=128), W=["xt"])
        linear(wb_out[l], 16, list(range(16)), lambda kc: mT[:, kc, :], [(0, T)], evout, ["mT"])
        if not last_layer:
            K.dma(x1[seg][HALO + r0:HALO + r0 + T, :].rearrange("(s p) d -> p s d", p=128), xt, R=["xt"], W=[("x1", seg)])
        else:
            for s in range(4):
                A(act, junk[:, :], xt[:, s, :], AF.Square, R=["xt"], W=["junk", "ss"], accum=ss[:, s:s + 1])
            TS(dve, rstd[:, 0:4], ss[:, 0:4], 1.0 / D, EPS, ALU.mult, ALU.add, R=["ss"], W=["rstd"])
            A(act, rstd[:, 0:4], rstd[:, 0:4], AF.Ln, R=["rstd"], W=["rstd"])
            A(act, rstd[:, 0:4], rstd[:, 0:4], AF.Exp, R=["rstd"], W=["rstd"], scale=-0.5)
            K.dma(fgb, final_g.partition_broadcast(128), W=["fgb"])
            for s in range(4):
                STT(dve, xt[:, s, :], xt[:, s, :], rstd[:, s:s + 1], fgb, ALU.mult, ALU.mult, R=["xt", "rstd", "fgb"], W=["xt"])
            K.dma(yout[seg][r0:r0 + T, :].rearrange("(s p) d -> p s d", p=128), xt, R=["xt"], W=[("yout", seg)])

    for l in range(depth):
        prep_layer(l)
        prep_s5(l, 0)
        for seg in range(nseg):
            nt = segs[seg] // T
            for ti in range(nt):
                passA(l, seg, ti)
        prep_s5(l, 1)
        for seg in range(nseg):
            nt = segs[seg] // T
            for ti in range(nt - 1, -1, -1):
                passB(l, seg, ti, nt, l == depth - 1)
    K.finish()
    return nc


def _consts():
    c = np.zeros((128, 128 + 256 + 256 + 512), np.float32)
    c[:, 0:128] = np.eye(128, dtype=np.float32)
    s = np.arange(128) % 64
    t = np.arange(256) % 64
    c[:, 128:384] = (s[:, None] <= t[None, :]).astype(np.float32)
    c[:, 384:640] = (s[:, None] >= t[None, :]).astype(np.float32)
    rm = np.ones(512, np.float32)
    rm[::64] = 0.0
    c[:, 640:1152] = rm[None, :]
    return c


def _rc(L):
    t = np.arange(L)
    out = np.zeros((4, L), np.float32)
    for g, win in enumerate((2, 4, 8, 16)):
        left = win // 2
        right = win - 1 - left
        lo = np.maximum(t - left, 0)
        hi = np.minimum(t + right, L - 1) + 1
        out[g] = 1.0 / (hi - lo).astype(np.float32)
    return out


_WNAMES = ["norm_g", "w_in", "conv_w", "conv_b", "conv_ln_g", "conv_ln_b", "w_a_out", "pool_w", "pool_scale", "w_b_out",
           "hg_lb", "hg_norm_g", "w_c_out", "s5_a_re", "s5_a_im", "s5_log_dt", "s5_b_re", "s5_b_im", "s5_c_re", "s5_c_im",
           "s5_d", "s5_glu_w", "s5_glu_b", "w_d_out", "w_out", "final_g"]


def run(x_prompt, x_sample, weights, ncores=8):
    Lp, Ls = x_prompt.shape[1], x_sample.shape[1]
    depth = weights["w_in"].shape[0]
    nb_p, nb_s = x_prompt.shape[0], x_sample.shape[0]
    slots = Lp // Ls
    segt = Ls // T
    nc = build([Lp], depth, segt=segt)
    base = {k: np.ascontiguousarray(np.asarray(weights[k], dtype=np.float32)) for k in _WNAMES}
    base["cst"] = _consts()
    rc_p = _rc(Lp)
    rc_s = np.ascontiguousarray(np.tile(_rc(Ls), (1, slots)))
    n_score = ncores - nb_p
    per = -(-nb_s // n_score)
    assert per <= slots
    in_maps, assign = [], []
    for c in range(ncores):
        m = dict(base)
        xp = np.zeros((Lp + 2 * HALO, D), np.float32)
        hm = np.ones((32, 3), np.float32)
        if c < nb_p:
            xp[HALO:HALO + Lp] = x_prompt[c]
            m["rc0"] = rc_p
            cf = 1.0
            assign.append(("p", c))
        else:
            ids = [i for i in range((c - nb_p) * per, min((c - nb_p + 1) * per, nb_s))]
            for k, i in enumerate(ids):
                xp[HALO + k * Ls:HALO + (k + 1) * Ls] = x_sample[i]
            m["rc0"] = rc_s
            cf = 0.0
            assign.append(("s", ids))
        hm[0:16, 1] = cf
        hm[16:32, 2] = cf
        m["x0"] = xp
        m["cfl"] = np.full((128, 1), cf, np.float32)
        m["hmk"] = hm
        in_maps.append(m)
    res = run_bass_kernel_spmd(nc, in_maps, core_ids=list(range(ncores)))
    yp = np.zeros(x_prompt.shape, np.float32)
    ysm = np.zeros(x_sample.shape, np.float32)
    for c in range(ncores):
        kind, ids = assign[c]
        y = res.results[c]["y0"]
        if kind == "p":
            yp[ids] = y
        else:
            for k, i in enumerate(ids):
                ysm[i] = y[k * Ls:(k + 1) * Ls]
    return yp, ysm


def kernel(x_prompt, x_sample, **weights):
    x_prompt = np.asarray(x_prompt, dtype=np.float32)
    x_sample = np.asarray(x_sample, dtype=np.float32)
    return run(x_prompt, x_sample, weights)
```
